# Optimizing a Trainium2 kernel written in Bass

```python
import jax, jax.numpy as jnp
from jax import lax
import numpy as np

D_MODEL = 1024
BATCH = 32
SEQ = 2048
DEPTH = 4

GRID_W = 64
CTX_LEN = 256
N_MIXERS = 3
N_MOD = 6
EPS = 1e-6
RET_HEADS = 4
RET_DK = D_MODEL // RET_HEADS
RET_DV = 2 * RET_DK
RET_CHUNK = 128
ROPE_BASE = 10000.0
LRU_WIDTH = (4 * D_MODEL // 3) // 128 * 128
LRU_BLOCKS = 8
LRU_BS = LRU_WIDTH // LRU_BLOCKS
LRU_CONV = 4
LRU_C = 8.0
NA_HEADS = 16
NA_HD = D_MODEL // NA_HEADS
NA_KH = 8
NA_KW = 16
FFN_DIM = (8 * D_MODEL // 3 + 127) // 128 * 128
FFN_CONV = 3

kernel_name = 'hybrid_dit_retention_rglru_natten'


def rmsnorm(x, gain):
    x32 = x.astype(jnp.float32)
    y = x32 * lax.rsqrt(jnp.mean(jnp.square(x32), axis=-1, keepdims=True) + EPS)
    return (y * gain.astype(jnp.float32)).astype(x.dtype)


def modulate(h, shift, scale):
    return h * (1.0 + scale) + shift


def flip(t):
    return t[:, ::-1]


def dwconv(x, w, b, left):
    k_w, n = w.shape[0], x.shape[1]
    xp = jnp.pad(x, ((0, 0), (left, k_w - 1 - left), (0, 0)))
    out = b
    for k in range(k_w):
        out = out + xp[:, k:k + n] * w[k]
    return out


def rope_2d(x):
    n, d = x.shape[1], x.shape[-1]
    t = jnp.arange(n)
    half = d // 2
    freqs = ROPE_BASE ** (-jnp.arange(0, half, 2, dtype=jnp.float32) / half)

    def rot(xa, pos):
        ang = pos.astype(jnp.float32)[:, None] * freqs
        cos = jnp.cos(ang)[None, :, None, :].astype(x.dtype)
        sin = jnp.sin(ang)[None, :, None, :].astype(x.dtype)
        x1, x2 = jnp.split(xa, 2, axis=-1)
        return jnp.concatenate([x1 * cos - x2 * sin, x1 * sin + x2 * cos], axis=-1)

    return jnp.concatenate([rot(x[..., :half], t // GRID_W), rot(x[..., half:], t % GRID_W)], axis=-1)


def head_groupnorm(y, gain):
    b, n, h, dv = y.shape
    y32 = y.astype(jnp.float32)
    mu = jnp.mean(y32, axis=-1, keepdims=True)
    var = jnp.mean(jnp.square(y32 - mu), axis=-1, keepdims=True)
    yn = ((y32 - mu) * lax.rsqrt(var + EPS)).reshape(b, n, h * dv)
    return (yn * gain.astype(jnp.float32)).astype(y.dtype)


def retention_scan(q, k, v, log_gamma, s0):
    b, n, h, _ = q.shape
    dv = v.shape[-1]
    ch = RET_CHUNK
    nc = n // ch

    def chunks(t):
        return t.reshape(b, nc, ch, h, t.shape[-1]).transpose(1, 0, 3, 2, 4)

    idx = jnp.arange(ch, dtype=jnp.float32)
    lg = log_gamma.astype(jnp.float32)
    diff = idx[:, None] - idx[None, :]
    dmat = jnp.where(diff >= 0, jnp.exp(lg[:, None, None] * jnp.maximum(diff, 0.0)), 0.0).astype(q.dtype)
    xi = jnp.exp(lg[:, None] * (idx + 1.0))[..., None].astype(q.dtype)
    zeta = jnp.exp(lg[:, None] * (ch - 1.0 - idx))[..., None].astype(q.dtype)
    g_chunk = jnp.exp(lg * ch)[:, None, None].astype(q.dtype)

    def step(state, qkv):
        qc, kc, vc = qkv
        inner = jnp.einsum('bhid,bhjd->bhij', qc, kc) * dmat
        y = jnp.einsum('bhij,bhjv->bhiv', inner, vc) + jnp.einsum('bhid,bhdv->bhiv', qc, state) * xi
        state = state * g_chunk + jnp.einsum('bhjd,bhjv->bhdv', kc * zeta, vc)
        return state, y

    s_fin, ys = lax.scan(step, s0, (chunks(q), chunks(k), chunks(v)))
    return ys.transpose(1, 0, 3, 2, 4).reshape(b, n, h, dv), s_fin


def retention_mixer(h_c, h_l, w_in, logit_decay, gn, w_out, need_ctx):
    b = h_l.shape[0]
    qk_w = RET_HEADS * RET_DK
    split_at = [qk_w, 2 * qk_w, 2 * qk_w + RET_HEADS * RET_DV]

    def project(h, rotate):
        n = h.shape[1]
        q, k, v, g = jnp.split(h @ w_in, split_at, axis=-1)
        q = q.reshape(b, n, RET_HEADS, RET_DK)
        k = k.reshape(b, n, RET_HEADS, RET_DK) * (RET_DK ** -0.5)
        v = v.reshape(b, n, RET_HEADS, RET_DV)
        if rotate:
            q, k = rope_2d(q), rope_2d(k)
        return q, k, v, g

    log_gamma = jax.nn.log_sigmoid(logit_decay.astype(jnp.float32))
    qc, kc, vc, gc = project(h_c, False)
    ql, kl, vl, gl = project(h_l, True)
    s0 = jnp.zeros((b, RET_HEADS, RET_DK, RET_DV), h_l.dtype)
    yc_f, sc_f = retention_scan(qc, kc, vc, log_gamma[0], s0)
    yc_b, sc_b = retention_scan(flip(qc), flip(kc), flip(vc), log_gamma[1], s0)
    yl_f, _ = retention_scan(ql, kl, vl, log_gamma[0], sc_f)
    yl_b, _ = retention_scan(flip(ql), flip(kl), flip(vl), log_gamma[1], sc_b)

    def readout(y, g):
        return (jax.nn.silu(g) * head_groupnorm(y, gn)) @ w_out

    out_l = readout(yl_f + flip(yl_b), gl)
    out_c = readout(yc_f + flip(yc_b), gc) if need_ctx else None
    return out_c, out_l


def _affine_combine(left, right):
    a_l, b_l = left
    a_r, b_r = right
    return a_l * a_r, a_r * b_l + b_r


def rglru_scan(x, gate_w, gate_b, lam, h0):
    b, n, w = x.shape
    xb = x.reshape(b, n, LRU_BLOCKS, LRU_BS)
    gates = jnp.einsum('bnkc,gkcd->gbnkd', xb, gate_w).reshape(2, b, n, w) + gate_b[:, None, None, :]
    r, i = jax.nn.sigmoid(gates[0]), jax.nn.sigmoid(gates[1])
    log_a = -LRU_C * r * jax.nn.softplus(-lam)
    a = jnp.exp(log_a)
    u = jnp.sqrt(-jnp.expm1(2.0 * log_a)) * (i * x)
    a_cum, h_part = lax.associative_scan(_affine_combine, (a, u), axis=1)
    return a_cum * h0[:, None, :] + h_part


def lru_mixer(h_c, h_l, w_in, conv_w, conv_b, gate_w, gate_b, lam, w_out, need_ctx):
    def branches(h):
        y_br, x_br = jnp.split(h @ w_in, 2, axis=-1)
        return y_br, dwconv(x_br, conv_w, conv_b, LRU_CONV // 2)

    yc, xc = branches(h_c)
    yl, xl = branches(h_l)
    h0 = jnp.zeros((h_l.shape[0], LRU_WIDTH), h_l.dtype)
    hc_f = rglru_scan(xc, gate_w[0], gate_b[0], lam[0], h0)
    hc_b = flip(rglru_scan(flip(xc), gate_w[1], gate_b[1], lam[1], h0))
    hl_f = rglru_scan(xl, gate_w[0], gate_b[0], lam[0], hc_f[:, -1])
    hl_b = flip(rglru_scan(flip(xl), gate_w[1], gate_b[1], lam[1], hc_b[:, 0]))

    def readout(hf, hb, y_br):
        return ((hf + hb) * jax.nn.gelu(y_br)) @ w_out

    out_l = readout(hl_f, hl_b, yl)
    out_c = readout(hc_f, hc_b, yc) if need_ctx else None
    return out_c, out_l


def na_mixer(h_c, h_l, w_qkv, rpb, w_out, need_ctx):
    b, n, _ = h_l.shape
    rows = n // GRID_W
    kh, kw = min(NA_KH, rows), NA_KW
    scale = NA_HD ** -0.5

    def heads(h):
        return [t.reshape(b, h.shape[1], NA_HEADS, NA_HD) for t in jnp.split(h @ w_qkv, 3, axis=-1)]

    qc, kc, vc = heads(h_c)
    ql, kl, vl = heads(h_l)
    k_grid = kl.reshape(b, rows, GRID_W, NA_HEADS, NA_HD)
    v_grid = vl.reshape(b, rows, GRID_W, NA_HEADS, NA_HD)
    q_rows = ql.reshape(b, rows, GRID_W, NA_HEADS, NA_HD).transpose(1, 0, 2, 3, 4)

    col = jnp.arange(GRID_W)
    col_start = jnp.clip(col - kw // 2, 0, GRID_W - kw)
    col_in = (col[None, :] >= col_start[:, None]) & (col[None, :] < col_start[:, None] + kw)
    dc_idx = jnp.clip(col[None, :] - col[:, None], -(kw - 1), kw - 1) + (kw - 1)
    rpb32 = rpb.astype(jnp.float32)

    def row_block(args):
        r, q_r = args
        start = jnp.clip(r - kh // 2, 0, rows - kh)
        k_blk = lax.dynamic_slice_in_dim(k_grid, start, kh, axis=1)
        v_blk = lax.dynamic_slice_in_dim(v_grid, start, kh, axis=1)
        dr_idx = start + jnp.arange(kh) - r + (NA_KH - 1)
        bias = rpb32[:, dr_idx][:, :, dc_idx]
        bias = jnp.where(col_in[None, None], bias, -jnp.inf).transpose(0, 2, 1, 3)
        s_lat = jnp.einsum('bqhd,bkwhd->bhqkw', q_r, k_blk).astype(jnp.float32) * scale + bias
        s_ctx = jnp.einsum('bqhd,bkhd->bhqk', q_r, kc).astype(jnp.float32) * scale
        s = jnp.concatenate([s_lat.reshape(b, NA_HEADS, GRID_W, kh * GRID_W), s_ctx], axis=-1)
        p = jax.nn.softmax(s, axis=-1).astype(q_r.dtype)
        p_lat = p[..., :kh * GRID_W].reshape(b, NA_HEADS, GRID_W, kh, GRID_W)
        p_ctx = p[..., kh * GRID_W:]
        return (jnp.einsum('bhqkw,bkwhd->bqhd', p_lat, v_blk)
                + jnp.einsum('bhqk,bkhd->bqhd', p_ctx, vc))

    o = lax.map(row_block, (jnp.arange(rows), q_rows))
    out_l = o.transpose(1, 0, 2, 3, 4).reshape(b, n, D_MODEL) @ w_out
    out_c = None
    if need_ctx:
        s = jnp.einsum('bqhd,bkhd->bhqk', qc, kc).astype(jnp.float32) * scale
        p = jax.nn.softmax(s, axis=-1).astype(vc.dtype)
        out_c = jnp.einsum('bhqk,bkhd->bqhd', p, vc).reshape(b, h_c.shape[1], D_MODEL) @ w_out
    return out_c, out_l


def conv_ffn(h, w_up, conv_w, conv_b, w_down):
    u = dwconv(h @ w_up, conv_w, conv_b, FFN_CONV // 2)
    g, v = jnp.split(u, 2, axis=-1)
    return (jax.nn.silu(g) * v) @ w_down


def setup_inputs(seed: int = 0) -> dict:
    key = jax.random.key(seed)
    ks = iter(jax.random.split(key, 40))
    f32 = jnp.float32

    def nrm(shape, scale):
        return jax.random.normal(next(ks), shape, f32) * scale

    n_ret = len(range(0, DEPTH, N_MIXERS))
    n_lru = len(range(1, DEPTH, N_MIXERS))
    n_na = len(range(2, DEPTH, N_MIXERS))
    ret_logit0 = jnp.asarray(np.log(2.0 ** (5.0 + np.arange(RET_HEADS)) - 1.0), f32)
    a0 = jax.random.uniform(next(ks), (n_lru, 2, LRU_WIDTH), f32, 0.9, 0.999)
    return {
        'x': nrm((BATCH, SEQ, D_MODEL), 1.0),
        'c': nrm((BATCH, D_MODEL), 1.0),
        'ctx': nrm((BATCH, CTX_LEN, D_MODEL), 1.0),
        'c_ctx': nrm((D_MODEL,), 1.0),
        'ada_w': nrm((DEPTH, D_MODEL, N_MOD * D_MODEL), 0.5 * D_MODEL ** -0.5),
        'ada_b': nrm((DEPTH, N_MOD * D_MODEL), 0.02),
        'norm_mix': 1.0 + nrm((DEPTH, D_MODEL), 0.05),
        'norm_ffn': 1.0 + nrm((DEPTH, D_MODEL), 0.05),
        'ffn_w_up': nrm((DEPTH, D_MODEL, 2 * FFN_DIM), D_MODEL ** -0.5),
        'ffn_conv_w': nrm((DEPTH, FFN_CONV, 2 * FFN_DIM), FFN_CONV ** -0.5),
        'ffn_conv_b': nrm((DEPTH, 2 * FFN_DIM), 0.02),
        'ffn_w_down': nrm((DEPTH, FFN_DIM, D_MODEL), FFN_DIM ** -0.5),
        'ret_w_in': nrm((n_ret, D_MODEL, 2 * RET_HEADS * RET_DK + 2 * RET_HEADS * RET_DV), D_MODEL ** -0.5),
        'ret_logit_decay': ret_logit0 + nrm((n_ret, 2, RET_HEADS), 0.1),
        'ret_gn': 1.0 + nrm((n_ret, RET_HEADS * RET_DV), 0.05),
        'ret_w_out': nrm((n_ret, RET_HEADS * RET_DV, D_MODEL), (RET_HEADS * RET_DV) ** -0.5),
        'lru_w_in': nrm((n_lru, D_MODEL, 2 * LRU_WIDTH), D_MODEL ** -0.5),
        'lru_conv_w': nrm((n_lru, LRU_CONV, LRU_WIDTH), LRU_CONV ** -0.5),
        'lru_conv_b': nrm((n_lru, LRU_WIDTH), 0.02),
        'lru_gate_w': nrm((n_lru, 2, 2, LRU_BLOCKS, LRU_BS, LRU_BS), LRU_BS ** -0.5),
        'lru_gate_b': nrm((n_lru, 2, 2, LRU_WIDTH), 0.02),
        'lru_lambda': jnp.log(a0 / (1.0 - a0)),
        'lru_w_out': nrm((n_lru, LRU_WIDTH, D_MODEL), LRU_WIDTH ** -0.5),
        'na_w_qkv': nrm((n_na, D_MODEL, 3 * D_MODEL), D_MODEL ** -0.5),
        'na_rpb': nrm((n_na, NA_HEADS, 2 * NA_KH - 1, 2 * NA_KW - 1), 0.05),
        'na_w_out': nrm((n_na, D_MODEL, D_MODEL), D_MODEL ** -0.5),
        'final_norm': 1.0 + nrm((D_MODEL,), 0.05),
    }


def reference(x, c, ctx, c_ctx, ada_w, ada_b, norm_mix, norm_ffn, ffn_w_up, ffn_conv_w, ffn_conv_b,
              ffn_w_down, ret_w_in, ret_logit_decay, ret_gn, ret_w_out, lru_w_in, lru_conv_w, lru_conv_b,
              lru_gate_w, lru_gate_b, lru_lambda, lru_w_out, na_w_qkv, na_rpb, na_w_out, final_norm):
    silu_c = jax.nn.silu(c)
    silu_cc = jax.nn.silu(c_ctx)
    for i in range(DEPTH):
        need_ctx = i < DEPTH - 1
        kind, j = i % N_MIXERS, i // N_MIXERS
        mod_l = (silu_c @ ada_w[i] + ada_b[i])[:, None, :]
        mod_c = silu_cc @ ada_w[i] + ada_b[i]
        sh1, sc1, g1, sh2, sc2, g2 = jnp.split(mod_l, N_MOD, axis=-1)
        csh1, csc1, cg1, csh2, csc2, cg2 = jnp.split(mod_c, N_MOD, axis=-1)

        h_l = modulate(rmsnorm(x, norm_mix[i]), sh1, sc1)
        h_c = modulate(rmsnorm(ctx, norm_mix[i]), csh1, csc1)
        if kind == 0:
            out_c, out_l = retention_mixer(h_c, h_l, ret_w_in[j], ret_logit_decay[j], ret_gn[j],
                                           ret_w_out[j], need_ctx)
        elif kind == 1:
            out_c, out_l = lru_mixer(h_c, h_l, lru_w_in[j], lru_conv_w[j], lru_conv_b[j], lru_gate_w[j],
                                     lru_gate_b[j], lru_lambda[j], lru_w_out[j], need_ctx)
        else:
            out_c, out_l = na_mixer(h_c, h_l, na_w_qkv[j], na_rpb[j], na_w_out[j], need_ctx)
        x = x + g1 * out_l
        if need_ctx:
            ctx = ctx + cg1 * out_c

        x = x + g2 * conv_ffn(modulate(rmsnorm(x, norm_ffn[i]), sh2, sc2),
                              ffn_w_up[i], ffn_conv_w[i], ffn_conv_b[i], ffn_w_down[i])
        if need_ctx:
            ctx = ctx + cg2 * conv_ffn(modulate(rmsnorm(ctx, norm_ffn[i]), csh2, csc2),
                                       ffn_w_up[i], ffn_conv_w[i], ffn_conv_b[i], ffn_w_down[i])
    return rmsnorm(x, final_norm)
```

```python
import numpy as np
import concourse.bass as bass
import concourse.mybir as mybir
from contextlib import ExitStack

F32 = mybir.dt.float32
BF16 = mybir.dt.bfloat16
AF = mybir.ActivationFunctionType
ALU = mybir.AluOpType
AX = mybir.AxisListType

ENGS = ("pe", "act", "dve", "pool", "sp")
NDMASEM = 24
SAME_ENGINE_SYNC = True


class Op:
    __slots__ = ("eng", "idx", "fn", "deps", "needs_inc", "incval", "dma", "dsem", "dval", "dprev", "waits")

    def __init__(self, eng, fn, dma):
        self.eng = eng
        self.fn = fn
        self.dma = dma
        self.deps = {}
        self.needs_inc = False
        self.incval = None
        self.waits = []


class Prog:
    def __init__(self, nc, es):
        self.nc = nc
        self.es = es
        self.ops = {e: [] for e in ENGS}
        self.trk = {}
        self.ndma = {e: 0 for e in ENGS}
        self.waited = {e: {} for e in ENGS}
        self.nops = 0
        self._cc = {}
        self.stack = [es]
        self.dmas = []

    def sb(self, name, shape, dtype, chunk=None):
        self.nsb = getattr(self, "nsb", 0) + 1
        name = "%s_%d" % (name, self.nsb)
        t = self.stack[-1].enter_context(self.nc.sbuf_tensor(name, list(shape), dtype))
        free = int(np.prod(shape[1:]))
        self.trk[name] = (free, chunk or free, {})
        return t

    def ps(self, name, shape, dtype=F32, chunk=None):
        t = self.es.enter_context(self.nc.psum_tensor(name, list(shape), dtype))
        free = int(np.prod(shape[1:]))
        self.trk[name] = (free, chunk or free, {})
        return t

    def dram(self, name, shape, dtype, kind="Internal", chunk=None):
        t = self.nc.dram_tensor(name, list(shape), dtype, kind=kind)
        n = int(np.prod(shape))
        self.trk[name] = (None, chunk or n, {})
        return t.ap()

    def _chunks(self, ap):
        name = ap.tensor.name
        tr = self.trk.get(name)
        if tr is None:
            return None, ()
        pstep, chunk, st = tr
        if chunk >= (1 << 29):
            return st, (0,)
        key = (name, ap.offset, ap.ap)
        r = self._cc.get(key)
        if r is not None:
            return st, r
        off = ap.offset
        dims = list(ap.ap)
        if pstep is not None:
            off = off % pstep
            dims = dims[1:]
        ivs = [(off, off)]
        for step, cnt in reversed(dims):
            if cnt <= 1:
                continue
            span = step * (cnt - 1)
            if abs(step) <= (ivs[0][1] - ivs[0][0] + 1) or len(ivs) * cnt > 256 or abs(step) < chunk // 2:
                ivs = [(lo + min(0, span), hi + max(0, span)) for lo, hi in ivs]
            else:
                ivs = [(lo + step * i, hi + step * i) for i in range(cnt) for lo, hi in ivs]
        cs = set()
        for lo, hi in ivs:
            cs.update(range(lo // chunk, hi // chunk + 1))
        r = tuple(cs)
        self._cc[key] = r
        return st, r

    def _dep(self, op, prod):
        if prod is None or prod is op:
            return
        if prod.dma:
            op.deps[("dma", id(prod))] = prod
            return
        if prod.eng == op.eng and not op.dma and (prod.eng == "pe" or not SAME_ENGINE_SYNC):
            return
        k = prod.eng
        cur = op.deps.get(k)
        if cur is None or cur.idx < prod.idx:
            op.deps[k] = prod

    def add(self, eng, fn, reads=(), writes=(), dma=False, extra=()):
        op = Op(eng, fn, dma)
        op.idx = len(self.ops[eng])
        self.nops += 1
        for p in extra:
            self._dep(op, p)
        for ap in reads:
            st, chs = self._chunks(ap)
            if st is None:
                continue
            for c in chs:
                s = st.get(c)
                if s is None:
                    s = st[c] = [None, {}]
                self._dep(op, s[0])
                s[1][("dma", id(op)) if dma else eng] = op
        for ap in writes:
            st, chs = self._chunks(ap)
            if st is None:
                continue
            for c in chs:
                s = st.get(c)
                if s is None:
                    s = st[c] = [None, {}]
                self._dep(op, s[0])
                for r in s[1].values():
                    self._dep(op, r)
                s[0] = op
                s[1] = {}
        w = self.waited[eng]
        for k, prod in op.deps.items():
            if prod.dma:
                key = ("dma", id(prod))
                if w.get(key):
                    continue
                w[key] = True
                op.waits.append(prod)
            else:
                if w.get(k, -1) >= prod.idx:
                    continue
                w[k] = prod.idx
                prod.needs_inc = True
                op.waits.append(prod)
        if dma:
            j = self.ndma[eng]
            self.ndma[eng] += 1
            op.dsem = j % NDMASEM
            op.dval = 16 * (j // NDMASEM + 1)
        self.ops[eng].append(op)
        if dma:
            self.dmas.append(op)
        return op

    def barrier(self):
        lasts = [self.ops[e][-1] for e in ENGS if self.ops[e]]
        dm = self.dmas
        self.dmas = []
        for e in ENGS:
            if self.ops[e]:
                self.add(e, None, extra=[p for p in lasts if p.eng != e] + dm)

    class _Scope:
        def __init__(self, P):
            self.P = P

        def __enter__(self):
            self.es = ExitStack()
            self.es.__enter__()
            self.P.stack.append(self.es)
            return self

        def __exit__(self, *a):
            self.P.barrier()
            self.P.stack.pop()
            return self.es.__exit__(*a)

    def scope(self):
        return Prog._Scope(self)

    def mm(self, out, lhsT, rhs, start=True, stop=True, **kw):
        return self.add("pe", lambda e: e.matmul(out, lhsT, rhs, start=start, stop=stop, **kw),
                        reads=(lhsT, rhs), writes=(out,))

    def tr(self, out, in_, ident):
        return self.add("pe", lambda e: e.transpose(out, in_, ident), reads=(in_, ident), writes=(out,))

    def act(self, out, in_, func, bias=None, scale=None, accum_out=None, eng="act"):
        kw = {}
        rd = [in_]
        if bias is not None:
            kw["bias"] = bias
            if not isinstance(bias, (int, float)):
                rd.append(bias)
        if scale is not None:
            kw["scale"] = scale
            if not isinstance(scale, (int, float)):
                rd.append(scale)
        wr = [out]
        if accum_out is not None:
            kw["accum_out"] = accum_out
            wr.append(accum_out)
        return self.add("act", lambda e: e.activation(out, in_, func, **kw), reads=rd, writes=wr)

    def tt(self, out, in0, in1, op, eng="dve"):
        return self.add(eng, lambda e: e.tensor_tensor(out, in0, in1, op), reads=(in0, in1), writes=(out,))

    def ts(self, out, in0, s1, s2, op0, op1=None, eng="dve", accum_out=None):
        rd = [in0] + [s for s in (s1, s2) if s is not None and not isinstance(s, (int, float))]
        wr = [out] + ([accum_out] if accum_out is not None else [])
        kw = {}
        if op1 is not None:
            kw["op1"] = op1
        if accum_out is not None:
            kw["accum_out"] = accum_out
        return self.add(eng, lambda e: e.tensor_scalar(out, in0, s1, s2, op0, **kw), reads=rd, writes=wr)

    def stt(self, out, in0, scalar, in1, op0, op1):
        rd = [in0, in1] + ([scalar] if not isinstance(scalar, (int, float)) else [])
        return self.add("dve", lambda e: e.scalar_tensor_tensor(out, in0, scalar, in1, op0, op1),
                        reads=rd, writes=(out,))

    def copy(self, out, in_, eng="dve"):
        if eng == "act":
            return self.add("act", lambda e: e.copy(out, in_), reads=(in_,), writes=(out,))
        return self.add(eng, lambda e: e.tensor_copy(out, in_), reads=(in_,), writes=(out,))

    def memset(self, ap, val, eng="dve"):
        return self.add(eng, lambda e: e.memset(ap, val), writes=(ap,))

    def dma(self, out, in_, q="sp", **kw):
        return self.add(q, lambda e: e.dma_start(out=out, in_=in_, **kw), reads=(in_,), writes=(out,), dma=True)

    def emit(self):
        nc = self.nc
        es = self.es
        sem = {e: es.enter_context(nc.semaphore("s_" + e)) for e in ENGS}
        dsem = {e: [es.enter_context(nc.semaphore("d_%s%d" % (e, i))) for i in range(NDMASEM)]
                for e in ENGS if self.ndma[e]}
        for e in ENGS:
            c = 0
            for op in self.ops[e]:
                if op.needs_inc:
                    c += 1
                    op.incval = c
        self.maxinc = {e: max([op.incval or 0 for op in self.ops[e]] + [0]) for e in ENGS}
        block = es.enter_context(nc.Block())
        hooks = {"pe": block.tensor, "act": block.scalar, "dve": block.vector, "pool": block.gpsimd,
                 "sp": block.sync}
        for e in ENGS:
            ops = self.ops[e]
            nd = self.ndma[e]

            def body(eng, e=e, ops=ops, nd=nd):
                for op in ops:
                    for p in op.waits:
                        if p.dma:
                            eng.wait_ge(dsem[p.eng][p.dsem], p.dval)
                        else:
                            eng.wait_ge(sem[p.eng], p.incval)
                    if op.dma and op.dval > 16:
                        eng.wait_ge(dsem[e][op.dsem], op.dval - 16)
                    if op.fn is None:
                        if op.needs_inc:
                            eng.nop().then_inc(sem[e], 1)
                        continue
                    ins = op.fn(eng)
                    if op.dma:
                        ins.then_inc(dsem[e][op.dsem], 16)
                    elif op.needs_inc:
                        ins.then_inc(sem[e], 1)
                if nd:
                    for i in range(min(nd, NDMASEM)):
                        last = ((nd - 1 - i) // NDMASEM) + 1
                        eng.wait_ge(dsem[e][i], 16 * last)
            if ops:
                hooks[e](body)

from concourse.bass_utils import run_bass_kernel_spmd

D = 1024
KC = 8
CTXN = 256
LATN = 2048
NT = CTXN + LATN
NLAYER = 4
FF = 2816
NPAIR = 22
EPS = 1e-6
LW = 1280
LKC = 10
LAT_SB = [0, 410, 820, 1230, 1639, 2048]
SBLOCKS = [(0, 256, 0, 256)] + [(256 + LAT_SB[i], 256 + LAT_SB[i + 1], 256, NT) for i in range(5)]
BLK512 = [(0, 256)] + [(256 + 512 * i, 256 + 512 * (i + 1)) for i in range(4)]

SM = {}
_o = 0
for _n, _w in [("cT", 40), ("ada_b", 4 * 48), ("gains", 4 * 2 * 8), ("ffn_cw", 4 * 44 * 4), ("ret_gn", 2 * 4 * 4),
               ("ret_decay", 2 * 2 * 4), ("lru_cw", 10 * 5), ("lru_gb", 2 * 2 * 10), ("lru_lam", 2 * 10),
               ]:
    SM[_n] = (_o, _o + _w)
    _o += _w
NSMALL = _o

WSPEC = {
    "ada_w": (4 * 48, 1024, False),
    "ffn_up": (4 * 22, 2048, True),
    "ffn_down": (4 * 22, 1024, True),
    "ret_in": (2 * 4, 8 * 1536, True),
    "ret_out": (2 * 4, 4 * 1024, True),
    "lru_in": (10, 2048, True),
    "lru_gate": (2 * 10, 768, True),
    "lru_out": (10, 1024, True),
    "na_in": (4, 8 * 768, True),
    "na_out": (4, 2 * 1024, True),
    "na_tab": (16, 16 * 64, True),
}


def _lru_slots(c):
    b0 = (128 * c) // 160
    b1 = (128 * c + 127) // 160
    k0 = (160 * b0) // 128
    k1 = (160 * (b1 + 1) - 1) // 128
    return list(range(k0, k1 + 1))


def host_shared(inp):
    f = np.float32
    out = {}
    aw = inp["ada_w"].reshape(4, 8, 128, 48, 128).transpose(0, 3, 2, 1, 4)
    out["ada_w"] = np.ascontiguousarray(aw).reshape(4 * 48, 128, 1024)
    wu = inp["ffn_w_up"].reshape(4, 8, 128, 2, 22, 128).transpose(0, 4, 2, 1, 3, 5)
    out["ffn_up"] = np.ascontiguousarray(wu).reshape(4 * 22, 128, 2048)
    out["ffn_down"] = np.ascontiguousarray(inp["ffn_w_down"].reshape(4 * 22, 128, 1024))
    ri = inp["ret_w_in"].reshape(2, 8, 128, 6144)
    tiles = []
    for j in range(2):
        for h in range(4):
            cols = np.concatenate([np.arange(h * 256, h * 256 + 256), 1024 + np.arange(h * 256, h * 256 + 256),
                                   2048 + np.arange(h * 512, h * 512 + 512), 4096 + np.arange(h * 512, h * 512 + 512)])
            tiles.append(ri[j][:, :, cols].transpose(1, 0, 2).reshape(128, 8 * 1536))
    out["ret_in"] = np.ascontiguousarray(np.stack(tiles))
    ro = inp["ret_w_out"].reshape(2, 4, 4, 128, 1024).transpose(0, 1, 3, 2, 4)
    out["ret_out"] = np.ascontiguousarray(ro).reshape(8, 128, 4096)
    li = inp["lru_w_in"][0].reshape(8, 128, 2, 10, 128).transpose(3, 1, 0, 2, 4)
    out["lru_in"] = np.ascontiguousarray(li).reshape(10, 128, 2048)
    gw = inp["lru_gate_w"][0]
    lg = np.zeros((2, 10, 128, 2, 3, 128), f)
    for d in range(2):
        for g in range(2):
            dense = np.zeros((LW, LW), f)
            for k in range(8):
                dense[k * 160:(k + 1) * 160, k * 160:(k + 1) * 160] = gw[d, g, k]
            for c in range(10):
                for si, kc in enumerate(_lru_slots(c)):
                    lg[d, c, :, g, si, :] = dense[kc * 128:(kc + 1) * 128, c * 128:(c + 1) * 128]
    out["lru_gate"] = lg.reshape(20, 128, 768)
    out["lru_out"] = np.ascontiguousarray(inp["lru_w_out"][0].reshape(10, 128, 1024))
    nq = inp["na_w_qkv"][0].reshape(8, 128, 3, 4, 256).transpose(3, 1, 0, 2, 4)
    out["na_in"] = np.ascontiguousarray(nq).reshape(4, 128, 8 * 768)
    no = inp["na_w_out"][0].reshape(4, 2, 128, 1024).transpose(0, 2, 1, 3)
    out["na_out"] = np.ascontiguousarray(no).reshape(4, 128, 2048)
    rpb = inp["na_rpb"][0]
    NEG = f(-30000.0)
    kc_i = np.arange(64)[:, None]
    qc_i = np.arange(64)[None, :]
    cs = np.clip(qc_i - 8, 0, 48)
    col_in = (kc_i >= cs) & (kc_i < cs + 16)
    dc = np.clip(kc_i - qc_i, -15, 15) + 15
    def blk(h, dr):
        if dr is None or dr < -7 or dr > 7:
            return np.full((64, 64), NEG, f)
        return np.where(col_in, rpb[h, dr + 7][dc], NEG).astype(f)
    tab = np.zeros((128, 16, 16, 64), f)
    for h in range(16):
        for sl in range(16):
            if sl < 14:
                d0, d1 = sl - 7, sl - 6
            elif sl == 14:
                d0, d1 = None, -4
            else:
                d0, d1 = 3, None
            tab[0:64, h, sl] = blk(h, d0)
            tab[64:128, h, sl] = blk(h, d1)
    out["na_tab"] = np.ascontiguousarray(tab.transpose(1, 0, 2, 3)).reshape(16, 128, 1024)
    sm = np.zeros((128, NSMALL), f)
    def put(name, arr):
        a, b = SM[name]
        sm[:, a:b] = arr.reshape(128, b - a)
    put("ada_b", inp["ada_b"].reshape(4, 48, 128).transpose(2, 0, 1))
    g = np.stack([inp["norm_mix"], inp["norm_ffn"]], axis=1).reshape(4, 2, 8, 128).transpose(3, 0, 1, 2)
    put("gains", g)
    cw = np.concatenate([inp["ffn_conv_w"], inp["ffn_conv_b"][:, None, :]], axis=1)
    put("ffn_cw", cw.reshape(4, 4, 44, 128).transpose(3, 0, 2, 1))
    put("ret_gn", inp["ret_gn"].reshape(2, 4, 4, 128).transpose(3, 0, 1, 2))
    put("ret_decay", np.broadcast_to(inp["ret_logit_decay"].reshape(1, 16), (128, 16)))
    lcw = np.concatenate([inp["lru_conv_w"][0], inp["lru_conv_b"][0][None]], axis=0)
    put("lru_cw", lcw.reshape(5, 10, 128).transpose(2, 1, 0))
    put("lru_gb", inp["lru_gate_b"][0].reshape(2, 2, 10, 128).transpose(3, 0, 1, 2))
    put("lru_lam", inp["lru_lambda"][0].reshape(2, 10, 128).transpose(2, 0, 1))
    half = 128
    freqs = (10000.0 ** (-np.arange(0, half, 2, dtype=np.float32) / half)).astype(f)
    p = np.arange(128)
    rc = np.zeros((128, 16, 2, 64), f)
    rs = np.zeros((128, 16, 2, 64), f)
    for m in range(16):
        row = (2 * m + p // 64).astype(f)
        col = (p % 64).astype(f)
        ar = row[:, None] * freqs[None, :]
        ac = col[:, None] * freqs[None, :]
        rc[:, m, 0] = np.cos(ar); rc[:, m, 1] = np.cos(ac)
        rs[:, m, 0] = np.sin(ar); rs[:, m, 1] = np.sin(ac)
    out["rope"] = np.concatenate([rc.reshape(128, 2048), rs.reshape(128, 2048),
                                  np.broadcast_to(np.arange(128, dtype=f)[None, :], (128, 128)),
                                  np.broadcast_to(np.arange(128, dtype=f)[:, None], (128, 8))], axis=1)
    out["fin"] = np.ascontiguousarray(np.broadcast_to(inp["final_norm"][None, :], (128, 1024))).astype(f)
    out["small"] = sm
    return out


def host_core(inp, shared_small, core, nseq):
    sm = shared_small.copy()
    c = inp["c"][core * nseq:(core + 1) * nseq]
    cT = np.zeros((128, 8, 5), np.float32)
    cT[:, :, :nseq] = c.reshape(nseq, 8, 128).transpose(2, 1, 0)
    cT[:, :, 4] = inp["c_ctx"].reshape(8, 128).T
    a, b = SM["cT"]
    sm[:, a:b] = cT.reshape(128, 40)
    return {"x": np.ascontiguousarray(inp["x"][core * nseq:(core + 1) * nseq]),
            "ctx": np.ascontiguousarray(inp["ctx"][core * nseq:(core + 1) * nseq]),
            "small": sm}


class Ctx:
    pass


def build(nseq, plan, final=True):
    nc = bass.Bass("TRN2", target_bir_lowering=False)
    es = ExitStack()
    P = Prog(nc, es)
    K = Ctx()
    K.P, K.nc, K.nseq = P, nc, nseq
    dr = {}
    dr["x"] = nc.dram_tensor("x", [nseq, LATN, D], F32, kind="ExternalInput").ap()
    dr["ctx"] = nc.dram_tensor("ctx", [nseq, CTXN, D], F32, kind="ExternalInput").ap()
    dr["small"] = nc.dram_tensor("small", [128, NSMALL], F32, kind="ExternalInput").ap()
    dr["rope"] = nc.dram_tensor("rope", [128, 4232], F32, kind="ExternalInput").ap()
    dr["fin"] = nc.dram_tensor("fin", [128, 1024], F32, kind="ExternalInput").ap()
    used = set(["ada_w", "ffn_up", "ffn_down"])
    kinds = {0: "ret", 1: "lru", 2: "na", 3: "ret"}
    for (l, hf) in plan:
        if hf == "m":
            used.update({"ret": ["ret_in", "ret_out"], "lru": ["lru_in", "lru_gate", "lru_out"],
                         "na": ["na_in", "na_out", "na_tab"]}[kinds[l]])
    K.used = used
    for n, (nt, te, cast) in WSPEC.items():
        dr[n] = nc.dram_tensor(n, [nt, 128, te], F32, kind="ExternalInput").ap()
        P.trk[n] = (None, 128 * te, {})
    K.y = nc.dram_tensor("y", [nseq, LATN, D], F32, kind="ExternalOutput").ap()
    P.trk["y"] = (None, 128 * D, {})
    if not final:
        K.yc = nc.dram_tensor("yc", [nseq, CTXN, D], F32, kind="ExternalOutput").ap()
        P.trk["yc"] = (None, 128 * D, {})
    K.dr = dr
    K.wb = {}
    for n, (nt, te, cast) in WSPEC.items():
        if cast and n in used:
            K.wb[n] = P.dram("wb_" + n, [nt, 128, te], BF16, chunk=128 * te)
    layers_needed = sorted(set(l for l, _ in plan))
    order = []
    for l in layers_needed:
        kd = kinds[l]
        j = l // 3
        if (l, "m") in plan:
            if kd == "ret":
                order += [("ret_in", j * 4, j * 4 + 4), ("ret_out", j * 4, j * 4 + 4)]
            elif kd == "lru":
                order += [("lru_in", 0, 10), ("lru_gate", 0, 20), ("lru_out", 0, 10)]
            else:
                order += [("na_in", 0, 4), ("na_out", 0, 4), ("na_tab", 0, 16)]
        if (l, "f") in plan:
            order += [("ffn_up", l * 22, l * 22 + 22), ("ffn_down", l * 22, l * 22 + 22)]
    for n, a, b in order:
        step = max(1, (1 << 20) // (128 * WSPEC[n][1] * 4) * 2)
        for i in range(a, b, step):
            e = min(b, i + step)
            P.dma(K.wb[n][i:e], dr[n][i:e], q="pool")

    K.pb = [P.ps("pb%d" % i, [128, 512], F32, chunk=1 << 30) for i in range(8)]
    K.small = P.sb("small_sb", [128, NSMALL], F32, chunk=64)
    K.xT = P.sb("xT", [128, KC, NT], F32, chunk=128)
    K.rstd = P.sb("rstd", [128, NT], F32, chunk=128)
    K.identf = P.sb("identf", [128, 128], F32)
    K.ident = P.sb("ident", [128, 128], BF16)
    K.onesb = P.sb("onesb", [128, 128], BF16)
    K.modT = P.sb("modT", [128, NLAYER, 6, 8, 5], F32, chunk=40)
    K.A = P.sb("Amod", [128, NLAYER, 2, 8, 5], F32, chunk=40)
    K.tmpf = [P.sb("tmpf%d" % i, [128, 512], F32) for i in range(2)]

    def sm(name):
        a, b = SM[name]
        return K.small[:, a:b]
    K.sm = sm
    P.dma(K.small[:], dr["small"], q="sp")
    P.memset(K.identf[:], 0.0)
    P.add("pool", lambda e: e.affine_select(K.identf[:], K.identf[:], [[-1, 128]], ALU.not_equal, 1.0, base=0,
                                            channel_multiplier=1), reads=(K.identf[:],), writes=(K.identf[:],))
    P.copy(K.ident[:], K.identf[:])
    P.memset(K.onesb[:], 1.0)

    prologue(K, layers_needed)
    for s in range(nseq):
        load_seq(K, s)
        for (l, hf) in plan:
            need_ctx = l < NLAYER - 1
            if hf == "m":
                kd = kinds[l]
                if kd == "ret":
                    ret_mixer(K, l, s, need_ctx)
                elif kd == "lru":
                    lru_mixer(K, l, s, need_ctx)
                else:
                    na_mixer(K, l, s, need_ctx)
            else:
                ffn(K, l, s, need_ctx)
        store_seq(K, s, final)
    P.emit()
    es.close()
    return nc, K


def prologue(K, layers):
    P = K.P
    with P.scope():
        sc = P.sb("silu_c", [128, 8, 5], F32)
        wt = [P.sb("adaw%d" % i, [128, 8, 128], F32) for i in range(3)]
        a, b = SM["cT"]
        P.act(sc[:].rearrange("p k s -> p (k s)"), K.small[:, a:b], AF.Silu)
        n = 0
        for l in layers:
            bank = K.pb[l % 2]
            for oc in range(48):
                w = wt[n % 3]
                n += 1
                P.dma(w[:].rearrange("p k c -> p (k c)"), K.dr["ada_w"][l * 48 + oc], q="sp")
                for kc in range(8):
                    P.mm(bank[:, oc * 5:oc * 5 + 5], w[:, kc, :], sc[:, kc, :], start=(kc == 0), stop=(kc == 7))
            a, b = SM["ada_b"]
            ab = K.small[:, a + l * 48:a + l * 48 + 48].unsqueeze(2).broadcast_to([128, 48, 5])
            P.tt(K.modT[:, l].rearrange("p m k s -> p (m k) s"), bank[:, 0:240].rearrange("p (o s) -> p o s", s=5), ab, ALU.add)
            a, b = SM["gains"]
            for w_ in range(2):
                g = K.small[:, a + (l * 2 + w_) * 8:a + (l * 2 + w_) * 8 + 8].unsqueeze(2).broadcast_to([128, 8, 5])
                P.ts(K.A[:, l, w_], K.modT[:, l, 1 + 3 * w_], 1.0, None, ALU.add)
                P.tt(K.A[:, l, w_], K.A[:, l, w_], g, ALU.mult)


def load_seq(K, s):
    P = K.P
    with P.scope():
        st = [P.sb("ldst%d" % i, [128, D], F32) for i in range(2)]
        for ti in range(18):
            src = K.dr["ctx"][s, ti * 128:(ti + 1) * 128, :] if ti < 2 else K.dr["x"][s, (ti - 2) * 128:(ti - 1) * 128, :]
            b = st[ti % 2]
            P.dma(b[:], src, q="sp")
            for hf in range(2):
                bank = K.pb[(ti * 2 + hf) % 4]
                for k in range(4):
                    P.tr(bank[:, k * 128:(k + 1) * 128], b[:, (hf * 4 + k) * 128:(hf * 4 + k + 1) * 128], K.identf[:])
                P.copy(K.xT[:, hf * 4:hf * 4 + 4, ti * 128:(ti + 1) * 128], bank[:].rearrange("p (k t) -> p k t", k=4),
                       eng=("act" if hf else "dve"))


def store_seq(K, s, final):
    P = K.P
    with P.scope():
        st = [P.sb("stst%d" % i, [128, D], F32) for i in range(2)]
        junk = P.sb("stjunk", [128, D], F32)
        ss = P.sb("stss", [128, 4], F32, chunk=1)
        fing = P.sb("fing", [128, D], F32)
        if final:
            P.dma(fing[:], K.dr["fin"], q="sp")
        tiles = range(2, 18) if final else range(18)
        for n, ti in enumerate(tiles):
            b = st[n % 2]
            banks = [K.pb[(n % 2) * 2], K.pb[(n % 2) * 2 + 1]]
            for kc in range(8):
                P.tr(banks[kc // 4][:, (kc % 4) * 128:(kc % 4 + 1) * 128], K.xT[:, kc, ti * 128:(ti + 1) * 128], K.identf[:])
            if final:
                q = ss[:, (n % 2) * 2:(n % 2) * 2 + 2]
                for hf in range(2):
                    P.act(junk[:, hf * 512:(hf + 1) * 512], banks[hf][:], AF.Square, accum_out=q[:, hf:hf + 1])
                P.tt(q[:, 0:1], q[:, 0:1], q[:, 1:2], ALU.add)
                P.act(q[:, 0:1], q[:, 0:1], AF.Sqrt, scale=1.0 / D, bias=EPS)
                P.add("dve", lambda e, q=q: e.reciprocal(q[:, 0:1], q[:, 0:1]), reads=(q[:, 0:1],), writes=(q[:, 0:1],))
                for hf in range(2):
                    P.stt(b[:, hf * 512:(hf + 1) * 512], banks[hf][:], q[:, 0:1], fing[:, hf * 512:(hf + 1) * 512],
                          ALU.mult, ALU.mult)
            else:
                for hf in range(2):
                    P.copy(b[:, hf * 512:(hf + 1) * 512], banks[hf][:], eng=("act" if hf else "dve"))
            if ti < 2:
                dst = K.yc[s, ti * 128:(ti + 1) * 128, :]
            else:
                dst = K.y[s, (ti - 2) * 128:(ti - 1) * 128, :]
            P.dma(dst, b[:], q="sp")


def compute_rstd(K, cols):
    P = K.P
    for bi, (a, b) in enumerate(cols):
        n = b - a
        bank = K.pb[6 + bi % 2]
        for kc in range(8):
            t = K.tmpf[kc % 2][:].bitcast(BF16)[:, 0:512]
            if kc % 2 == 0:
                P.act(t[:, :n], K.xT[:, kc, a:b], AF.Square)
            else:
                P.tt(t[:, :n], K.xT[:, kc, a:b], K.xT[:, kc, a:b], ALU.mult)
            P.mm(bank[:, :n], K.onesb[:], t[:, :n], start=(kc == 0), stop=(kc == 7))
        P.act(K.rstd[:, a:b], bank[:, :n], AF.Sqrt, scale=1.0 / D, bias=EPS)
        P.add("dve", lambda e, a=a, b=b: e.reciprocal(K.rstd[:, a:b], K.rstd[:, a:b]),
              reads=(K.rstd[:, a:b],), writes=(K.rstd[:, a:b],))


def make_h(K, dst, a, b, l, w, s, off=0):
    P = K.P
    n = b - a
    for kc in range(8):
        t = K.tmpf[kc % 2]
        P.stt(t[:, :n], K.xT[:, kc, a:b], K.A[:, l, w, kc, s:s + 1], K.rstd[:, a:b], ALU.mult, ALU.mult)
        P.act(dst[:, kc, off:off + n], t[:, :n], AF.Identity, bias=K.modT[:, l, 3 * w, kc, s:s + 1])


def ffn(K, l, s, need_ctx):
    P = K.P
    blocks = SBLOCKS if need_ctx else SBLOCKS[1:]
    compute_rstd(K, BLK512 if need_ctx else BLK512[1:])
    a0, _ = SM["ffn_cw"]
    with P.scope():
        wd = P.sb("ffn_wd", [128, NPAIR, D], BF16, chunk=D)
        wu = [P.sb("ffn_wu%d" % i, [128, 8, 256], BF16) for i in range(3)]
        hbs = [P.sb("ffn_h%d" % i, [128, 8, 512], BF16, chunk=512) for i in range(2)]
        aT = P.sb("ffn_a", [128, NPAIR, 512], BF16, chunk=512)
        hs = P.sb("ffn_hs", [128, 8, 1], BF16)
        tg = [P.sb("ffn_tg%d" % i, [128, 512], F32) for i in range(2)]
        tv = [P.sb("ffn_tv%d" % i, [128, 512], F32) for i in range(2)]
        for i in range(0, NPAIR, 2):
            P.dma(wd[:, i:i + 2, :], K.wb["ffn_down"][l * 22 + i:l * 22 + i + 2].rearrange("i p c -> p i c"), q="sp")
        nb = 0

        def geom(blk):
            c0, c1, s0, s1 = blk
            return c0, c1, max(c0 - 1, s0), min(c1 + 1, s1), (4 if c0 < CTXN else s)

        def build_h(k):
            c0, c1, lo, hi, st = geom(blocks[k])
            hb = hbs[k % 2]
            if lo < c0:
                make_h(K, hb, c0, hi, l, 1, st, off=1)
                P.copy(hb[:, :, 0:1], hs[:], eng="pool")
            else:
                make_h(K, hb, lo, hi, l, 1, st)
            P.copy(hs[:], hb[:, :, c1 - 1 - lo:c1 - lo], eng="pool")

        build_h(0)
        for k, blk in enumerate(blocks):
            c0, c1, lo, hi, st = geom(blk)
            hb = hbs[k % 2]
            n, W = hi - lo, c1 - c0
            for i in range(NPAIR):
                if i == 14 and k + 1 < len(blocks):
                    build_h(k + 1)
                w = wu[i % 3]
                P.dma(w[:].rearrange("p k c -> p (k c)"), K.wb["ffn_up"][l * 22 + i], q="sp")
                outs = []
                for hf in range(2):
                    bank = K.pb[nb % 6]
                    nb += 1
                    for kc in range(8):
                        P.mm(bank[:, :n], w[:, kc, hf * 128:(hf + 1) * 128], hb[:, kc, :n], start=(kc == 0), stop=(kc == 7))
                    t = (tg if hf == 0 else tv)[i % 2]
                    ch = i + 22 * hf
                    cw = K.small[:, a0 + (l * 44 + ch) * 4:a0 + (l * 44 + ch) * 4 + 4]
                    o = c0 - lo
                    P.act(t[:, :W], bank[:, o:o + W], AF.Identity, scale=cw[:, 1:2], bias=cw[:, 3:4])
                    if o == 1:
                        P.stt(t[:, :W], bank[:, 0:W], cw[:, 0:1], t[:, :W], ALU.mult, ALU.add)
                    else:
                        P.stt(t[:, 1:W], bank[:, 0:W - 1], cw[:, 0:1], t[:, 1:W], ALU.mult, ALU.add)
                    if hi == c1 + 1:
                        P.stt(t[:, :W], bank[:, o + 1:o + 1 + W], cw[:, 2:3], t[:, :W], ALU.mult, ALU.add)
                    else:
                        P.stt(t[:, :W - 1], bank[:, o + 1:o + W], cw[:, 2:3], t[:, :W - 1], ALU.mult, ALU.add)
                    outs.append(t)
                P.act(outs[0][:, :W], outs[0][:, :W], AF.Silu)
                P.tt(aT[:, i, :W], outs[0][:, :W], outs[1][:, :W], ALU.mult)
            for j in range(8):
                bank = K.pb[6 + j % 2]
                for i in range(NPAIR):
                    P.mm(bank[:, :W], wd[:, i, j * 128:(j + 1) * 128], aT[:, i, :W], start=(i == 0), stop=(i == NPAIR - 1))
                P.stt(K.xT[:, j, c0:c1], bank[:, :W], K.modT[:, l, 5, j, st:st + 1], K.xT[:, j, c0:c1], ALU.mult, ALU.add)


def ret_mixer(K, l, s, need_ctx):
    P = K.P
    j = l // 3
    compute_rstd(K, BLK512)
    a_dec, _ = SM["ret_decay"]
    a_gn, _ = SM["ret_gn"]
    if not hasattr(K, "sstate"):
        K.sstate = P.dram("sb_state", [18, 128, 1024], BF16, chunk=128 * 1024)
        K.kvc = P.dram("kv_cache", [18, 128, 768], BF16, chunk=128 * 768)
    with P.scope():
        hT = P.sb("rt_h", [128, 8, NT], BF16, chunk=128)
        for (a, b) in BLK512:
            make_h(K, hT[:, :, a:b], a, b, l, 0, 4 if a < CTXN else s)
        rc = P.sb("rt_rc", [128, 16, 2, 64], BF16)
        rs = P.sb("rt_rs", [128, 16, 2, 64], BF16)
        io = P.sb("rt_io", [128, 136], F32)
        P.dma(rc[:].rearrange("p m h f -> p (m h f)"), K.dr["rope"][:, 0:2048], q="pool")
        P.dma(rs[:].rearrange("p m h f -> p (m h f)"), K.dr["rope"][:, 2048:4096], q="pool")
        P.dma(io[:], K.dr["rope"][:, 4096:4232], q="sp")
        io_row, io_p = io[:, 0:128], io[:, 128:129]
        lg = P.sb("rt_lg", [128, 8], F32)
        ct = P.sb("rt_ct", [128, 8], F32)
        nlg = P.sb("rt_nlg", [128, 8], F32)
        lg127 = P.sb("rt_lg127", [128, 8], F32)
        lg128 = P.sb("rt_lg128", [128, 8], F32)
        gch = P.sb("rt_gch", [128, 8], F32)
        zeta = P.sb("rt_zeta", [128, 8], F32)
        logit = K.small[:, a_dec + j * 8:a_dec + j * 8 + 8]
        P.act(ct[:], logit, AF.Abs)
        P.act(ct[:], ct[:], AF.Exp, scale=-1.0)
        P.act(ct[:], ct[:], AF.Ln, bias=1.0)
        P.ts(nlg[:], logit, -1.0, 0.0, ALU.mult, ALU.max)
        P.tt(nlg[:], nlg[:], ct[:], ALU.add)
        P.ts(lg[:], nlg[:], -1.0, None, ALU.mult)
        P.ts(lg127[:], lg[:], 127.0, None, ALU.mult)
        P.ts(lg128[:], lg[:], 128.0, None, ALU.mult)
        P.act(gch[:], lg128[:], AF.Exp)
        MT = P.sb("rt_MT", [128, 4, 128], BF16)
        XI = P.sb("rt_XI", [128, 8, 128], BF16)
        with P.scope():
            Dm = P.sb("rt_D", [128, 128], F32)
            Dp = P.sb("rt_Dp", [128, 128], F32)
            Dn = P.sb("rt_Dn", [128, 128], F32)
            ge0 = P.sb("rt_ge0", [128, 128], F32)
            le0 = P.sb("rt_le0", [128, 128], F32)
            e1 = P.sb("rt_e1", [128, 128], F32)
            e2 = P.sb("rt_e2", [128, 128], F32)
            P.ts(Dm[:], io_row, io_p, None, ALU.subtract)
            P.ts(Dp[:], Dm[:], 0.0, None, ALU.max)
            P.ts(Dn[:], Dm[:], -1.0, 0.0, ALU.mult, ALU.max)
            P.ts(ge0[:], Dm[:], 0.0, None, ALU.is_ge)
            P.ts(le0[:], Dm[:], 0.0, None, ALU.is_le)
            for h in range(4):
                P.act(e1[:], Dp[:], AF.Exp, scale=lg[:, h:h + 1])
                P.tt(e1[:], e1[:], ge0[:], ALU.mult)
                P.act(e2[:], Dn[:], AF.Exp, scale=lg[:, 4 + h:5 + h])
                P.tt(e2[:], e2[:], le0[:], ALU.mult)
                P.tt(MT[:, h, :], e1[:], e2[:], ALU.add)
                P.act(XI[:, h, :], io_row, AF.Exp, scale=lg[:, h:h + 1], bias=lg[:, h:h + 1])
                P.act(XI[:, 4 + h, :], io_row, AF.Exp, scale=nlg[:, 4 + h:5 + h], bias=lg128[:, 4 + h:5 + h])
                P.act(zeta[:, h:h + 1], io_p, AF.Exp, scale=nlg[:, h:h + 1], bias=lg127[:, h:h + 1])
                P.act(zeta[:, 4 + h:5 + h], io_p, AF.Exp, scale=lg[:, 4 + h:5 + h])
        P.ts(zeta[:], zeta[:], 1.0 / 16.0, None, ALU.mult)

        mhalf = P.sb("rt_mhalf", [128, 1], F32)
        P.memset(mhalf[:], -0.5)
        W = P.sb("rt_w", [128, 8, 1536], BF16)
        Wo = P.sb("rt_wo", [128, 4, D], BF16, chunk=D)
        Sm = P.sb("rt_S", [128, 2, 512], F32)
        Sb = [P.sb("rt_Sb%d" % i, [128, 2, 512], BF16) for i in range(2)]
        Sl = [P.sb("rt_Sl%d" % i, [128, 2, 512], BF16) for i in range(2)]
        tu = K.tmpf[0][:, 0:256].rearrange("p (h r f) -> p h r f", h=2, r=2)
        tv_ = K.tmpf[0][:, 256:512].rearrange("p (h r f) -> p h r f", h=2, r=2)
        kvb = [P.sb("rt_kv%d" % i, [128, 768], BF16, chunk=64) for i in range(2)]
        krot = [kvb[i][:, 0:256] for i in range(2)]
        qrot = [P.sb("rt_qr%d" % i, [128, 256], BF16) for i in range(2)]
        kz = [P.sb("rt_kz%d" % i, [128, 256], BF16) for i in range(2)]
        vb = [kvb[i][:, 256:768] for i in range(2)]
        sg = [P.sb("rt_sg%d" % i, [128, 512], BF16) for i in range(2)]
        qT3 = [P.sb("rt_qT%d" % i, [128, 3, 2, 128], BF16) for i in range(2)]
        kT = [P.sb("rt_kT%d" % i, [128, 2, 128], BF16) for i in range(2)]
        inn = [P.sb("rt_in%d" % i, [128, 128], BF16) for i in range(2)]
        yn = [P.sb("rt_yn0", [128, 512], BF16)] * 2
        zz = [P.sb("rt_z0", [128, 512], BF16)] * 2
        zT = [P.sb("rt_zT0", [128, 4, 512], BF16)] * 2
        st6 = [P.sb("rt_st%d" % i, [128, 6], F32) for i in range(2)]
        mv = [P.sb("rt_mv%d" % i, [128, 4], F32) for i in range(2)]
        pb = K.pb
        tb3 = pb[3][:].bitcast(BF16)
        tb7 = pb[7][:].bitcast(BF16)

        def rope(src, dst, m):
            x4 = src.rearrange("p (h r f) -> p h r f", h=2, r=2)
            d4 = dst.rearrange("p (h r f) -> p h r f", h=2, r=2)
            C = rc[:, m].unsqueeze(2).broadcast_to([128, 2, 2, 64])
            S_ = rs[:, m].unsqueeze(2).broadcast_to([128, 2, 2, 64])
            P.tt(tu, x4, C, ALU.mult)
            P.tt(tv_, x4[:, :, ::-1, :], S_, ALU.mult)
            P.tt(d4[:, :, 0, :], tu[:, :, 0, :], tv_[:, :, 0, :], ALU.subtract)
            P.tt(d4[:, :, 1, :], tu[:, :, 1, :], tv_[:, :, 1, :], ALU.add)

        def kv_proj(t, q, with_q=False):
            cols = slice(t * 128, (t + 1) * 128)
            if with_q:
                for kc in range(8):
                    P.mm(pb[0][:], hT[:, kc, cols], W[:, kc, 0:512], start=(kc == 0), stop=(kc == 7))
            else:
                for kc in range(8):
                    P.mm(pb[0][:, 256:512], hT[:, kc, cols], W[:, kc, 256:512], start=(kc == 0), stop=(kc == 7))
            for kc in range(8):
                P.mm(pb[1][:], hT[:, kc, cols], W[:, kc, 512:1024], start=(kc == 0), stop=(kc == 7))
            if t >= 2:
                rope(pb[0][:, 256:512], krot[q][:], t - 2)
            else:
                P.copy(krot[q][:], pb[0][:, 256:512])
            P.copy(vb[q][:], pb[1][:], eng="act")

        def s_update(q, h, d):
            for c2 in range(2):
                P.mm(pb[5 + c2][:], kz[q][:, c2 * 128:(c2 + 1) * 128], vb[q][:])
            for c2 in range(2):
                P.stt(Sm[:, c2, :], Sm[:, c2, :], gch[:, d * 4 + h:d * 4 + h + 1], pb[5 + c2][:], ALU.mult, ALU.add)

        nq = 0
        for h in range(4):
            P.dma(W[:].rearrange("p k c -> p (k c)"), K.wb["ret_in"][j * 4 + h], q="sp")
            P.dma(Wo[:].rearrange("p k c -> p (k c)"), K.wb["ret_out"][j * 4 + h], q="sp")
            for kc in range(4):
                g_ = K.small[:, a_gn + (j * 4 + h) * 4 + kc:a_gn + (j * 4 + h) * 4 + kc + 1]
                P.act(Wo[:, kc, :], Wo[:, kc, :], AF.Copy, scale=g_)
            P.memset(Sm[:], 0.0)
            order1 = [1, 0] + list(range(17, 1, -1))
            q0 = nq % 2
            kv_proj(order1[0], q0)
            P.dma(K.kvc[order1[0]], kvb[q0][:], q="pool")
            P.ts(kz[q0][:], krot[q0][:], zeta[:, 4 + h:5 + h], None, ALU.mult)
            for idx, t in enumerate(order1):
                q = nq % 2
                nq += 1
                P.copy(Sb[q][:], Sm[:], eng="act")
                if need_ctx or t >= 2:
                    P.dma(K.sstate[t], Sb[q][:].rearrange("p c v -> p (c v)"), q="pool")
                if t == 2:
                    break
                tn = order1[idx + 1]
                qn = nq % 2
                kv_proj(tn, qn)
                P.dma(K.kvc[tn], kvb[qn][:], q="pool")
                if tn != 2:
                    P.ts(kz[qn][:], krot[qn][:], zeta[:, 4 + h:5 + h], None, ALU.mult)
                s_update(q, h, 1)
            P.memset(Sm[:], 0.0)
            base = nq
            nq += 18

            def A2(t):
                q = (base + t) % 2
                cols = slice(t * 128, (t + 1) * 128)
                active = need_ctx or t >= 2
                P.dma(kvb[q][:], K.kvc[t], q="sp")
                if active:
                    P.dma(Sl[q][:].rearrange("p c v -> p (c v)"), K.sstate[t], q="sp")
                    for kc in range(8):
                        P.mm(pb[0][:, 0:256], hT[:, kc, cols], W[:, kc, 0:256], start=(kc == 0), stop=(kc == 7))
                    for kc in range(8):
                        P.mm(pb[2][:], hT[:, kc, cols], W[:, kc, 1024:1536], start=(kc == 0), stop=(kc == 7))
                    if t >= 2:
                        rope(pb[0][:, 0:256], qrot[q][:], t - 2)
                    else:
                        P.copy(qrot[q][:], pb[0][:, 0:256])
                    P.act(sg[q][:], pb[2][:], AF.Silu)

            def B2(t):
                q = (base + t) % 2
                active = need_ctx or t >= 2
                P.ts(kz[q][:], krot[q][:], zeta[:, h:h + 1], None, ALU.mult)
                if active:
                    for c2 in range(2):
                        P.tr(tb3[:, c2 * 128:(c2 + 1) * 128], qrot[q][:, c2 * 128:(c2 + 1) * 128], K.ident[:])
                        P.tr(tb3[:, 256 + c2 * 128:256 + (c2 + 1) * 128], krot[q][:, c2 * 128:(c2 + 1) * 128], K.ident[:])
                    tq = tb3[:, 0:256].rearrange("p (c t) -> p c t", c=2)
                    P.act(qT3[q][:, 0], tq, AF.Copy, scale=1.0 / 16.0)
                    P.tt(qT3[q][:, 1], tq, XI[:, h, :].unsqueeze(1).broadcast_to([128, 2, 128]), ALU.mult)
                    P.tt(qT3[q][:, 2], tq, XI[:, 4 + h, :].unsqueeze(1).broadcast_to([128, 2, 128]), ALU.mult)
                    P.copy(kT[q][:], tb3[:, 256:512].rearrange("p (c t) -> p c t", c=2), eng="act")
                    for c2 in range(2):
                        P.mm(pb[3][:, 256:384], kT[q][:, c2, :], qT3[q][:, 0, c2, :], start=(c2 == 0), stop=(c2 == 1))
                    P.tt(inn[q][:], pb[3][:, 256:384], MT[:, h, :], ALU.mult)

            def C2(t):
                q = (base + t) % 2
                active = need_ctx or t >= 2
                if active:
                    P.mm(pb[4][:], inn[q][:], vb[q][:], start=True, stop=False)
                    for c2 in range(2):
                        P.mm(pb[4][:], qT3[q][:, 1, c2, :], Sb[q][:, c2, :], start=False, stop=False)
                    for c2 in range(2):
                        P.mm(pb[4][:], qT3[q][:, 2, c2, :], Sl[q][:, c2, :], start=False, stop=(c2 == 1))
                    P.add("dve", lambda e, q=q: e.bn_stats(st6[q][:], pb[4][:]), reads=(pb[4][:],), writes=(st6[q][:],))
                    P.add("dve", lambda e, q=q: e.bn_aggr(mv[q][:, 0:2], st6[q][:]), reads=(st6[q][:],), writes=(mv[q][:, 0:2],))
                    P.ts(mv[q][:, 2:3], mv[q][:, 1:2], EPS, None, ALU.add)
                    P.tt(mv[q][:, 2:3], mv[q][:, 2:3], mhalf[:], ALU.pow, eng="pool")
                    P.stt(mv[q][:, 3:4], mv[q][:, 0:1], -1.0, mv[q][:, 2:3], ALU.mult, ALU.mult)
                    P.act(yn[q][:], pb[4][:], AF.Identity, scale=mv[q][:, 2:3], bias=mv[q][:, 3:4])
                    P.tt(zz[q][:], yn[q][:], sg[q][:], ALU.mult)
                    blk = 0 if t < 2 else 1 + (t - 2) // 4
                    pos = t if t < 2 else (t - 2) % 4
                    zb = zT[blk % 2]
                    for c4 in range(4):
                        P.tr(tb7[:, c4 * 128:(c4 + 1) * 128], zz[q][:, c4 * 128:(c4 + 1) * 128], K.ident[:])
                    P.copy(zb[:, :, pos * 128:(pos + 1) * 128], tb7[:, 0:512].rearrange("p (c t) -> p c t", c=4), eng="act")
                s_update(q, h, 0)
                if t + 1 < 18:
                    P.copy(Sb[(base + t + 1) % 2][:], Sm[:], eng="act")
                if active and (t == 1 or (t >= 2 and (t - 2) % 4 == 3)):
                    a, b = BLK512[blk]
                    n = b - a
                    st = 4 if a < CTXN else s
                    for jj in range(8):
                        bank = pb[5 + jj % 2]
                        for c4 in range(4):
                            P.mm(bank[:, :n], Wo[:, c4, jj * 128:(jj + 1) * 128], zb[:, c4, :n], start=(c4 == 0), stop=(c4 == 3))
                        P.stt(K.xT[:, jj, a:b], bank[:, :n], K.modT[:, l, 2, jj, st:st + 1], K.xT[:, jj, a:b], ALU.mult, ALU.add)

            P.copy(Sb[base % 2][:], Sm[:], eng="act")
            A2(0)
            B2(0)
            for t in range(18):
                if t + 1 < 18:
                    A2(t + 1)
                    B2(t + 1)
                C2(t)


def lru_mixer(K, l, s, need_ctx):
    P = K.P
    compute_rstd(K, BLK512)
    a_cw, _ = SM["lru_cw"]
    a_gb, _ = SM["lru_gb"]
    a_lam, _ = SM["lru_lam"]
    with P.scope():
        xc = P.sb("lru_xc", [128, LKC, NT], BF16, chunk=128)
        gy = P.sb("lru_gy", [128, LKC, NT], BF16, chunk=128)
        coef = P.sb("lru_coef", [128, 20], F32)
        ct = P.sb("lru_ct", [128, 20], F32)
        lam = K.small[:, a_lam:a_lam + 20]
        P.act(ct[:], lam, AF.Abs)
        P.act(ct[:], ct[:], AF.Exp, scale=-1.0)
        P.act(ct[:], ct[:], AF.Ln, bias=1.0)
        P.ts(coef[:], lam, -1.0, 0.0, ALU.mult, ALU.max)
        P.tt(coef[:], coef[:], ct[:], ALU.add)
        P.ts(coef[:], coef[:], -4.0, None, ALU.mult)
        hgb = P.sb("lru_hgb", [128, 40], F32)
        P.ts(hgb[:], K.small[:, a_gb:a_gb + 40], 0.5, None, ALU.mult)
        nb = 0
        with P.scope():
            hb = P.sb("lru_h", [128, 8, 416], BF16, chunk=416)
            wi = [P.sb("lru_wi%d" % i, [128, 8, 256], BF16) for i in range(2)]
            tt_ = [P.sb("lru_t%d" % i, [128, 416], F32) for i in range(4)]
            for (c0, c1, s0, s1) in SBLOCKS:
                st = 4 if c0 < CTXN else s
                lo, hi = max(c0 - 2, s0), min(c1 + 1, s1)
                n, W = hi - lo, c1 - c0
                o = c0 - lo
                make_h(K, hb, lo, hi, l, 0, st)
                for c in range(LKC):
                    w = wi[c % 2]
                    P.dma(w[:].rearrange("p k c -> p (k c)"), K.wb["lru_in"][c], q="sp")
                    by = K.pb[nb % 6]
                    bx = K.pb[(nb + 1) % 6]
                    nb += 2
                    for kc in range(8):
                        P.mm(by[:, :W], w[:, kc, 0:128], hb[:, kc, o:o + W], start=(kc == 0), stop=(kc == 7))
                    for kc in range(8):
                        P.mm(bx[:, :n], w[:, kc, 128:256], hb[:, kc, :n], start=(kc == 0), stop=(kc == 7))
                    t1 = tt_[(c % 2) * 2]
                    P.act(t1[:, :W], by[:, :W], AF.Square)
                    P.act(t1[:, :W], t1[:, :W], AF.Identity, scale=0.044715, bias=1.0)
                    P.tt(t1[:, :W], t1[:, :W], by[:, :W], ALU.mult)
                    P.act(t1[:, :W], t1[:, :W], AF.Sigmoid, scale=1.5957691216057308)
                    P.tt(gy[:, c, c0:c1], t1[:, :W], by[:, :W], ALU.mult)
                    t2 = tt_[(c % 2) * 2 + 1]
                    cw = K.small[:, a_cw + c * 5:a_cw + c * 5 + 5]
                    P.act(t2[:, :W], bx[:, o:o + W], AF.Identity, scale=cw[:, 2:3], bias=cw[:, 4:5])
                    taps = [(-2, 0), (-1, 1), (1, 3)]
                    for ti_, (dl, wk) in enumerate(taps):
                        cs = max(c0, lo - dl)
                        ce = min(c1, hi - dl)
                        dst = xc[:, c, cs:ce] if ti_ == 2 else t2[:, cs - c0:ce - c0]
                        P.stt(dst, bx[:, cs + dl - lo:ce + dl - lo], cw[:, wk:wk + 1], t2[:, cs - c0:ce - c0], ALU.mult, ALU.add)
                    if ce < c1:
                        P.copy(xc[:, c, ce:c1], t2[:, ce - c0:W])
        with P.scope():
            hf = P.sb("lru_hf", [128, NT], BF16, chunk=128)
            gwt = [P.sb("lru_gw%d" % i, [128, 2, 3, 128], BF16) for i in range(2)]
            tr_ = [P.sb("lru_r%d" % i, [128, 512], F32) for i in range(2)]
            ti2 = [P.sb("lru_i%d" % i, [128, 512], F32) for i in range(2)]
            ta = [P.sb("lru_a%d" % i, [128, 512], F32) for i in range(2)]
            th = K.tmpf
            nq = 0
            for c in range(LKC):
                slots = _lru_slots(c)
                for d in range(2):
                    gw = gwt[(c * 2 + d) % 2]
                    P.dma(gw[:].rearrange("p g s c -> p (g s c)"), K.wb["lru_gate"][d * 10 + c], q="sp")
                    order = BLK512 if d == 0 else [BLK512[0]] + BLK512[:0:-1]
                    gb = lambda g: hgb[:, (d * 2 + g) * 10 + c:(d * 2 + g) * 10 + c + 1]
                    cf = coef[:, d * 10 + c:d * 10 + c + 1]
                    state = {"prev": None}

                    def G(k):
                        a, b = order[k]
                        n = b - a
                        br = K.pb[(2 * k) % 6]
                        bi = K.pb[(2 * k + 1) % 6]
                        for g, bank in ((0, br), (1, bi)):
                            for si, kc in enumerate(slots):
                                P.mm(bank[:, :n], gw[:, g, si, :], xc[:, kc, a:b], start=(si == 0), stop=(si == len(slots) - 1))
                        r, i_, av = tr_[k % 2], ti2[k % 2], ta[k % 2]
                        P.act(r[:, :n], br[:, :n], AF.Tanh, scale=0.5, bias=gb(0))
                        P.act(i_[:, :n], bi[:, :n], AF.Tanh, scale=0.5, bias=gb(1))
                        P.act(av[:, :n], r[:, :n], AF.Exp, scale=cf, bias=cf)
                        P.tt(r[:, :n], av[:, :n], av[:, :n], ALU.mult)

                    def S(k):
                        a, b = order[k]
                        n = b - a
                        r, i_, av, hh = tr_[k % 2], ti2[k % 2], ta[k % 2], th[k % 2]
                        prev = state["prev"]
                        P.act(r[:, :n], r[:, :n], AF.Sqrt, scale=-1.0, bias=1.0 + 1e-6)
                        P.stt(i_[:, :n], i_[:, :n], 1.0, xc[:, c, a:b], ALU.add, ALU.mult)
                        P.stt(i_[:, :n], i_[:, :n], 0.5, r[:, :n], ALU.mult, ALU.mult)
                        init = 0.0 if prev is None else prev
                        rd = [av[:, :n], i_[:, :n]] + ([] if prev is None else [prev])
                        if d == 0:
                            P.add("dve", lambda e, hh=hh, av=av, i_=i_, n=n, init=init: e.tensor_tensor_scan(
                                hh[:, :n], av[:, :n], i_[:, :n], init, ALU.mult, ALU.add), reads=rd, writes=(hh[:, :n],))
                            state["prev"] = hh[:, n - 1:n]
                            P.copy(hf[:, a:b], hh[:, :n], eng="pool")
                        else:
                            P.add("dve", lambda e, hh=hh, av=av, i_=i_, n=n, init=init: e.tensor_tensor_scan(
                                hh[:, 0:n][:, ::-1], av[:, 0:n][:, ::-1], i_[:, 0:n][:, ::-1], init, ALU.mult, ALU.add),
                                reads=rd, writes=(hh[:, :n],))
                            state["prev"] = hh[:, 0:1]
                            P.tt(r[:, :n], hh[:, :n], hf[:, a:b], ALU.add)
                            P.tt(gy[:, c, a:b], r[:, :n], gy[:, c, a:b], ALU.mult)

                    k0 = 0
                    while k0 < len(order):
                        ks = list(range(k0, min(k0 + 2, len(order))))
                        for k in ks:
                            G(k)
                        for k in ks:
                            S(k)
                        k0 += 2
        with P.scope():
            wo = P.sb("lru_wo", [128, LKC, D], BF16, chunk=D)
            for c in range(0, LKC, 2):
                P.dma(wo[:, c:c + 2, :], K.wb["lru_out"][c:c + 2].rearrange("i p c -> p i c"), q="sp")
            for (a, b) in (BLK512 if need_ctx else BLK512[1:]):
                st = 4 if a < CTXN else s
                n = b - a
                for j in range(8):
                    bank = K.pb[6 + j % 2]
                    for c in range(LKC):
                        P.mm(bank[:, :n], wo[:, c, j * 128:(j + 1) * 128], gy[:, c, a:b], start=(c == 0), stop=(c == LKC - 1))
                    P.stt(K.xT[:, j, a:b], bank[:, :n], K.modT[:, l, 2, j, st:st + 1], K.xT[:, j, a:b], ALU.mult, ALU.add)


def _na_tiles(m):
    res = {}
    for ql in (0, 1):
        r = 2 * m + ql
        start = min(max(r - 4, 0), 24)
        for kt in range(16):
            v0 = start <= 2 * kt < start + 8
            v1 = start <= 2 * kt + 1 < start + 8
            if not (v0 or v1):
                continue
            if v0 and v1:
                sl = (2 * kt - r) + 7
                assert 0 <= sl <= 13
            elif v1:
                assert 2 * kt + 1 - r == -4
                sl = 14
            else:
                assert 2 * kt - r == 3
                sl = 15
            res.setdefault(kt, [None, None])[ql] = sl
    return sorted(res.items())


def na_mixer(K, l, s, need_ctx):
    P = K.P
    compute_rstd(K, BLK512)
    with P.scope():
        hT = P.sb("na_h", [128, 8, NT], BF16, chunk=128)
        W = P.sb("na_w", [128, 8, 768], BF16)
        Wo = P.sb("na_wo", [128, 2, D], BF16)
        tab = P.sb("na_tabs", [128, 4, 16, 64], BF16)
        qT = P.sb("na_q", [128, 2, NT], BF16, chunk=128)
        kT = P.sb("na_k", [128, 2, NT], BF16, chunk=128)
        V = P.sb("na_v", [128, 18, 4, 65], BF16, chunk=260)
        OT = P.sb("na_ot", [128, 2, NT], BF16, chunk=128)
        PT = [P.sb("na_pt%d" % i, [128, 7, 128], BF16, chunk=128) for i in range(2)]
        tmp = [P.sb("na_tmp%d" % i, [128, 5, 128], F32, chunk=64) for i in range(2)]
        Ot = [P.sb("na_o%d" % i, [128, 4, 64], BF16) for i in range(2)]
        rec = [P.sb("na_rec%d" % i, [128, 4], F32) for i in range(2)]
        for (a, b) in BLK512:
            make_h(K, hT[:, :, a:b], a, b, l, 0, 4 if a < CTXN else s)
        P.memset(V[:, :, :, 64:65], 1.0, eng="pool")
        nb = 0
        no = 0
        for G in range(4):
            P.dma(W[:].rearrange("p k c -> p (k c)"), K.wb["na_in"][G], q="sp")
            P.dma(Wo[:].rearrange("p k c -> p (k c)"), K.wb["na_out"][G], q="sp")
            P.dma(tab[:].rearrange("p h s c -> p h (s c)"), K.wb["na_tab"][4 * G:4 * G + 4].rearrange("h p c -> p h c"), q="sp")
            for (a, b) in BLK512:
                n = b - a
                for which, dst in ((0, qT), (1, kT)):
                    for k2 in range(2):
                        bank = K.pb[nb % 4]
                        nb += 1
                        for kc in range(8):
                            P.mm(bank[:, :n], W[:, kc, which * 256 + k2 * 128:which * 256 + (k2 + 1) * 128], hT[:, kc, a:b],
                                 start=(kc == 0), stop=(kc == 7))
                        if which == 0:
                            P.act(dst[:, k2, a:b], bank[:, :n], AF.Copy, scale=0.125)
                        else:
                            P.copy(dst[:, k2, a:b], bank[:, :n])
            for ti in range(18):
                bank = K.pb[nb % 4]
                nb += 1
                for kc in range(8):
                    P.mm(bank[:, :256], hT[:, kc, ti * 128:(ti + 1) * 128], W[:, kc, 512:768], start=(kc == 0), stop=(kc == 7))
                P.copy(V[:, ti, :, 0:64], bank[:, :256].rearrange("p (h d) -> p h d", h=4), eng=("act" if ti % 2 else "dve"))
            jobs = []
            if need_ctx:
                for qt in range(2):
                    jobs.append((qt, [(0, None), (1, None)]))
            for m in range(16):
                jobs.append((2 + m, [(2 + kt, sl) for kt, sl in _na_tiles(m)] + [(0, None), (1, None)]))
            steps = [(ji, hh) for ji in range(len(jobs)) for hh in range(4)]

            def scores(n):
                ji, hh = steps[n]
                qt, tiles = jobs[ji]
                q0 = qt * 128
                k2, p0 = hh // 2, (hh % 2) * 64
                sb = [K.pb[(n % 2) * 2], K.pb[(n % 2) * 2 + 1]]
                pt = PT[n % 2]
                tm = tmp[n % 2]
                nlat = sum(1 for _, sl in tiles if sl is not None)
                for i, (kt, sl) in enumerate(tiles):
                    P.mm(sb[i // 4][:, (i % 4) * 128:(i % 4 + 1) * 128], kT[p0:p0 + 64, k2, kt * 128:(kt + 1) * 128],
                         qT[p0:p0 + 64, k2, q0:q0 + 128])
                for i, (kt, sl) in enumerate(tiles):
                    if sl is None:
                        continue
                    if sl[0] is not None and sl[1] is not None and sl[0] < 14 and sl[1] == sl[0] - 1:
                        src = sb[i // 4][:, (i % 4) * 128:(i % 4 + 1) * 128].rearrange("p (a b) -> p a b", a=2)
                        P.tt(tm[:, i, :].rearrange("p (a b) -> p a b", a=2), src, tab[:, hh, sl[1]:sl[0] + 1, :][:, ::-1, :], ALU.add)
                        continue
                    for ql in range(2):
                        src = sb[i // 4][:, (i % 4) * 128 + ql * 64:(i % 4) * 128 + ql * 64 + 64]
                        if sl[ql] is None:
                            P.memset(tm[:, i, ql * 64:ql * 64 + 64], -30000.0, eng="pool")
                        else:
                            P.tt(tm[:, i, ql * 64:ql * 64 + 64], src, tab[:, hh, sl[ql], :], ALU.add)
                if nlat:
                    P.act(pt[:, 0:nlat, :], tm[:, 0:nlat, :], AF.Exp)
                ci = [i for i, (kt, sl) in enumerate(tiles) if sl is None]
                assert len(ci) == 2 and ci[1] == ci[0] + 1 and ci[0] // 4 == ci[1] // 4
                P.act(pt[:, ci[0]:ci[0] + 2, :], sb[ci[0] // 4][:, (ci[0] % 4) * 128:(ci[0] % 4 + 2) * 128].rearrange("p (a b) -> p a b", a=2), AF.Exp)

            def pv(n):
                ji, hh = steps[n]
                qt, tiles = jobs[ji]
                q0 = qt * 128
                no = ji
                ob = K.pb[4 + no % 2]
                pt = PT[n % 2]
                for i, (kt, sl) in enumerate(tiles):
                    P.mm(ob[:, hh * 65:hh * 65 + 65], pt[:, i, :], V[:, kt, hh, :], start=(i == 0), stop=(i == len(tiles) - 1))
                if hh < 3:
                    return
                o_t = Ot[no % 2]
                rc = rec[no % 2]
                obv = ob[:, 0:260].rearrange("p (h e) -> p h e", h=4)
                P.add("dve", lambda e, rc=rc, obv=obv: e.reciprocal(rc[:], obv[:, :, 64]), reads=(ob[:, 0:260],), writes=(rc[:],))
                P.tt(o_t[:], obv[:, :, 0:64], rc[:].unsqueeze(2).broadcast_to([128, 4, 64]), ALU.mult)
                tb = K.pb[6 + no % 2]
                tbv = tb[:].bitcast(BF16)
                for k2 in range(2):
                    P.tr(tbv[:, k2 * 128:(k2 + 1) * 128], o_t[:, 2 * k2:2 * k2 + 2, :].rearrange("p h d -> p (h d)"), K.ident[:])
                P.copy(OT[:, :, q0:q0 + 128], tbv[:, 0:256].rearrange("p (k t) -> p k t", k=2), eng="act")

            for n in range(len(steps) + 1):
                if n < len(steps):
                    scores(n)
                if n >= 1:
                    pv(n - 1)
            for (a, b) in (BLK512 if need_ctx else BLK512[1:]):
                st = 4 if a < CTXN else s
                n = b - a
                for j in range(8):
                    bank = K.pb[4 + j % 2]
                    for k2 in range(2):
                        P.mm(bank[:, :n], Wo[:, k2, j * 128:(j + 1) * 128], OT[:, k2, a:b], start=(k2 == 0), stop=(k2 == 1))
                    P.stt(K.xT[:, j, a:b], bank[:, :n], K.modT[:, l, 2, j, st:st + 1], K.xT[:, j, a:b], ALU.mult, ALU.add)


_SHARED = {}


def run(inputs, nseq, ncores, plan, final=True, trace=False):
    sh = host_shared(inputs)
    nc, K = build(nseq, plan, final)
    in_maps = []
    for c in range(ncores):
        m = host_core(inputs, sh["small"], c, nseq)
        for n in list(WSPEC) + ["rope", "fin"]:
            m[n] = sh[n]
        in_maps.append(m)
    res = run_bass_kernel_spmd(nc, in_maps, core_ids=list(range(ncores)), trace=trace)
    return res


FULL_PLAN = [(l, h) for l in range(NLAYER) for h in ("m", "f")]


def kernel(**inputs):
    inputs = {k: np.asarray(v) for k, v in inputs.items()}
    res = run(inputs, 4, 8, FULL_PLAN, final=True)
    return np.concatenate([r["y"] for r in res.results], axis=0).astype(np.float32)
```

```python
import numpy as np
import concourse.bass as bass
import concourse.mybir as mybir
from contextlib import ExitStack

F32 = mybir.dt.float32
BF16 = mybir.dt.bfloat16
AF = mybir.ActivationFunctionType
ALU = mybir.AluOpType
AX = mybir.AxisListType

ENGS = ("pe", "act", "dve", "pool", "sp")
NDMASEM = 24
SAME_ENGINE_SYNC = True


class Op:
    __slots__ = ("eng", "idx", "fn", "deps", "needs_inc", "incval", "dma", "dsem", "dval", "dprev", "waits")

    def __init__(self, eng, fn, dma):
        self.eng = eng
        self.fn = fn
        self.dma = dma
        self.deps = {}
        self.needs_inc = False
        self.incval = None
        self.waits = []


class Prog:
    def __init__(self, nc, es):
        self.nc = nc
        self.es = es
        self.ops = {e: [] for e in ENGS}
        self.trk = {}
        self.ndma = {e: 0 for e in ENGS}
        self.waited = {e: {} for e in ENGS}
        self.nops = 0
        self._cc = {}
        self.stack = [es]
        self.dmas = []

    def sb(self, name, shape, dtype, chunk=None):
        self.nsb = getattr(self, "nsb", 0) + 1
        name = "%s_%d" % (name, self.nsb)
        t = self.stack[-1].enter_context(self.nc.sbuf_tensor(name, list(shape), dtype))
        free = int(np.prod(shape[1:]))
        self.trk[name] = (free, chunk or free, {})
        return t

    def ps(self, name, shape, dtype=F32, chunk=None):
        t = self.es.enter_context(self.nc.psum_tensor(name, list(shape), dtype))
        free = int(np.prod(shape[1:]))
        self.trk[name] = (free, chunk or free, {})
        return t

    def dram(self, name, shape, dtype, kind="Internal", chunk=None):
        t = self.nc.dram_tensor(name, list(shape), dtype, kind=kind)
        n = int(np.prod(shape))
        self.trk[name] = (None, chunk or n, {})
        return t.ap()

    def _chunks(self, ap):
        name = ap.tensor.name
        tr = self.trk.get(name)
        if tr is None:
            return None, ()
        pstep, chunk, st = tr
        if chunk >= (1 << 29):
            return st, (0,)
        key = (name, ap.offset, ap.ap)
        r = self._cc.get(key)
        if r is not None:
            return st, r
        off = ap.offset
        dims = list(ap.ap)
        if pstep is not None:
            off = off % pstep
            dims = dims[1:]
        ivs = [(off, off)]
        for step, cnt in reversed(dims):
            if cnt <= 1:
                continue
            span = step * (cnt - 1)
            if abs(step) <= (ivs[0][1] - ivs[0][0] + 1) or len(ivs) * cnt > 256 or abs(step) < chunk // 2:
                ivs = [(lo + min(0, span), hi + max(0, span)) for lo, hi in ivs]
            else:
                ivs = [(lo + step * i, hi + step * i) for i in range(cnt) for lo, hi in ivs]
        cs = set()
        for lo, hi in ivs:
            cs.update(range(lo // chunk, hi // chunk + 1))
        r = tuple(cs)
        self._cc[key] = r
        return st, r

    def _dep(self, op, prod):
        if prod is None or prod is op:
            return
        if prod.dma:
            op.deps[("dma", id(prod))] = prod
            return
        if prod.eng == op.eng and not op.dma and (prod.eng == "pe" or not SAME_ENGINE_SYNC):
            return
        k = prod.eng
        cur = op.deps.get(k)
        if cur is None or cur.idx < prod.idx:
            op.deps[k] = prod

    def add(self, eng, fn, reads=(), writes=(), dma=False, extra=()):
        op = Op(eng, fn, dma)
        op.idx = len(self.ops[eng])
        self.nops += 1
        for p in extra:
            self._dep(op, p)
        for ap in reads:
            st, chs = self._chunks(ap)
            if st is None:
                continue
            for c in chs:
                s = st.get(c)
                if s is None:
                    s = st[c] = [None, {}]
                self._dep(op, s[0])
                s[1][("dma", id(op)) if dma else eng] = op
        for ap in writes:
            st, chs = self._chunks(ap)
            if st is None:
                continue
            for c in chs:
                s = st.get(c)
                if s is None:
                    s = st[c] = [None, {}]
                self._dep(op, s[0])
                for r in s[1].values():
                    self._dep(op, r)
                s[0] = op
                s[1] = {}
        w = self.waited[eng]
        for k, prod in op.deps.items():
            if prod.dma:
                key = ("dma", id(prod))
                if w.get(key):
                    continue
                w[key] = True
                op.waits.append(prod)
            else:
                if w.get(k, -1) >= prod.idx:
                    continue
                w[k] = prod.idx
                prod.needs_inc = True
                op.waits.append(prod)
        if dma:
            j = self.ndma[eng]
            self.ndma[eng] += 1
            op.dsem = j % NDMASEM
            op.dval = 16 * (j // NDMASEM + 1)
        self.ops[eng].append(op)
        if dma:
            self.dmas.append(op)
        return op

    def barrier(self):
        lasts = [self.ops[e][-1] for e in ENGS if self.ops[e]]
        dm = self.dmas
        self.dmas = []
        for e in ENGS:
            if self.ops[e]:
                self.add(e, None, extra=[p for p in lasts if p.eng != e] + dm)

    class _Scope:
        def __init__(self, P):
            self.P = P

        def __enter__(self):
            self.es = ExitStack()
            self.es.__enter__()
            self.P.stack.append(self.es)
            return self

        def __exit__(self, *a):
            self.P.barrier()
            self.P.stack.pop()
            return self.es.__exit__(*a)

    def scope(self):
        return Prog._Scope(self)

    def mm(self, out, lhsT, rhs, start=True, stop=True, **kw):
        return self.add("pe", lambda e: e.matmul(out, lhsT, rhs, start=start, stop=stop, **kw),
                        reads=(lhsT, rhs), writes=(out,))

    def tr(self, out, in_, ident):
        return self.add("pe", lambda e: e.transpose(out, in_, ident), reads=(in_, ident), writes=(out,))

    def act(self, out, in_, func, bias=None, scale=None, accum_out=None, eng="act"):
        kw = {}
        rd = [in_]
        if bias is not None:
            kw["bias"] = bias
            if not isinstance(bias, (int, float)):
                rd.append(bias)
        if scale is not None:
            kw["scale"] = scale
            if not isinstance(scale, (int, float)):
                rd.append(scale)
        wr = [out]
        if accum_out is not None:
            kw["accum_out"] = accum_out
            wr.append(accum_out)
        return self.add("act", lambda e: e.activation(out, in_, func, **kw), reads=rd, writes=wr)

    def tt(self, out, in0, in1, op, eng="dve"):
        return self.add(eng, lambda e: e.tensor_tensor(out, in0, in1, op), reads=(in0, in1), writes=(out,))

    def ts(self, out, in0, s1, s2, op0, op1=None, eng="dve", accum_out=None):
        rd = [in0] + [s for s in (s1, s2) if s is not None and not isinstance(s, (int, float))]
        wr = [out] + ([accum_out] if accum_out is not None else [])
        kw = {}
        if op1 is not None:
            kw["op1"] = op1
        if accum_out is not None:
            kw["accum_out"] = accum_out
        return self.add(eng, lambda e: e.tensor_scalar(out, in0, s1, s2, op0, **kw), reads=rd, writes=wr)

    def stt(self, out, in0, scalar, in1, op0, op1):
        rd = [in0, in1] + ([scalar] if not isinstance(scalar, (int, float)) else [])
        return self.add("dve", lambda e: e.scalar_tensor_tensor(out, in0, scalar, in1, op0, op1),
                        reads=rd, writes=(out,))

    def copy(self, out, in_, eng="dve"):
        if eng == "act":
            return self.add("act", lambda e: e.copy(out, in_), reads=(in_,), writes=(out,))
        return self.add(eng, lambda e: e.tensor_copy(out, in_), reads=(in_,), writes=(out,))

    def memset(self, ap, val, eng="dve"):
        return self.add(eng, lambda e: e.memset(ap, val), writes=(ap,))

    def dma(self, out, in_, q="sp", **kw):
        return self.add(q, lambda e: e.dma_start(out=out, in_=in_, **kw), reads=(in_,), writes=(out,), dma=True)

    def emit(self):
        nc = self.nc
        es = self.es
        sem = {e: es.enter_context(nc.semaphore("s_" + e)) for e in ENGS}
        dsem = {e: [es.enter_context(nc.semaphore("d_%s%d" % (e, i))) for i in range(NDMASEM)]
                for e in ENGS if self.ndma[e]}
        for e in ENGS:
            c = 0
            for op in self.ops[e]:
                if op.needs_inc:
                    c += 1
                    op.incval = c
        self.maxinc = {e: max([op.incval or 0 for op in self.ops[e]] + [0]) for e in ENGS}
        block = es.enter_context(nc.Block())
        hooks = {"pe": block.tensor, "act": block.scalar, "dve": block.vector, "pool": block.gpsimd,
                 "sp": block.sync}
        for e in ENGS:
            ops = self.ops[e]
            nd = self.ndma[e]

            def body(eng, e=e, ops=ops, nd=nd):
                for op in ops:
                    for p in op.waits:
                        if p.dma:
                            eng.wait_ge(dsem[p.eng][p.dsem], p.dval)
                        else:
                            eng.wait_ge(sem[p.eng], p.incval)
                    if op.dma and op.dval > 16:
                        eng.wait_ge(dsem[e][op.dsem], op.dval - 16)
                    if op.fn is None:
                        if op.needs_inc:
                            eng.nop().then_inc(sem[e], 1)
                        continue
                    ins = op.fn(eng)
                    if op.dma:
                        ins.then_inc(dsem[e][op.dsem], 16)
                    elif op.needs_inc:
                        ins.then_inc(sem[e], 1)
                if nd:
                    for i in range(min(nd, NDMASEM)):
                        last = ((nd - 1 - i) // NDMASEM) + 1
                        eng.wait_ge(dsem[e][i], 16 * last)
            if ops:
                hooks[e](body)

from concourse.bass_utils import run_bass_kernel_spmd

D = 1024
KC = 8
CTXN = 256
LATN = 2048
NT = CTXN + LATN
NLAYER = 4
FF = 2816
NPAIR = 22
EPS = 1e-6
LW = 1280
LKC = 10
LAT_SB = [0, 410, 820, 1230, 1639, 2048]
SBLOCKS = [(0, 256, 0, 256)] + [(256 + LAT_SB[i], 256 + LAT_SB[i + 1], 256, NT) for i in range(5)]
BLK512 = [(0, 256)] + [(256 + 512 * i, 256 + 512 * (i + 1)) for i in range(4)]

SM = {}
_o = 0
for _n, _w in [("cT", 40), ("ada_b", 4 * 48), ("gains", 4 * 2 * 8), ("ffn_cw", 4 * 44 * 4), ("ret_gn", 2 * 4 * 4),
               ("ret_decay", 2 * 2 * 4), ("lru_cw", 10 * 5), ("lru_gb", 2 * 2 * 10), ("lru_lam", 2 * 10),
               ]:
    SM[_n] = (_o, _o + _w)
    _o += _w
NSMALL = _o

WSPEC = {
    "ada_w": (4 * 48, 1024, False),
    "ffn_up": (4 * 22, 2048, True),
    "ffn_down": (4 * 22, 1024, True),
    "ret_in": (2 * 4, 8 * 1536, True),
    "ret_out": (2 * 4, 4 * 1024, True),
    "lru_in": (10, 2048, True),
    "lru_gate": (2 * 10, 768, True),
    "lru_out": (10, 1024, True),
    "na_in": (4, 8 * 768, True),
    "na_out": (4, 2 * 1024, True),
    "na_tab": (16, 16 * 64, True),
}


def _lru_slots(c):
    b0 = (128 * c) // 160
    b1 = (128 * c + 127) // 160
    k0 = (160 * b0) // 128
    k1 = (160 * (b1 + 1) - 1) // 128
    return list(range(k0, k1 + 1))


def host_shared(inp):
    f = np.float32
    out = {}
    aw = inp["ada_w"].reshape(4, 8, 128, 48, 128).transpose(0, 3, 2, 1, 4)
    out["ada_w"] = np.ascontiguousarray(aw).reshape(4 * 48, 128, 1024)
    wu = inp["ffn_w_up"].reshape(4, 8, 128, 2, 22, 128).transpose(0, 4, 2, 1, 3, 5)
    out["ffn_up"] = np.ascontiguousarray(wu).reshape(4 * 22, 128, 2048)
    out["ffn_down"] = np.ascontiguousarray(inp["ffn_w_down"].reshape(4 * 22, 128, 1024))
    ri = inp["ret_w_in"].reshape(2, 8, 128, 6144)
    tiles = []
    for j in range(2):
        for h in range(4):
            cols = np.concatenate([np.arange(h * 256, h * 256 + 256), 1024 + np.arange(h * 256, h * 256 + 256),
                                   2048 + np.arange(h * 512, h * 512 + 512), 4096 + np.arange(h * 512, h * 512 + 512)])
            tiles.append(ri[j][:, :, cols].transpose(1, 0, 2).reshape(128, 8 * 1536))
    out["ret_in"] = np.ascontiguousarray(np.stack(tiles))
    ro = inp["ret_w_out"].reshape(2, 4, 4, 128, 1024).transpose(0, 1, 3, 2, 4)
    out["ret_out"] = np.ascontiguousarray(ro).reshape(8, 128, 4096)
    li = inp["lru_w_in"][0].reshape(8, 128, 2, 10, 128).transpose(3, 1, 0, 2, 4)
    out["lru_in"] = np.ascontiguousarray(li).reshape(10, 128, 2048)
    gw = inp["lru_gate_w"][0]
    lg = np.zeros((2, 10, 128, 2, 3, 128), f)
    for d in range(2):
        for g in range(2):
            dense = np.zeros((LW, LW), f)
            for k in range(8):
                dense[k * 160:(k + 1) * 160, k * 160:(k + 1) * 160] = gw[d, g, k]
            for c in range(10):
                for si, kc in enumerate(_lru_slots(c)):
                    lg[d, c, :, g, si, :] = dense[kc * 128:(kc + 1) * 128, c * 128:(c + 1) * 128]
    out["lru_gate"] = lg.reshape(20, 128, 768)
    out["lru_out"] = np.ascontiguousarray(inp["lru_w_out"][0].reshape(10, 128, 1024))
    nq = inp["na_w_qkv"][0].reshape(8, 128, 3, 4, 256).transpose(3, 1, 0, 2, 4)
    out["na_in"] = np.ascontiguousarray(nq).reshape(4, 128, 8 * 768)
    no = inp["na_w_out"][0].reshape(4, 2, 128, 1024).transpose(0, 2, 1, 3)
    out["na_out"] = np.ascontiguousarray(no).reshape(4, 128, 2048)
    rpb = inp["na_rpb"][0]
    NEG = f(-30000.0)
    kc_i = np.arange(64)[:, None]
    qc_i = np.arange(64)[None, :]
    cs = np.clip(qc_i - 8, 0, 48)
    col_in = (kc_i >= cs) & (kc_i < cs + 16)
    dc = np.clip(kc_i - qc_i, -15, 15) + 15
    def blk(h, dr):
        if dr is None or dr < -7 or dr > 7:
            return np.full((64, 64), NEG, f)
        return np.where(col_in, rpb[h, dr + 7][dc], NEG).astype(f)
    tab = np.zeros((128, 16, 16, 64), f)
    for h in range(16):
        for sl in range(16):
            if sl < 14:
                d0, d1 = sl - 7, sl - 6
            elif sl == 14:
                d0, d1 = None, -4
            else:
                d0, d1 = 3, None
            tab[0:64, h, sl] = blk(h, d0)
            tab[64:128, h, sl] = blk(h, d1)
    out["na_tab"] = np.ascontiguousarray(tab.transpose(1, 0, 2, 3)).reshape(16, 128, 1024)
    sm = np.zeros((128, NSMALL), f)
    def put(name, arr):
        a, b = SM[name]
        sm[:, a:b] = arr.reshape(128, b - a)
    put("ada_b", inp["ada_b"].reshape(4, 48, 128).transpose(2, 0, 1))
    g = np.stack([inp["norm_mix"], inp["norm_ffn"]], axis=1).reshape(4, 2, 8, 128).transpose(3, 0, 1, 2)
    put("gains", g)
    cw = np.concatenate([inp["ffn_conv_w"], inp["ffn_conv_b"][:, None, :]], axis=1)
    put("ffn_cw", cw.reshape(4, 4, 44, 128).transpose(3, 0, 2, 1))
    put("ret_gn", inp["ret_gn"].reshape(2, 4, 4, 128).transpose(3, 0, 1, 2))
    put("ret_decay", np.broadcast_to(inp["ret_logit_decay"].reshape(1, 16), (128, 16)))
    lcw = np.concatenate([inp["lru_conv_w"][0], inp["lru_conv_b"][0][None]], axis=0)
    put("lru_cw", lcw.reshape(5, 10, 128).transpose(2, 1, 0))
    put("lru_gb", inp["lru_gate_b"][0].reshape(2, 2, 10, 128).transpose(3, 0, 1, 2))
    put("lru_lam", inp["lru_lambda"][0].reshape(2, 10, 128).transpose(2, 0, 1))
    half = 128
    freqs = (10000.0 ** (-np.arange(0, half, 2, dtype=np.float32) / half)).astype(f)
    p = np.arange(128)
    rc = np.zeros((128, 16, 2, 64), f)
    rs = np.zeros((128, 16, 2, 64), f)
    for m in range(16):
        row = (2 * m + p // 64).astype(f)
        col = (p % 64).astype(f)
        ar = row[:, None] * freqs[None, :]
        ac = col[:, None] * freqs[None, :]
        rc[:, m, 0] = np.cos(ar); rc[:, m, 1] = np.cos(ac)
        rs[:, m, 0] = np.sin(ar); rs[:, m, 1] = np.sin(ac)
    out["rope"] = np.concatenate([rc.reshape(128, 2048), rs.reshape(128, 2048),
                                  np.broadcast_to(np.arange(128, dtype=f)[None, :], (128, 128)),
                                  np.broadcast_to(np.arange(128, dtype=f)[:, None], (128, 8))], axis=1)
    out["fin"] = np.ascontiguousarray(np.broadcast_to(inp["final_norm"][None, :], (128, 1024))).astype(f)
    out["small"] = sm
    return out


def host_core(inp, shared_small, core, nseq):
    sm = shared_small.copy()
    c = inp["c"][core * nseq:(core + 1) * nseq]
    cT = np.zeros((128, 8, 5), np.float32)
    cT[:, :, :nseq] = c.reshape(nseq, 8, 128).transpose(2, 1, 0)
    cT[:, :, 4] = inp["c_ctx"].reshape(8, 128).T
    a, b = SM["cT"]
    sm[:, a:b] = cT.reshape(128, 40)
    return {"x": np.ascontiguousarray(inp["x"][core * nseq:(core + 1) * nseq]),
            "ctx": np.ascontiguousarray(inp["ctx"][core * nseq:(core + 1) * nseq]),
            "small": sm}


class Ctx:
    pass


def build(nseq, plan, final=True):
    nc = bass.Bass("TRN2", target_bir_lowering=False)
    es = ExitStack()
    P = Prog(nc, es)
    K = Ctx()
    K.P, K.nc, K.nseq = P, nc, nseq
    dr = {}
    dr["x"] = nc.dram_tensor("x", [nseq, LATN, D], F32, kind="ExternalInput").ap()
    dr["ctx"] = nc.dram_tensor("ctx", [nseq, CTXN, D], F32, kind="ExternalInput").ap()
    dr["small"] = nc.dram_tensor("small", [128, NSMALL], F32, kind="ExternalInput").ap()
    dr["rope"] = nc.dram_tensor("rope", [128, 4232], F32, kind="ExternalInput").ap()
    dr["fin"] = nc.dram_tensor("fin", [128, 1024], F32, kind="ExternalInput").ap()
    used = set(["ada_w", "ffn_up", "ffn_down"])
    kinds = {0: "ret", 1: "lru", 2: "na", 3: "ret"}
    for (l, hf) in plan:
        if hf == "m":
            used.update({"ret": ["ret_in", "ret_out"], "lru": ["lru_in", "lru_gate", "lru_out"],
                         "na": ["na_in", "na_out", "na_tab"]}[kinds[l]])
    K.used = used
    for n, (nt, te, cast) in WSPEC.items():
        dr[n] = nc.dram_tensor(n, [nt, 128, te], F32, kind="ExternalInput").ap()
        P.trk[n] = (None, 128 * te, {})
    K.y = nc.dram_tensor("y", [nseq, LATN, D], F32, kind="ExternalOutput").ap()
    P.trk["y"] = (None, 128 * D, {})
    if not final:
        K.yc = nc.dram_tensor("yc", [nseq, CTXN, D], F32, kind="ExternalOutput").ap()
        P.trk["yc"] = (None, 128 * D, {})
    K.dr = dr
    K.wb = {}
    for n, (nt, te, cast) in WSPEC.items():
        if cast and n in used:
            K.wb[n] = P.dram("wb_" + n, [nt, 128, te], BF16, chunk=128 * te)
    layers_needed = sorted(set(l for l, _ in plan))
    order = []
    for l in layers_needed:
        kd = kinds[l]
        j = l // 3
        if (l, "m") in plan:
            if kd == "ret":
                order += [("ret_in", j * 4, j * 4 + 4), ("ret_out", j * 4, j * 4 + 4)]
            elif kd == "lru":
                order += [("lru_in", 0, 10), ("lru_gate", 0, 20), ("lru_out", 0, 10)]
            else:
                order += [("na_in", 0, 4), ("na_out", 0, 4), ("na_tab", 0, 16)]
        if (l, "f") in plan:
            order += [("ffn_up", l * 22, l * 22 + 22), ("ffn_down", l * 22, l * 22 + 22)]
    for n, a, b in order:
        step = max(1, (1 << 20) // (128 * WSPEC[n][1] * 4) * 2)
        for i in range(a, b, step):
            e = min(b, i + step)
            P.dma(K.wb[n][i:e], dr[n][i:e], q="pool")

    K.pb = [P.ps("pb%d" % i, [128, 512], F32, chunk=1 << 30) for i in range(8)]
    K.small = P.sb("small_sb", [128, NSMALL], F32, chunk=64)
    K.xT = P.sb("xT", [128, KC, NT], F32, chunk=128)
    K.rstd = P.sb("rstd", [128, NT], F32, chunk=128)
    K.identf = P.sb("identf", [128, 128], F32)
    K.ident = P.sb("ident", [128, 128], BF16)
    K.onesb = P.sb("onesb", [128, 128], BF16)
    K.modT = P.sb("modT", [128, NLAYER, 6, 8, 5], F32, chunk=40)
    K.A = P.sb("Amod", [128, NLAYER, 2, 8, 5], F32, chunk=40)
    K.tmpf = [P.sb("tmpf%d" % i, [128, 512], F32) for i in range(2)]

    def sm(name):
        a, b = SM[name]
        return K.small[:, a:b]
    K.sm = sm
    P.dma(K.small[:], dr["small"], q="sp")
    P.memset(K.identf[:], 0.0)
    P.add("pool", lambda e: e.affine_select(K.identf[:], K.identf[:], [[-1, 128]], ALU.not_equal, 1.0, base=0,
                                            channel_multiplier=1), reads=(K.identf[:],), writes=(K.identf[:],))
    P.copy(K.ident[:], K.identf[:])
    P.memset(K.onesb[:], 1.0)

    prologue(K, layers_needed)
    for s in range(nseq):
        load_seq(K, s)
        for (l, hf) in plan:
            need_ctx = l < NLAYER - 1
            if hf == "m":
                kd = kinds[l]
                if kd == "ret":
                    ret_mixer(K, l, s, need_ctx)
                elif kd == "lru":
                    lru_mixer(K, l, s, need_ctx)
                else:
                    na_mixer(K, l, s, need_ctx)
            else:
                ffn(K, l, s, need_ctx)
        store_seq(K, s, final)
    P.emit()
    es.close()
    return nc, K


def prologue(K, layers):
    P = K.P
    with P.scope():
        sc = P.sb("silu_c", [128, 8, 5], F32)
        wt = [P.sb("adaw%d" % i, [128, 8, 128], F32) for i in range(3)]
        a, b = SM["cT"]
        P.act(sc[:].rearrange("p k s -> p (k s)"), K.small[:, a:b], AF.Silu)
        n = 0
        for l in layers:
            bank = K.pb[l % 2]
            for oc in range(48):
                w = wt[n % 3]
                n += 1
                P.dma(w[:].rearrange("p k c -> p (k c)"), K.dr["ada_w"][l * 48 + oc], q="sp")
                for kc in range(8):
                    P.mm(bank[:, oc * 5:oc * 5 + 5], w[:, kc, :], sc[:, kc, :], start=(kc == 0), stop=(kc == 7))
            a, b = SM["ada_b"]
            ab = K.small[:, a + l * 48:a + l * 48 + 48].unsqueeze(2).broadcast_to([128, 48, 5])
            P.tt(K.modT[:, l].rearrange("p m k s -> p (m k) s"), bank[:, 0:240].rearrange("p (o s) -> p o s", s=5), ab, ALU.add)
            a, b = SM["gains"]
            for w_ in range(2):
                g = K.small[:, a + (l * 2 + w_) * 8:a + (l * 2 + w_) * 8 + 8].unsqueeze(2).broadcast_to([128, 8, 5])
                P.ts(K.A[:, l, w_], K.modT[:, l, 1 + 3 * w_], 1.0, None, ALU.add)
                P.tt(K.A[:, l, w_], K.A[:, l, w_], g, ALU.mult)


def load_seq(K, s):
    P = K.P
    with P.scope():
        st = [P.sb("ldst%d" % i, [128, D], F32) for i in range(2)]
        for ti in range(18):
            src = K.dr["ctx"][s, ti * 128:(ti + 1) * 128, :] if ti < 2 else K.dr["x"][s, (ti - 2) * 128:(ti - 1) * 128, :]
            b = st[ti % 2]
            P.dma(b[:], src, q="sp")
            for hf in range(2):
                bank = K.pb[(ti * 2 + hf) % 4]
                for k in range(4):
                    P.tr(bank[:, k * 128:(k + 1) * 128], b[:, (hf * 4 + k) * 128:(hf * 4 + k + 1) * 128], K.identf[:])
                P.copy(K.xT[:, hf * 4:hf * 4 + 4, ti * 128:(ti + 1) * 128], bank[:].rearrange("p (k t) -> p k t", k=4),
                       eng=("act" if hf else "dve"))


def store_seq(K, s, final):
    P = K.P
    with P.scope():
        st = [P.sb("stst%d" % i, [128, D], F32) for i in range(2)]
        junk = P.sb("stjunk", [128, D], F32)
        ss = P.sb("stss", [128, 4], F32, chunk=1)
        fing = P.sb("fing", [128, D], F32)
        if final:
            P.dma(fing[:], K.dr["fin"], q="sp")
        tiles = range(2, 18) if final else range(18)
        for n, ti in enumerate(tiles):
            b = st[n % 2]
            banks = [K.pb[(n % 2) * 2], K.pb[(n % 2) * 2 + 1]]
            for kc in range(8):
                P.tr(banks[kc // 4][:, (kc % 4) * 128:(kc % 4 + 1) * 128], K.xT[:, kc, ti * 128:(ti + 1) * 128], K.identf[:])
            if final:
                q = ss[:, (n % 2) * 2:(n % 2) * 2 + 2]
                for hf in range(2):
                    P.act(junk[:, hf * 512:(hf + 1) * 512], banks[hf][:], AF.Square, accum_out=q[:, hf:hf + 1])
                P.tt(q[:, 0:1], q[:, 0:1], q[:, 1:2], ALU.add)
                P.act(q[:, 0:1], q[:, 0:1], AF.Sqrt, scale=1.0 / D, bias=EPS)
                P.add("dve", lambda e, q=q: e.reciprocal(q[:, 0:1], q[:, 0:1]), reads=(q[:, 0:1],), writes=(q[:, 0:1],))
                for hf in range(2):
                    P.stt(b[:, hf * 512:(hf + 1) * 512], banks[hf][:], q[:, 0:1], fing[:, hf * 512:(hf + 1) * 512],
                          ALU.mult, ALU.mult)
            else:
                for hf in range(2):
                    P.copy(b[:, hf * 512:(hf + 1) * 512], banks[hf][:], eng=("act" if hf else "dve"))
            if ti < 2:
                dst = K.yc[s, ti * 128:(ti + 1) * 128, :]
            else:
                dst = K.y[s, (ti - 2) * 128:(ti - 1) * 128, :]
            P.dma(dst, b[:], q="sp")


def compute_rstd(K, cols):
    P = K.P
    for bi, (a, b) in enumerate(cols):
        n = b - a
        bank = K.pb[6 + bi % 2]
        for kc in range(8):
            t = K.tmpf[kc % 2][:].bitcast(BF16)[:, 0:512]
            if kc % 2 == 0:
                P.act(t[:, :n], K.xT[:, kc, a:b], AF.Square)
            else:
                P.tt(t[:, :n], K.xT[:, kc, a:b], K.xT[:, kc, a:b], ALU.mult)
            P.mm(bank[:, :n], K.onesb[:], t[:, :n], start=(kc == 0), stop=(kc == 7))
        P.act(K.rstd[:, a:b], bank[:, :n], AF.Sqrt, scale=1.0 / D, bias=EPS)
        P.add("dve", lambda e, a=a, b=b: e.reciprocal(K.rstd[:, a:b], K.rstd[:, a:b]),
              reads=(K.rstd[:, a:b],), writes=(K.rstd[:, a:b],))


def make_h(K, dst, a, b, l, w, s, off=0):
    P = K.P
    n = b - a
    for kc in range(8):
        t = K.tmpf[kc % 2]
        P.stt(t[:, :n], K.xT[:, kc, a:b], K.A[:, l, w, kc, s:s + 1], K.rstd[:, a:b], ALU.mult, ALU.mult)
        P.act(dst[:, kc, off:off + n], t[:, :n], AF.Identity, bias=K.modT[:, l, 3 * w, kc, s:s + 1])


def ffn(K, l, s, need_ctx):
    P = K.P
    blocks = SBLOCKS if need_ctx else SBLOCKS[1:]
    compute_rstd(K, BLK512 if need_ctx else BLK512[1:])
    a0, _ = SM["ffn_cw"]
    with P.scope():
        wd = P.sb("ffn_wd", [128, NPAIR, D], BF16, chunk=D)
        wu = [P.sb("ffn_wu%d" % i, [128, 8, 256], BF16) for i in range(3)]
        hbs = [P.sb("ffn_h%d" % i, [128, 8, 512], BF16, chunk=512) for i in range(2)]
        aT = P.sb("ffn_a", [128, NPAIR, 512], BF16, chunk=512)
        hs = P.sb("ffn_hs", [128, 8, 1], BF16)
        tg = [P.sb("ffn_tg%d" % i, [128, 512], F32) for i in range(2)]
        tv = [P.sb("ffn_tv%d" % i, [128, 512], F32) for i in range(2)]
        for i in range(0, NPAIR, 2):
            P.dma(wd[:, i:i + 2, :], K.wb["ffn_down"][l * 22 + i:l * 22 + i + 2].rearrange("i p c -> p i c"), q="sp")
        nb = 0

        def geom(blk):
            c0, c1, s0, s1 = blk
            return c0, c1, max(c0 - 1, s0), min(c1 + 1, s1), (4 if c0 < CTXN else s)

        def build_h(k):
            c0, c1, lo, hi, st = geom(blocks[k])
            hb = hbs[k % 2]
            if lo < c0:
                make_h(K, hb, c0, hi, l, 1, st, off=1)
                P.copy(hb[:, :, 0:1], hs[:], eng="pool")
            else:
                make_h(K, hb, lo, hi, l, 1, st)
            P.copy(hs[:], hb[:, :, c1 - 1 - lo:c1 - lo], eng="pool")

        build_h(0)
        for k, blk in enumerate(blocks):
            c0, c1, lo, hi, st = geom(blk)
            hb = hbs[k % 2]
            n, W = hi - lo, c1 - c0
            for i in range(NPAIR):
                if i == 14 and k + 1 < len(blocks):
                    build_h(k + 1)
                w = wu[i % 3]
                P.dma(w[:].rearrange("p k c -> p (k c)"), K.wb["ffn_up"][l * 22 + i], q="sp")
                outs = []
                for hf in range(2):
                    bank = K.pb[nb % 6]
                    nb += 1
                    for kc in range(8):
                        P.mm(bank[:, :n], w[:, kc, hf * 128:(hf + 1) * 128], hb[:, kc, :n], start=(kc == 0), stop=(kc == 7))
                    t = (tg if hf == 0 else tv)[i % 2]
                    ch = i + 22 * hf
                    cw = K.small[:, a0 + (l * 44 + ch) * 4:a0 + (l * 44 + ch) * 4 + 4]
                    o = c0 - lo
                    P.act(t[:, :W], bank[:, o:o + W], AF.Identity, scale=cw[:, 1:2], bias=cw[:, 3:4])
                    if o == 1:
                        P.stt(t[:, :W], bank[:, 0:W], cw[:, 0:1], t[:, :W], ALU.mult, ALU.add)
                    else:
                        P.stt(t[:, 1:W], bank[:, 0:W - 1], cw[:, 0:1], t[:, 1:W], ALU.mult, ALU.add)
                    if hi == c1 + 1:
                        P.stt(t[:, :W], bank[:, o + 1:o + 1 + W], cw[:, 2:3], t[:, :W], ALU.mult, ALU.add)
                    else:
                        P.stt(t[:, :W - 1], bank[:, o + 1:o + W], cw[:, 2:3], t[:, :W - 1], ALU.mult, ALU.add)
                    outs.append(t)
                P.act(outs[0][:, :W], outs[0][:, :W], AF.Silu)
                P.tt(aT[:, i, :W], outs[0][:, :W], outs[1][:, :W], ALU.mult)
            for j in range(8):
                bank = K.pb[6 + j % 2]
                for i in range(NPAIR):
                    P.mm(bank[:, :W], wd[:, i, j * 128:(j + 1) * 128], aT[:, i, :W], start=(i == 0), stop=(i == NPAIR - 1))
                P.stt(K.xT[:, j, c0:c1], bank[:, :W], K.modT[:, l, 5, j, st:st + 1], K.xT[:, j, c0:c1], ALU.mult, ALU.add)


def ret_mixer(K, l, s, need_ctx):
    P = K.P
    j = l // 3
    compute_rstd(K, BLK512)
    a_dec, _ = SM["ret_decay"]
    a_gn, _ = SM["ret_gn"]
    if not hasattr(K, "sstate"):
        K.sstate = P.dram("sb_state", [18, 128, 1024], BF16, chunk=128 * 1024)
        K.kvc = P.dram("kv_cache", [18, 128, 768], BF16, chunk=128 * 768)
    with P.scope():
        hT = P.sb("rt_h", [128, 8, NT], BF16, chunk=128)
        for (a, b) in BLK512:
            make_h(K, hT[:, :, a:b], a, b, l, 0, 4 if a < CTXN else s)
        rc = P.sb("rt_rc", [128, 16, 2, 64], BF16)
        rs = P.sb("rt_rs", [128, 16, 2, 64], BF16)
        io = P.sb("rt_io", [128, 136], F32)
        P.dma(rc[:].rearrange("p m h f -> p (m h f)"), K.dr["rope"][:, 0:2048], q="pool")
        P.dma(rs[:].rearrange("p m h f -> p (m h f)"), K.dr["rope"][:, 2048:4096], q="pool")
        P.dma(io[:], K.dr["rope"][:, 4096:4232], q="sp")
        io_row, io_p = io[:, 0:128], io[:, 128:129]
        lg = P.sb("rt_lg", [128, 8], F32)
        ct = P.sb("rt_ct", [128, 8], F32)
        nlg = P.sb("rt_nlg", [128, 8], F32)
        lg127 = P.sb("rt_lg127", [128, 8], F32)
        lg128 = P.sb("rt_lg128", [128, 8], F32)
        gch = P.sb("rt_gch", [128, 8], F32)
        zeta = P.sb("rt_zeta", [128, 8], F32)
        logit = K.small[:, a_dec + j * 8:a_dec + j * 8 + 8]
        P.act(ct[:], logit, AF.Abs)
        P.act(ct[:], ct[:], AF.Exp, scale=-1.0)
        P.act(ct[:], ct[:], AF.Ln, bias=1.0)
        P.ts(nlg[:], logit, -1.0, 0.0, ALU.mult, ALU.max)
        P.tt(nlg[:], nlg[:], ct[:], ALU.add)
        P.ts(lg[:], nlg[:], -1.0, None, ALU.mult)
        P.ts(lg127[:], lg[:], 127.0, None, ALU.mult)
        P.ts(lg128[:], lg[:], 128.0, None, ALU.mult)
        P.act(gch[:], lg128[:], AF.Exp)
        MT = P.sb("rt_MT", [128, 4, 128], BF16)
        XI = P.sb("rt_XI", [128, 8, 128], BF16)
        with P.scope():
            Dm = P.sb("rt_D", [128, 128], F32)
            Dp = P.sb("rt_Dp", [128, 128], F32)
            Dn = P.sb("rt_Dn", [128, 128], F32)
            ge0 = P.sb("rt_ge0", [128, 128], F32)
            le0 = P.sb("rt_le0", [128, 128], F32)
            e1 = P.sb("rt_e1", [128, 128], F32)
            e2 = P.sb("rt_e2", [128, 128], F32)
            P.ts(Dm[:], io_row, io_p, None, ALU.subtract)
            P.ts(Dp[:], Dm[:], 0.0, None, ALU.max)
            P.ts(Dn[:], Dm[:], -1.0, 0.0, ALU.mult, ALU.max)
            P.ts(ge0[:], Dm[:], 0.0, None, ALU.is_ge)
            P.ts(le0[:], Dm[:], 0.0, None, ALU.is_le)
            for h in range(4):
                P.act(e1[:], Dp[:], AF.Exp, scale=lg[:, h:h + 1])
                P.tt(e1[:], e1[:], ge0[:], ALU.mult)
                P.act(e2[:], Dn[:], AF.Exp, scale=lg[:, 4 + h:5 + h])
                P.tt(e2[:], e2[:], le0[:], ALU.mult)
                P.tt(MT[:, h, :], e1[:], e2[:], ALU.add)
                P.act(XI[:, h, :], io_row, AF.Exp, scale=lg[:, h:h + 1], bias=lg[:, h:h + 1])
                P.act(XI[:, 4 + h, :], io_row, AF.Exp, scale=nlg[:, 4 + h:5 + h], bias=lg128[:, 4 + h:5 + h])
                P.act(zeta[:, h:h + 1], io_p, AF.Exp, scale=nlg[:, h:h + 1], bias=lg127[:, h:h + 1])
                P.act(zeta[:, 4 + h:5 + h], io_p, AF.Exp, scale=lg[:, 4 + h:5 + h])
        P.ts(zeta[:], zeta[:], 1.0 / 16.0, None, ALU.mult)

        mhalf = P.sb("rt_mhalf", [128, 1], F32)
        P.memset(mhalf[:], -0.5)
        W = P.sb("rt_w", [128, 8, 1536], BF16)
        Wo = P.sb("rt_wo", [128, 4, D], BF16, chunk=D)
        Sm = P.sb("rt_S", [128, 2, 512], F32)
        Sb = [P.sb("rt_Sb%d" % i, [128, 2, 512], BF16) for i in range(2)]
        Sl = [P.sb("rt_Sl%d" % i, [128, 2, 512], BF16) for i in range(2)]
        tu = K.tmpf[0][:, 0:256].rearrange("p (h r f) -> p h r f", h=2, r=2)
        tv_ = K.tmpf[0][:, 256:512].rearrange("p (h r f) -> p h r f", h=2, r=2)
        kvb = [P.sb("rt_kv%d" % i, [128, 768], BF16, chunk=64) for i in range(2)]
        krot = [kvb[i][:, 0:256] for i in range(2)]
        qrot = [P.sb("rt_qr%d" % i, [128, 256], BF16) for i in range(2)]
        kz = [P.sb("rt_kz%d" % i, [128, 256], BF16) for i in range(2)]
        vb = [kvb[i][:, 256:768] for i in range(2)]
        sg = [P.sb("rt_sg%d" % i, [128, 512], BF16) for i in range(2)]
        qT3 = [P.sb("rt_qT%d" % i, [128, 3, 2, 128], BF16) for i in range(2)]
        kT = [P.sb("rt_kT%d" % i, [128, 2, 128], BF16) for i in range(2)]
        inn = [P.sb("rt_in%d" % i, [128, 128], BF16) for i in range(2)]
        yn = [P.sb("rt_yn0", [128, 512], BF16)] * 2
        zz = [P.sb("rt_z0", [128, 512], BF16)] * 2
        zT = [P.sb("rt_zT0", [128, 4, 512], BF16)] * 2
        st6 = [P.sb("rt_st%d" % i, [128, 6], F32) for i in range(2)]
        mv = [P.sb("rt_mv%d" % i, [128, 4], F32) for i in range(2)]
        pb = K.pb
        tb3 = pb[3][:].bitcast(BF16)
        tb7 = pb[7][:].bitcast(BF16)

        def rope(src, dst, m):
            x4 = src.rearrange("p (h r f) -> p h r f", h=2, r=2)
            d4 = dst.rearrange("p (h r f) -> p h r f", h=2, r=2)
            C = rc[:, m].unsqueeze(2).broadcast_to([128, 2, 2, 64])
            S_ = rs[:, m].unsqueeze(2).broadcast_to([128, 2, 2, 64])
            P.tt(tu, x4, C, ALU.mult)
            P.tt(tv_, x4[:, :, ::-1, :], S_, ALU.mult)
            P.tt(d4[:, :, 0, :], tu[:, :, 0, :], tv_[:, :, 0, :], ALU.subtract)
            P.tt(d4[:, :, 1, :], tu[:, :, 1, :], tv_[:, :, 1, :], ALU.add)

        def kv_proj(t, q, with_q=False):
            cols = slice(t * 128, (t + 1) * 128)
            if with_q:
                for kc in range(8):
                    P.mm(pb[0][:], hT[:, kc, cols], W[:, kc, 0:512], start=(kc == 0), stop=(kc == 7))
            else:
                for kc in range(8):
                    P.mm(pb[0][:, 256:512], hT[:, kc, cols], W[:, kc, 256:512], start=(kc == 0), stop=(kc == 7))
            for kc in range(8):
                P.mm(pb[1][:], hT[:, kc, cols], W[:, kc, 512:1024], start=(kc == 0), stop=(kc == 7))
            if t >= 2:
                rope(pb[0][:, 256:512], krot[q][:], t - 2)
            else:
                P.copy(krot[q][:], pb[0][:, 256:512])
            P.copy(vb[q][:], pb[1][:], eng="act")

        def s_update(q, h, d):
            for c2 in range(2):
                P.mm(pb[5 + c2][:], kz[q][:, c2 * 128:(c2 + 1) * 128], vb[q][:])
            for c2 in range(2):
                P.stt(Sm[:, c2, :], Sm[:, c2, :], gch[:, d * 4 + h:d * 4 + h + 1], pb[5 + c2][:], ALU.mult, ALU.add)

        nq = 0
        for h in range(4):
            P.dma(W[:].rearrange("p k c -> p (k c)"), K.wb["ret_in"][j * 4 + h], q="sp")
            P.dma(Wo[:].rearrange("p k c -> p (k c)"), K.wb["ret_out"][j * 4 + h], q="sp")
            for kc in range(4):
                g_ = K.small[:, a_gn + (j * 4 + h) * 4 + kc:a_gn + (j * 4 + h) * 4 + kc + 1]
                P.act(Wo[:, kc, :], Wo[:, kc, :], AF.Copy, scale=g_)
            P.memset(Sm[:], 0.0)
            order1 = [1, 0] + list(range(17, 1, -1))
            q0 = nq % 2
            kv_proj(order1[0], q0)
            P.dma(K.kvc[order1[0]], kvb[q0][:], q="pool")
            P.ts(kz[q0][:], krot[q0][:], zeta[:, 4 + h:5 + h], None, ALU.mult)
            for idx, t in enumerate(order1):
                q = nq % 2
                nq += 1
                P.copy(Sb[q][:], Sm[:], eng="act")
                if need_ctx or t >= 2:
                    P.dma(K.sstate[t], Sb[q][:].rearrange("p c v -> p (c v)"), q="pool")
                if t == 2:
                    break
                tn = order1[idx + 1]
                qn = nq % 2
                kv_proj(tn, qn)
                P.dma(K.kvc[tn], kvb[qn][:], q="pool")
                if tn != 2:
                    P.ts(kz[qn][:], krot[qn][:], zeta[:, 4 + h:5 + h], None, ALU.mult)
                s_update(q, h, 1)
            P.memset(Sm[:], 0.0)
            base = nq
            nq += 18

            def A2(t):
                q = (base + t) % 2
                cols = slice(t * 128, (t + 1) * 128)
                active = need_ctx or t >= 2
                P.dma(kvb[q][:], K.kvc[t], q="sp")
                if active:
                    P.dma(Sl[q][:].rearrange("p c v -> p (c v)"), K.sstate[t], q="sp")
                    for kc in range(8):
                        P.mm(pb[0][:, 0:256], hT[:, kc, cols], W[:, kc, 0:256], start=(kc == 0), stop=(kc == 7))
                    for kc in range(8):
                        P.mm(pb[2][:], hT[:, kc, cols], W[:, kc, 1024:1536], start=(kc == 0), stop=(kc == 7))
                    if t >= 2:
                        rope(pb[0][:, 0:256], qrot[q][:], t - 2)
                    else:
                        P.copy(qrot[q][:], pb[0][:, 0:256])
                    P.act(sg[q][:], pb[2][:], AF.Silu)

            def B2a(t):
                q = (base + t) % 2
                active = need_ctx or t >= 2
                P.ts(kz[q][:], krot[q][:], zeta[:, h:h + 1], None, ALU.mult)
                if active:
                    for c2 in range(2):
                        P.tr(tb3[:, c2 * 128:(c2 + 1) * 128], qrot[q][:, c2 * 128:(c2 + 1) * 128], K.ident[:])
                        P.tr(tb3[:, 256 + c2 * 128:256 + (c2 + 1) * 128], krot[q][:, c2 * 128:(c2 + 1) * 128], K.ident[:])
                    tq = tb3[:, 0:256].rearrange("p (c t) -> p c t", c=2)
                    P.act(qT3[q][:, 0], tq, AF.Copy, scale=1.0 / 16.0)
                    P.tt(qT3[q][:, 1], tq, XI[:, h, :].unsqueeze(1).broadcast_to([128, 2, 128]), ALU.mult)
                    P.tt(qT3[q][:, 2], tq, XI[:, 4 + h, :].unsqueeze(1).broadcast_to([128, 2, 128]), ALU.mult)
                    P.copy(kT[q][:], tb3[:, 256:512].rearrange("p (c t) -> p c t", c=2), eng="act")

            def B2b(t):
                q = (base + t) % 2
                active = need_ctx or t >= 2
                if active:
                    for c2 in range(2):
                        P.mm(pb[3][:, 256:384], kT[q][:, c2, :], qT3[q][:, 0, c2, :], start=(c2 == 0), stop=(c2 == 1))
                    P.tt(inn[q][:], pb[3][:, 256:384], MT[:, h, :], ALU.mult)

            def C2a(t):
                q = (base + t) % 2
                active = need_ctx or t >= 2
                if active:
                    P.mm(pb[4][:], inn[q][:], vb[q][:], start=True, stop=False)
                    for c2 in range(2):
                        P.mm(pb[4][:], qT3[q][:, 1, c2, :], Sb[q][:, c2, :], start=False, stop=False)
                    for c2 in range(2):
                        P.mm(pb[4][:], qT3[q][:, 2, c2, :], Sl[q][:, c2, :], start=False, stop=(c2 == 1))
                    P.add("dve", lambda e, q=q: e.bn_stats(st6[q][:], pb[4][:]), reads=(pb[4][:],), writes=(st6[q][:],))
                    P.add("dve", lambda e, q=q: e.bn_aggr(mv[q][:, 0:2], st6[q][:]), reads=(st6[q][:],), writes=(mv[q][:, 0:2],))
                    P.ts(mv[q][:, 2:3], mv[q][:, 1:2], EPS, None, ALU.add)
                    P.tt(mv[q][:, 2:3], mv[q][:, 2:3], mhalf[:], ALU.pow, eng="pool")
                    P.stt(mv[q][:, 3:4], mv[q][:, 0:1], -1.0, mv[q][:, 2:3], ALU.mult, ALU.mult)
                    P.act(yn[q][:], pb[4][:], AF.Identity, scale=mv[q][:, 2:3], bias=mv[q][:, 3:4])
                    P.tt(zz[q][:], yn[q][:], sg[q][:], ALU.mult)

            def C2b(t):
                q = (base + t) % 2
                active = need_ctx or t >= 2
                if active:
                    blk = 0 if t < 2 else 1 + (t - 2) // 4
                    pos = t if t < 2 else (t - 2) % 4
                    zb = zT[blk % 2]
                    for c4 in range(4):
                        P.tr(tb7[:, c4 * 128:(c4 + 1) * 128], zz[q][:, c4 * 128:(c4 + 1) * 128], K.ident[:])
                    P.copy(zb[:, :, pos * 128:(pos + 1) * 128], tb7[:, 0:512].rearrange("p (c t) -> p c t", c=4), eng="act")

            def C2c(t):
                q = (base + t) % 2
                active = need_ctx or t >= 2
                s_update(q, h, 0)
                if t + 1 < 18:
                    P.copy(Sb[(base + t + 1) % 2][:], Sm[:], eng="act")
                if active and (t == 1 or (t >= 2 and (t - 2) % 4 == 3)):
                    blk = 0 if t < 2 else 1 + (t - 2) // 4
                    zb = zT[blk % 2]
                    a, b = BLK512[blk]
                    n = b - a
                    st = 4 if a < CTXN else s
                    for jj in range(8):
                        bank = pb[5 + jj % 2]
                        for c4 in range(4):
                            P.mm(bank[:, :n], Wo[:, c4, jj * 128:(jj + 1) * 128], zb[:, c4, :n], start=(c4 == 0), stop=(c4 == 3))
                        P.stt(K.xT[:, jj, a:b], bank[:, :n], K.modT[:, l, 2, jj, st:st + 1], K.xT[:, jj, a:b], ALU.mult, ALU.add)

            P.copy(Sb[base % 2][:], Sm[:], eng="act")
            A2(0)
            B2a(0)
            B2b(0)
            for t in range(18):
                nxt = t + 1 < 18
                if nxt:
                    A2(t + 1)
                C2a(t)
                if nxt:
                    B2a(t + 1)
                C2b(t)
                if nxt:
                    B2b(t + 1)
                C2c(t)


def lru_mixer(K, l, s, need_ctx):
    P = K.P
    compute_rstd(K, BLK512)
    a_cw, _ = SM["lru_cw"]
    a_gb, _ = SM["lru_gb"]
    a_lam, _ = SM["lru_lam"]
    with P.scope():
        xc = P.sb("lru_xc", [128, LKC, NT], BF16, chunk=128)
        gy = P.sb("lru_gy", [128, LKC, NT], BF16, chunk=128)
        coef = P.sb("lru_coef", [128, 20], F32)
        ct = P.sb("lru_ct", [128, 20], F32)
        lam = K.small[:, a_lam:a_lam + 20]
        P.act(ct[:], lam, AF.Abs)
        P.act(ct[:], ct[:], AF.Exp, scale=-1.0)
        P.act(ct[:], ct[:], AF.Ln, bias=1.0)
        P.ts(coef[:], lam, -1.0, 0.0, ALU.mult, ALU.max)
        P.tt(coef[:], coef[:], ct[:], ALU.add)
        P.ts(coef[:], coef[:], -4.0, None, ALU.mult)
        hgb = P.sb("lru_hgb", [128, 40], F32)
        P.ts(hgb[:], K.small[:, a_gb:a_gb + 40], 0.5, None, ALU.mult)
        nb = 0
        with P.scope():
            hb = P.sb("lru_h", [128, 8, 416], BF16, chunk=416)
            wi = [P.sb("lru_wi%d" % i, [128, 8, 256], BF16) for i in range(2)]
            tt_ = [P.sb("lru_t%d" % i, [128, 416], F32) for i in range(4)]
            for (c0, c1, s0, s1) in SBLOCKS:
                st = 4 if c0 < CTXN else s
                lo, hi = max(c0 - 2, s0), min(c1 + 1, s1)
                n, W = hi - lo, c1 - c0
                o = c0 - lo
                make_h(K, hb, lo, hi, l, 0, st)
                for c in range(LKC):
                    w = wi[c % 2]
                    P.dma(w[:].rearrange("p k c -> p (k c)"), K.wb["lru_in"][c], q="sp")
                    by = K.pb[nb % 6]
                    bx = K.pb[(nb + 1) % 6]
                    nb += 2
                    for kc in range(8):
                        P.mm(by[:, :W], w[:, kc, 0:128], hb[:, kc, o:o + W], start=(kc == 0), stop=(kc == 7))
                    for kc in range(8):
                        P.mm(bx[:, :n], w[:, kc, 128:256], hb[:, kc, :n], start=(kc == 0), stop=(kc == 7))
                    t1 = tt_[(c % 2) * 2]
                    P.act(t1[:, :W], by[:, :W], AF.Square)
                    P.act(t1[:, :W], t1[:, :W], AF.Identity, scale=0.044715, bias=1.0)
                    P.tt(t1[:, :W], t1[:, :W], by[:, :W], ALU.mult)
                    P.act(t1[:, :W], t1[:, :W], AF.Sigmoid, scale=1.5957691216057308)
                    P.tt(gy[:, c, c0:c1], t1[:, :W], by[:, :W], ALU.mult)
                    t2 = tt_[(c % 2) * 2 + 1]
                    cw = K.small[:, a_cw + c * 5:a_cw + c * 5 + 5]
                    P.act(t2[:, :W], bx[:, o:o + W], AF.Identity, scale=cw[:, 2:3], bias=cw[:, 4:5])
                    taps = [(-2, 0), (-1, 1), (1, 3)]
                    for ti_, (dl, wk) in enumerate(taps):
                        cs = max(c0, lo - dl)
                        ce = min(c1, hi - dl)
                        dst = xc[:, c, cs:ce] if ti_ == 2 else t2[:, cs - c0:ce - c0]
                        P.stt(dst, bx[:, cs + dl - lo:ce + dl - lo], cw[:, wk:wk + 1], t2[:, cs - c0:ce - c0], ALU.mult, ALU.add)
                    if ce < c1:
                        P.copy(xc[:, c, ce:c1], t2[:, ce - c0:W])
        with P.scope():
            hf = P.sb("lru_hf", [128, NT], BF16, chunk=128)
            gwt = [P.sb("lru_gw%d" % i, [128, 2, 3, 128], BF16) for i in range(2)]
            tr_ = [P.sb("lru_r%d" % i, [128, 512], F32) for i in range(2)]
            ti2 = [P.sb("lru_i%d" % i, [128, 512], F32) for i in range(2)]
            ta = [P.sb("lru_a%d" % i, [128, 512], F32) for i in range(2)]
            th = K.tmpf
            nq = 0
            for c in range(LKC):
                slots = _lru_slots(c)
                for d in range(2):
                    gw = gwt[(c * 2 + d) % 2]
                    P.dma(gw[:].rearrange("p g s c -> p (g s c)"), K.wb["lru_gate"][d * 10 + c], q="sp")
                    order = BLK512 if d == 0 else [BLK512[0]] + BLK512[:0:-1]
                    gb = lambda g: hgb[:, (d * 2 + g) * 10 + c:(d * 2 + g) * 10 + c + 1]
                    cf = coef[:, d * 10 + c:d * 10 + c + 1]
                    state = {"prev": None}

                    def G(k):
                        a, b = order[k]
                        n = b - a
                        br = K.pb[(2 * k) % 6]
                        bi = K.pb[(2 * k + 1) % 6]
                        for g, bank in ((0, br), (1, bi)):
                            for si, kc in enumerate(slots):
                                P.mm(bank[:, :n], gw[:, g, si, :], xc[:, kc, a:b], start=(si == 0), stop=(si == len(slots) - 1))
                        r, i_, av = tr_[k % 2], ti2[k % 2], ta[k % 2]
                        P.act(r[:, :n], br[:, :n], AF.Tanh, scale=0.5, bias=gb(0))
                        P.act(i_[:, :n], bi[:, :n], AF.Tanh, scale=0.5, bias=gb(1))
                        P.act(av[:, :n], r[:, :n], AF.Exp, scale=cf, bias=cf)
                        P.tt(r[:, :n], av[:, :n], av[:, :n], ALU.mult)

                    def S(k):
                        a, b = order[k]
                        n = b - a
                        r, i_, av, hh = tr_[k % 2], ti2[k % 2], ta[k % 2], th[k % 2]
                        prev = state["prev"]
                        P.act(r[:, :n], r[:, :n], AF.Sqrt, scale=-1.0, bias=1.0 + 1e-6)
                        P.stt(i_[:, :n], i_[:, :n], 1.0, xc[:, c, a:b], ALU.add, ALU.mult)
                        P.stt(i_[:, :n], i_[:, :n], 0.5, r[:, :n], ALU.mult, ALU.mult)
                        init = 0.0 if prev is None else prev
                        rd = [av[:, :n], i_[:, :n]] + ([] if prev is None else [prev])
                        if d == 0:
                            P.add("dve", lambda e, hh=hh, av=av, i_=i_, n=n, init=init: e.tensor_tensor_scan(
                                hh[:, :n], av[:, :n], i_[:, :n], init, ALU.mult, ALU.add), reads=rd, writes=(hh[:, :n],))
                            state["prev"] = hh[:, n - 1:n]
                            P.copy(hf[:, a:b], hh[:, :n], eng="pool")
                        else:
                            P.add("dve", lambda e, hh=hh, av=av, i_=i_, n=n, init=init: e.tensor_tensor_scan(
                                hh[:, 0:n][:, ::-1], av[:, 0:n][:, ::-1], i_[:, 0:n][:, ::-1], init, ALU.mult, ALU.add),
                                reads=rd, writes=(hh[:, :n],))
                            state["prev"] = hh[:, 0:1]
                            P.tt(r[:, :n], hh[:, :n], hf[:, a:b], ALU.add)
                            P.tt(gy[:, c, a:b], r[:, :n], gy[:, c, a:b], ALU.mult)

                    k0 = 0
                    while k0 < len(order):
                        ks = list(range(k0, min(k0 + 2, len(order))))
                        for k in ks:
                            G(k)
                        for k in ks:
                            S(k)
                        k0 += 2
        with P.scope():
            wo = P.sb("lru_wo", [128, LKC, D], BF16, chunk=D)
            for c in range(0, LKC, 2):
                P.dma(wo[:, c:c + 2, :], K.wb["lru_out"][c:c + 2].rearrange("i p c -> p i c"), q="sp")
            for (a, b) in (BLK512 if need_ctx else BLK512[1:]):
                st = 4 if a < CTXN else s
                n = b - a
                for j in range(8):
                    bank = K.pb[6 + j % 2]
                    for c in range(LKC):
                        P.mm(bank[:, :n], wo[:, c, j * 128:(j + 1) * 128], gy[:, c, a:b], start=(c == 0), stop=(c == LKC - 1))
                    P.stt(K.xT[:, j, a:b], bank[:, :n], K.modT[:, l, 2, j, st:st + 1], K.xT[:, j, a:b], ALU.mult, ALU.add)


def _na_tiles(m):
    res = {}
    for ql in (0, 1):
        r = 2 * m + ql
        start = min(max(r - 4, 0), 24)
        for kt in range(16):
            v0 = start <= 2 * kt < start + 8
            v1 = start <= 2 * kt + 1 < start + 8
            if not (v0 or v1):
                continue
            if v0 and v1:
                sl = (2 * kt - r) + 7
                assert 0 <= sl <= 13
            elif v1:
                assert 2 * kt + 1 - r == -4
                sl = 14
            else:
                assert 2 * kt - r == 3
                sl = 15
            res.setdefault(kt, [None, None])[ql] = sl
    return sorted(res.items())


def na_mixer(K, l, s, need_ctx):
    P = K.P
    compute_rstd(K, BLK512)
    with P.scope():
        hT = P.sb("na_h", [128, 8, NT], BF16, chunk=128)
        W = P.sb("na_w", [128, 8, 768], BF16)
        Wo = P.sb("na_wo", [128, 2, D], BF16)
        tab = P.sb("na_tabs", [128, 4, 16, 64], BF16)
        qT = P.sb("na_q", [128, 2, NT], BF16, chunk=128)
        kT = P.sb("na_k", [128, 2, NT], BF16, chunk=128)
        V = P.sb("na_v", [128, 18, 4, 65], BF16, chunk=260)
        OT = P.sb("na_ot", [128, 2, NT], BF16, chunk=128)
        PT = [P.sb("na_pt%d" % i, [128, 7, 128], BF16, chunk=128) for i in range(2)]
        tmp = [P.sb("na_tmp%d" % i, [128, 5, 128], F32, chunk=64) for i in range(2)]
        Ot = [P.sb("na_o%d" % i, [128, 4, 64], BF16) for i in range(2)]
        rec = [P.sb("na_rec%d" % i, [128, 4], F32) for i in range(2)]
        for (a, b) in BLK512:
            make_h(K, hT[:, :, a:b], a, b, l, 0, 4 if a < CTXN else s)
        P.memset(V[:, :, :, 64:65], 1.0, eng="pool")
        nb = 0
        no = 0
        for G in range(4):
            P.dma(W[:].rearrange("p k c -> p (k c)"), K.wb["na_in"][G], q="sp")
            P.dma(Wo[:].rearrange("p k c -> p (k c)"), K.wb["na_out"][G], q="sp")
            P.dma(tab[:].rearrange("p h s c -> p h (s c)"), K.wb["na_tab"][4 * G:4 * G + 4].rearrange("h p c -> p h c"), q="sp")
            for (a, b) in BLK512:
                n = b - a
                for which, dst in ((0, qT), (1, kT)):
                    for k2 in range(2):
                        bank = K.pb[nb % 4]
                        nb += 1
                        for kc in range(8):
                            P.mm(bank[:, :n], W[:, kc, which * 256 + k2 * 128:which * 256 + (k2 + 1) * 128], hT[:, kc, a:b],
                                 start=(kc == 0), stop=(kc == 7))
                        if which == 0:
                            P.act(dst[:, k2, a:b], bank[:, :n], AF.Copy, scale=0.125)
                        else:
                            P.copy(dst[:, k2, a:b], bank[:, :n])
            for ti in range(18):
                bank = K.pb[nb % 4]
                nb += 1
                for kc in range(8):
                    P.mm(bank[:, :256], hT[:, kc, ti * 128:(ti + 1) * 128], W[:, kc, 512:768], start=(kc == 0), stop=(kc == 7))
                P.copy(V[:, ti, :, 0:64], bank[:, :256].rearrange("p (h d) -> p h d", h=4), eng=("act" if ti % 2 else "dve"))
            jobs = []
            if need_ctx:
                for qt in range(2):
                    jobs.append((qt, [(0, None), (1, None)]))
            for m in range(16):
                jobs.append((2 + m, [(2 + kt, sl) for kt, sl in _na_tiles(m)] + [(0, None), (1, None)]))
            steps = [(ji, hh) for ji in range(len(jobs)) for hh in range(4)]

            def scores(n):
                ji, hh = steps[n]
                qt, tiles = jobs[ji]
                q0 = qt * 128
                k2, p0 = hh // 2, (hh % 2) * 64
                sb = [K.pb[(n % 2) * 2], K.pb[(n % 2) * 2 + 1]]
                pt = PT[n % 2]
                tm = tmp[n % 2]
                nlat = sum(1 for _, sl in tiles if sl is not None)
                for i, (kt, sl) in enumerate(tiles):
                    P.mm(sb[i // 4][:, (i % 4) * 128:(i % 4 + 1) * 128], kT[p0:p0 + 64, k2, kt * 128:(kt + 1) * 128],
                         qT[p0:p0 + 64, k2, q0:q0 + 128])
                for i, (kt, sl) in enumerate(tiles):
                    if sl is None:
                        continue
                    if sl[0] is not None and sl[1] is not None and sl[0] < 14 and sl[1] == sl[0] - 1:
                        src = sb[i // 4][:, (i % 4) * 128:(i % 4 + 1) * 128].rearrange("p (a b) -> p a b", a=2)
                        P.tt(tm[:, i, :].rearrange("p (a b) -> p a b", a=2), src, tab[:, hh, sl[1]:sl[0] + 1, :][:, ::-1, :], ALU.add)
                        continue
                    for ql in range(2):
                        src = sb[i // 4][:, (i % 4) * 128 + ql * 64:(i % 4) * 128 + ql * 64 + 64]
                        if sl[ql] is None:
                            P.memset(tm[:, i, ql * 64:ql * 64 + 64], -30000.0, eng="pool")
                        else:
                            P.tt(tm[:, i, ql * 64:ql * 64 + 64], src, tab[:, hh, sl[ql], :], ALU.add)
                if nlat:
                    P.act(pt[:, 0:nlat, :], tm[:, 0:nlat, :], AF.Exp)
                ci = [i for i, (kt, sl) in enumerate(tiles) if sl is None]
                assert len(ci) == 2 and ci[1] == ci[0] + 1 and ci[0] // 4 == ci[1] // 4
                P.act(pt[:, ci[0]:ci[0] + 2, :], sb[ci[0] // 4][:, (ci[0] % 4) * 128:(ci[0] % 4 + 2) * 128].rearrange("p (a b) -> p a b", a=2), AF.Exp)

            def pv(n):
                ji, hh = steps[n]
                qt, tiles = jobs[ji]
                q0 = qt * 128
                no = ji
                ob = K.pb[4 + no % 2]
                pt = PT[n % 2]
                for i, (kt, sl) in enumerate(tiles):
                    P.mm(ob[:, hh * 65:hh * 65 + 65], pt[:, i, :], V[:, kt, hh, :], start=(i == 0), stop=(i == len(tiles) - 1))
                if hh < 3:
                    return
                o_t = Ot[no % 2]
                rc = rec[no % 2]
                obv = ob[:, 0:260].rearrange("p (h e) -> p h e", h=4)
                P.add("dve", lambda e, rc=rc, obv=obv: e.reciprocal(rc[:], obv[:, :, 64]), reads=(ob[:, 0:260],), writes=(rc[:],))
                P.tt(o_t[:], obv[:, :, 0:64], rc[:].unsqueeze(2).broadcast_to([128, 4, 64]), ALU.mult)
                tb = K.pb[6 + no % 2]
                tbv = tb[:].bitcast(BF16)
                for k2 in range(2):
                    P.tr(tbv[:, k2 * 128:(k2 + 1) * 128], o_t[:, 2 * k2:2 * k2 + 2, :].rearrange("p h d -> p (h d)"), K.ident[:])
                P.copy(OT[:, :, q0:q0 + 128], tbv[:, 0:256].rearrange("p (k t) -> p k t", k=2), eng="act")

            for n in range(len(steps) + 1):
                if n < len(steps):
                    scores(n)
                if n >= 1:
                    pv(n - 1)
            for (a, b) in (BLK512 if need_ctx else BLK512[1:]):
                st = 4 if a < CTXN else s
                n = b - a
                for j in range(8):
                    bank = K.pb[4 + j % 2]
                    for k2 in range(2):
                        P.mm(bank[:, :n], Wo[:, k2, j * 128:(j + 1) * 128], OT[:, k2, a:b], start=(k2 == 0), stop=(k2 == 1))
                    P.stt(K.xT[:, j, a:b], bank[:, :n], K.modT[:, l, 2, j, st:st + 1], K.xT[:, j, a:b], ALU.mult, ALU.add)


_SHARED = {}


def run(inputs, nseq, ncores, plan, final=True, trace=False):
    sh = host_shared(inputs)
    nc, K = build(nseq, plan, final)
    in_maps = []
    for c in range(ncores):
        m = host_core(inputs, sh["small"], c, nseq)
        for n in list(WSPEC) + ["rope", "fin"]:
            m[n] = sh[n]
        in_maps.append(m)
    res = run_bass_kernel_spmd(nc, in_maps, core_ids=list(range(ncores)), trace=trace)
    return res


FULL_PLAN = [(l, h) for l in range(NLAYER) for h in ("m", "f")]


def kernel(**inputs):
    inputs = {k: np.asarray(v) for k, v in inputs.items()}
    res = run(inputs, 4, 8, FULL_PLAN, final=True)
    return np.concatenate([r["y"] for r in res.results], axis=0).astype(np.float32)
```

```python
import numpy as np
import concourse.bass as bass
import concourse.mybir as mybir
from contextlib import ExitStack

F32 = mybir.dt.float32
BF16 = mybir.dt.bfloat16
AF = mybir.ActivationFunctionType
ALU = mybir.AluOpType
AX = mybir.AxisListType

ENGS = ("pe", "act", "dve", "pool", "sp")
NDMASEM = 24
SAME_ENGINE_SYNC = True


class Op:
    __slots__ = ("eng", "idx", "fn", "deps", "needs_inc", "incval", "dma", "dsem", "dval", "dprev", "waits")

    def __init__(self, eng, fn, dma):
        self.eng = eng
        self.fn = fn
        self.dma = dma
        self.deps = {}
        self.needs_inc = False
        self.incval = None
        self.waits = []


class Prog:
    def __init__(self, nc, es):
        self.nc = nc
        self.es = es
        self.ops = {e: [] for e in ENGS}
        self.trk = {}
        self.ndma = {e: 0 for e in ENGS}
        self.waited = {e: {} for e in ENGS}
        self.nops = 0
        self._cc = {}
        self.stack = [es]
        self.dmas = []

    def sb(self, name, shape, dtype, chunk=None):
        self.nsb = getattr(self, "nsb", 0) + 1
        name = "%s_%d" % (name, self.nsb)
        t = self.stack[-1].enter_context(self.nc.sbuf_tensor(name, list(shape), dtype))
        free = int(np.prod(shape[1:]))
        self.trk[name] = (free, chunk or free, {})
        return t

    def ps(self, name, shape, dtype=F32, chunk=None):
        t = self.es.enter_context(self.nc.psum_tensor(name, list(shape), dtype))
        free = int(np.prod(shape[1:]))
        self.trk[name] = (free, chunk or free, {})
        return t

    def dram(self, name, shape, dtype, kind="Internal", chunk=None):
        t = self.nc.dram_tensor(name, list(shape), dtype, kind=kind)
        n = int(np.prod(shape))
        self.trk[name] = (None, chunk or n, {})
        return t.ap()

    def _chunks(self, ap):
        name = ap.tensor.name
        tr = self.trk.get(name)
        if tr is None:
            return None, ()
        pstep, chunk, st = tr
        if chunk >= (1 << 29):
            return st, (0,)
        key = (name, ap.offset, ap.ap)
        r = self._cc.get(key)
        if r is not None:
            return st, r
        off = ap.offset
        dims = list(ap.ap)
        if pstep is not None:
            off = off % pstep
            dims = dims[1:]
        ivs = [(off, off)]
        for step, cnt in reversed(dims):
            if cnt <= 1:
                continue
            span = step * (cnt - 1)
            if abs(step) <= (ivs[0][1] - ivs[0][0] + 1) or len(ivs) * cnt > 256 or abs(step) < chunk // 2:
                ivs = [(lo + min(0, span), hi + max(0, span)) for lo, hi in ivs]
            else:
                ivs = [(lo + step * i, hi + step * i) for i in range(cnt) for lo, hi in ivs]
        cs = set()
        for lo, hi in ivs:
            cs.update(range(lo // chunk, hi // chunk + 1))
        r = tuple(cs)
        self._cc[key] = r
        return st, r

    def _dep(self, op, prod):
        if prod is None or prod is op:
            return
        if prod.dma:
            op.deps[("dma", id(prod))] = prod
            return
        if prod.eng == op.eng and not op.dma and (prod.eng == "pe" or not SAME_ENGINE_SYNC):
            return
        k = prod.eng
        cur = op.deps.get(k)
        if cur is None or cur.idx < prod.idx:
            op.deps[k] = prod

    def add(self, eng, fn, reads=(), writes=(), dma=False, extra=()):
        op = Op(eng, fn, dma)
        op.idx = len(self.ops[eng])
        self.nops += 1
        for p in extra:
            self._dep(op, p)
        for ap in reads:
            st, chs = self._chunks(ap)
            if st is None:
                continue
            for c in chs:
                s = st.get(c)
                if s is None:
                    s = st[c] = [None, {}]
                self._dep(op, s[0])
                s[1][("dma", id(op)) if dma else eng] = op
        for ap in writes:
            st, chs = self._chunks(ap)
            if st is None:
                continue
            for c in chs:
                s = st.get(c)
                if s is None:
                    s = st[c] = [None, {}]
                self._dep(op, s[0])
                for r in s[1].values():
                    self._dep(op, r)
                s[0] = op
                s[1] = {}
        w = self.waited[eng]
        for k, prod in op.deps.items():
            if prod.dma:
                key = ("dma", id(prod))
                if w.get(key):
                    continue
                w[key] = True
                op.waits.append(prod)
            else:
                if w.get(k, -1) >= prod.idx:
                    continue
                w[k] = prod.idx
                prod.needs_inc = True
                op.waits.append(prod)
        if dma:
            j = self.ndma[eng]
            self.ndma[eng] += 1
            op.dsem = j % NDMASEM
            op.dval = 16 * (j // NDMASEM + 1)
        self.ops[eng].append(op)
        if dma:
            self.dmas.append(op)
        return op

    def barrier(self):
        lasts = [self.ops[e][-1] for e in ENGS if self.ops[e]]
        dm = self.dmas
        self.dmas = []
        for e in ENGS:
            if self.ops[e]:
                self.add(e, None, extra=[p for p in lasts if p.eng != e] + dm)

    class _Scope:
        def __init__(self, P):
            self.P = P

        def __enter__(self):
            self.es = ExitStack()
            self.es.__enter__()
            self.P.stack.append(self.es)
            return self

        def __exit__(self, *a):
            self.P.barrier()
            self.P.stack.pop()
            return self.es.__exit__(*a)

    def scope(self):
        return Prog._Scope(self)

    def mm(self, out, lhsT, rhs, start=True, stop=True, **kw):
        return self.add("pe", lambda e: e.matmul(out, lhsT, rhs, start=start, stop=stop, **kw),
                        reads=(lhsT, rhs), writes=(out,))

    def tr(self, out, in_, ident):
        return self.add("pe", lambda e: e.transpose(out, in_, ident), reads=(in_, ident), writes=(out,))

    def act(self, out, in_, func, bias=None, scale=None, accum_out=None, eng="act"):
        kw = {}
        rd = [in_]
        if bias is not None:
            kw["bias"] = bias
            if not isinstance(bias, (int, float)):
                rd.append(bias)
        if scale is not None:
            kw["scale"] = scale
            if not isinstance(scale, (int, float)):
                rd.append(scale)
        wr = [out]
        if accum_out is not None:
            kw["accum_out"] = accum_out
            wr.append(accum_out)
        return self.add("act", lambda e: e.activation(out, in_, func, **kw), reads=rd, writes=wr)

    def tt(self, out, in0, in1, op, eng="dve"):
        return self.add(eng, lambda e: e.tensor_tensor(out, in0, in1, op), reads=(in0, in1), writes=(out,))

    def ts(self, out, in0, s1, s2, op0, op1=None, eng="dve", accum_out=None):
        rd = [in0] + [s for s in (s1, s2) if s is not None and not isinstance(s, (int, float))]
        wr = [out] + ([accum_out] if accum_out is not None else [])
        kw = {}
        if op1 is not None:
            kw["op1"] = op1
        if accum_out is not None:
            kw["accum_out"] = accum_out
        return self.add(eng, lambda e: e.tensor_scalar(out, in0, s1, s2, op0, **kw), reads=rd, writes=wr)

    def stt(self, out, in0, scalar, in1, op0, op1):
        rd = [in0, in1] + ([scalar] if not isinstance(scalar, (int, float)) else [])
        return self.add("dve", lambda e: e.scalar_tensor_tensor(out, in0, scalar, in1, op0, op1),
                        reads=rd, writes=(out,))

    def copy(self, out, in_, eng="dve"):
        if eng == "act":
            return self.add("act", lambda e: e.copy(out, in_), reads=(in_,), writes=(out,))
        return self.add(eng, lambda e: e.tensor_copy(out, in_), reads=(in_,), writes=(out,))

    def memset(self, ap, val, eng="dve"):
        return self.add(eng, lambda e: e.memset(ap, val), writes=(ap,))

    def dma(self, out, in_, q="sp", **kw):
        return self.add(q, lambda e: e.dma_start(out=out, in_=in_, **kw), reads=(in_,), writes=(out,), dma=True)

    def emit(self):
        nc = self.nc
        es = self.es
        sem = {e: es.enter_context(nc.semaphore("s_" + e)) for e in ENGS}
        dsem = {e: [es.enter_context(nc.semaphore("d_%s%d" % (e, i))) for i in range(NDMASEM)]
                for e in ENGS if self.ndma[e]}
        for e in ENGS:
            c = 0
            for op in self.ops[e]:
                if op.needs_inc:
                    c += 1
                    op.incval = c
        self.maxinc = {e: max([op.incval or 0 for op in self.ops[e]] + [0]) for e in ENGS}
        block = es.enter_context(nc.Block())
        hooks = {"pe": block.tensor, "act": block.scalar, "dve": block.vector, "pool": block.gpsimd,
                 "sp": block.sync}
        for e in ENGS:
            ops = self.ops[e]
            nd = self.ndma[e]

            def body(eng, e=e, ops=ops, nd=nd):
                for op in ops:
                    for p in op.waits:
                        if p.dma:
                            eng.wait_ge(dsem[p.eng][p.dsem], p.dval)
                        else:
                            eng.wait_ge(sem[p.eng], p.incval)
                    if op.dma and op.dval > 16:
                        eng.wait_ge(dsem[e][op.dsem], op.dval - 16)
                    if op.fn is None:
                        if op.needs_inc:
                            eng.nop().then_inc(sem[e], 1)
                        continue
                    ins = op.fn(eng)
                    if op.dma:
                        ins.then_inc(dsem[e][op.dsem], 16)
                    elif op.needs_inc:
                        ins.then_inc(sem[e], 1)
                if nd:
                    for i in range(min(nd, NDMASEM)):
                        last = ((nd - 1 - i) // NDMASEM) + 1
                        eng.wait_ge(dsem[e][i], 16 * last)
            if ops:
                hooks[e](body)

from concourse.bass_utils import run_bass_kernel_spmd

D = 1024
KC = 8
CTXN = 256
LATN = 2048
NT = CTXN + LATN
NLAYER = 4
FF = 2816
NPAIR = 22
EPS = 1e-6
LW = 1280
LKC = 10
LAT_SB = [0, 410, 820, 1230, 1639, 2048]
SBLOCKS = [(0, 256, 0, 256)] + [(256 + LAT_SB[i], 256 + LAT_SB[i + 1], 256, NT) for i in range(5)]
BLK512 = [(0, 256)] + [(256 + 512 * i, 256 + 512 * (i + 1)) for i in range(4)]

SM = {}
_o = 0
for _n, _w in [("cT", 40), ("ada_b", 4 * 48), ("gains", 4 * 2 * 8), ("ffn_cw", 4 * 44 * 4), ("ret_gn", 2 * 4 * 4),
               ("ret_decay", 2 * 2 * 4), ("lru_cw", 10 * 5), ("lru_gb", 2 * 2 * 10), ("lru_lam", 2 * 10),
               ]:
    SM[_n] = (_o, _o + _w)
    _o += _w
NSMALL = _o

WSPEC = {
    "ada_w": (4 * 48, 1024, False),
    "ffn_up": (4 * 22, 2048, True),
    "ffn_down": (4 * 22, 1024, True),
    "ret_in": (2 * 4, 8 * 1536, True),
    "ret_out": (2 * 4, 4 * 1024, True),
    "lru_in": (10, 2048, True),
    "lru_gate": (2 * 10, 768, True),
    "lru_out": (10, 1024, True),
    "na_in": (4, 8 * 768, True),
    "na_out": (4, 2 * 1024, True),
    "na_tab": (16, 16 * 64, True),
}


def _lru_slots(c):
    b0 = (128 * c) // 160
    b1 = (128 * c + 127) // 160
    k0 = (160 * b0) // 128
    k1 = (160 * (b1 + 1) - 1) // 128
    return list(range(k0, k1 + 1))


def host_shared(inp):
    f = np.float32
    out = {}
    aw = inp["ada_w"].reshape(4, 8, 128, 48, 128).transpose(0, 3, 2, 1, 4)
    out["ada_w"] = np.ascontiguousarray(aw).reshape(4 * 48, 128, 1024)
    wu = inp["ffn_w_up"].reshape(4, 8, 128, 2, 22, 128).transpose(0, 4, 2, 1, 3, 5)
    out["ffn_up"] = np.ascontiguousarray(wu).reshape(4 * 22, 128, 2048)
    out["ffn_down"] = np.ascontiguousarray(inp["ffn_w_down"].reshape(4 * 22, 128, 1024))
    ri = inp["ret_w_in"].reshape(2, 8, 128, 6144)
    tiles = []
    for j in range(2):
        for h in range(4):
            cols = np.concatenate([np.arange(h * 256, h * 256 + 256), 1024 + np.arange(h * 256, h * 256 + 256),
                                   2048 + np.arange(h * 512, h * 512 + 512), 4096 + np.arange(h * 512, h * 512 + 512)])
            tiles.append(ri[j][:, :, cols].transpose(1, 0, 2).reshape(128, 8 * 1536))
    out["ret_in"] = np.ascontiguousarray(np.stack(tiles))
    ro = inp["ret_w_out"].reshape(2, 4, 4, 128, 1024).transpose(0, 1, 3, 2, 4)
    out["ret_out"] = np.ascontiguousarray(ro).reshape(8, 128, 4096)
    li = inp["lru_w_in"][0].reshape(8, 128, 2, 10, 128).transpose(3, 1, 0, 2, 4)
    out["lru_in"] = np.ascontiguousarray(li).reshape(10, 128, 2048)
    gw = inp["lru_gate_w"][0]
    lg = np.zeros((2, 10, 128, 2, 3, 128), f)
    for d in range(2):
        for g in range(2):
            dense = np.zeros((LW, LW), f)
            for k in range(8):
                dense[k * 160:(k + 1) * 160, k * 160:(k + 1) * 160] = gw[d, g, k]
            for c in range(10):
                for si, kc in enumerate(_lru_slots(c)):
                    lg[d, c, :, g, si, :] = dense[kc * 128:(kc + 1) * 128, c * 128:(c + 1) * 128]
    out["lru_gate"] = lg.reshape(20, 128, 768)
    out["lru_out"] = np.ascontiguousarray(inp["lru_w_out"][0].reshape(10, 128, 1024))
    nq = inp["na_w_qkv"][0].reshape(8, 128, 3, 4, 256).transpose(3, 1, 0, 2, 4)
    out["na_in"] = np.ascontiguousarray(nq).reshape(4, 128, 8 * 768)
    no = inp["na_w_out"][0].reshape(4, 2, 128, 1024).transpose(0, 2, 1, 3)
    out["na_out"] = np.ascontiguousarray(no).reshape(4, 128, 2048)
    rpb = inp["na_rpb"][0]
    NEG = f(-30000.0)
    kc_i = np.arange(64)[:, None]
    qc_i = np.arange(64)[None, :]
    cs = np.clip(qc_i - 8, 0, 48)
    col_in = (kc_i >= cs) & (kc_i < cs + 16)
    dc = np.clip(kc_i - qc_i, -15, 15) + 15
    def blk(h, dr):
        if dr is None or dr < -7 or dr > 7:
            return np.full((64, 64), NEG, f)
        return np.where(col_in, rpb[h, dr + 7][dc], NEG).astype(f)
    tab = np.zeros((128, 16, 16, 64), f)
    for h in range(16):
        for sl in range(16):
            if sl < 14:
                d0, d1 = sl - 7, sl - 6
            elif sl == 14:
                d0, d1 = None, -4
            else:
                d0, d1 = 3, None
            tab[0:64, h, sl] = blk(h, d0)
            tab[64:128, h, sl] = blk(h, d1)
    out["na_tab"] = np.ascontiguousarray(tab.transpose(1, 0, 2, 3)).reshape(16, 128, 1024)
    sm = np.zeros((128, NSMALL), f)
    def put(name, arr):
        a, b = SM[name]
        sm[:, a:b] = arr.reshape(128, b - a)
    put("ada_b", inp["ada_b"].reshape(4, 48, 128).transpose(2, 0, 1))
    g = np.stack([inp["norm_mix"], inp["norm_ffn"]], axis=1).reshape(4, 2, 8, 128).transpose(3, 0, 1, 2)
    put("gains", g)
    cw = np.concatenate([inp["ffn_conv_w"], inp["ffn_conv_b"][:, None, :]], axis=1)
    put("ffn_cw", cw.reshape(4, 4, 44, 128).transpose(3, 0, 2, 1))
    put("ret_gn", inp["ret_gn"].reshape(2, 4, 4, 128).transpose(3, 0, 1, 2))
    put("ret_decay", np.broadcast_to(inp["ret_logit_decay"].reshape(1, 16), (128, 16)))
    lcw = np.concatenate([inp["lru_conv_w"][0], inp["lru_conv_b"][0][None]], axis=0)
    put("lru_cw", lcw.reshape(5, 10, 128).transpose(2, 1, 0))
    put("lru_gb", inp["lru_gate_b"][0].reshape(2, 2, 10, 128).transpose(3, 0, 1, 2))
    put("lru_lam", inp["lru_lambda"][0].reshape(2, 10, 128).transpose(2, 0, 1))
    half = 128
    freqs = (10000.0 ** (-np.arange(0, half, 2, dtype=np.float32) / half)).astype(f)
    p = np.arange(128)
    rc = np.zeros((128, 16, 2, 64), f)
    rs = np.zeros((128, 16, 2, 64), f)
    for m in range(16):
        row = (2 * m + p // 64).astype(f)
        col = (p % 64).astype(f)
        ar = row[:, None] * freqs[None, :]
        ac = col[:, None] * freqs[None, :]
        rc[:, m, 0] = np.cos(ar); rc[:, m, 1] = np.cos(ac)
        rs[:, m, 0] = np.sin(ar); rs[:, m, 1] = np.sin(ac)
    out["rope"] = np.concatenate([rc.reshape(128, 2048), rs.reshape(128, 2048),
                                  np.broadcast_to(np.arange(128, dtype=f)[None, :], (128, 128)),
                                  np.broadcast_to(np.arange(128, dtype=f)[:, None], (128, 8))], axis=1)
    out["fin"] = np.ascontiguousarray(np.broadcast_to(inp["final_norm"][None, :], (128, 1024))).astype(f)
    out["small"] = sm
    return out


def host_core(inp, shared_small, core, nseq):
    sm = shared_small.copy()
    c = inp["c"][core * nseq:(core + 1) * nseq]
    cT = np.zeros((128, 8, 5), np.float32)
    cT[:, :, :nseq] = c.reshape(nseq, 8, 128).transpose(2, 1, 0)
    cT[:, :, 4] = inp["c_ctx"].reshape(8, 128).T
    a, b = SM["cT"]
    sm[:, a:b] = cT.reshape(128, 40)
    return {"x": np.ascontiguousarray(inp["x"][core * nseq:(core + 1) * nseq]),
            "ctx": np.ascontiguousarray(inp["ctx"][core * nseq:(core + 1) * nseq]),
            "small": sm}


class Ctx:
    pass


def build(nseq, plan, final=True):
    nc = bass.Bass("TRN2", target_bir_lowering=False)
    es = ExitStack()
    P = Prog(nc, es)
    K = Ctx()
    K.P, K.nc, K.nseq = P, nc, nseq
    dr = {}
    dr["x"] = nc.dram_tensor("x", [nseq, LATN, D], F32, kind="ExternalInput").ap()
    dr["ctx"] = nc.dram_tensor("ctx", [nseq, CTXN, D], F32, kind="ExternalInput").ap()
    dr["small"] = nc.dram_tensor("small", [128, NSMALL], F32, kind="ExternalInput").ap()
    dr["rope"] = nc.dram_tensor("rope", [128, 4232], F32, kind="ExternalInput").ap()
    dr["fin"] = nc.dram_tensor("fin", [128, 1024], F32, kind="ExternalInput").ap()
    used = set(["ada_w", "ffn_up", "ffn_down"])
    kinds = {0: "ret", 1: "lru", 2: "na", 3: "ret"}
    for (l, hf) in plan:
        if hf == "m":
            used.update({"ret": ["ret_in", "ret_out"], "lru": ["lru_in", "lru_gate", "lru_out"],
                         "na": ["na_in", "na_out", "na_tab"]}[kinds[l]])
    K.used = used
    for n, (nt, te, cast) in WSPEC.items():
        dr[n] = nc.dram_tensor(n, [nt, 128, te], F32, kind="ExternalInput").ap()
        P.trk[n] = (None, 128 * te, {})
    K.y = nc.dram_tensor("y", [nseq, LATN, D], F32, kind="ExternalOutput").ap()
    P.trk["y"] = (None, 128 * D, {})
    if not final:
        K.yc = nc.dram_tensor("yc", [nseq, CTXN, D], F32, kind="ExternalOutput").ap()
        P.trk["yc"] = (None, 128 * D, {})
    K.dr = dr
    K.wb = {}
    for n, (nt, te, cast) in WSPEC.items():
        if cast and n in used:
            K.wb[n] = P.dram("wb_" + n, [nt, 128, te], BF16, chunk=128 * te)
    layers_needed = sorted(set(l for l, _ in plan))
    order = []
    for l in layers_needed:
        kd = kinds[l]
        j = l // 3
        if (l, "m") in plan:
            if kd == "ret":
                order += [("ret_in", j * 4, j * 4 + 4), ("ret_out", j * 4, j * 4 + 4)]
            elif kd == "lru":
                order += [("lru_in", 0, 10), ("lru_gate", 0, 20), ("lru_out", 0, 10)]
            else:
                order += [("na_in", 0, 4), ("na_out", 0, 4), ("na_tab", 0, 16)]
        if (l, "f") in plan:
            order += [("ffn_up", l * 22, l * 22 + 22), ("ffn_down", l * 22, l * 22 + 22)]
    for n, a, b in order:
        step = max(1, (1 << 20) // (128 * WSPEC[n][1] * 4) * 2)
        for i in range(a, b, step):
            e = min(b, i + step)
            P.dma(K.wb[n][i:e], dr[n][i:e], q="pool")

    K.pb = [P.ps("pb%d" % i, [128, 512], F32, chunk=1 << 30) for i in range(8)]
    K.small = P.sb("small_sb", [128, NSMALL], F32, chunk=64)
    K.xT = P.sb("xT", [128, KC, NT], F32, chunk=128)
    K.rstd = P.sb("rstd", [128, NT], F32, chunk=128)
    K.identf = P.sb("identf", [128, 128], F32)
    K.ident = P.sb("ident", [128, 128], BF16)
    K.onesb = P.sb("onesb", [128, 128], BF16)
    K.modT = P.sb("modT", [128, NLAYER, 6, 8, 5], F32, chunk=40)
    K.A = P.sb("Amod", [128, NLAYER, 2, 8, 5], F32, chunk=40)
    K.tmpf = [P.sb("tmpf%d" % i, [128, 512], F32) for i in range(2)]

    def sm(name):
        a, b = SM[name]
        return K.small[:, a:b]
    K.sm = sm
    P.dma(K.small[:], dr["small"], q="sp")
    P.memset(K.identf[:], 0.0)
    P.add("pool", lambda e: e.affine_select(K.identf[:], K.identf[:], [[-1, 128]], ALU.not_equal, 1.0, base=0,
                                            channel_multiplier=1), reads=(K.identf[:],), writes=(K.identf[:],))
    P.copy(K.ident[:], K.identf[:])
    P.memset(K.onesb[:], 1.0)

    prologue(K, layers_needed)
    for s in range(nseq):
        load_seq(K, s)
        for (l, hf) in plan:
            need_ctx = l < NLAYER - 1
            if hf == "m":
                kd = kinds[l]
                if kd == "ret":
                    ret_mixer(K, l, s, need_ctx)
                elif kd == "lru":
                    lru_mixer(K, l, s, need_ctx)
                else:
                    na_mixer(K, l, s, need_ctx)
            else:
                ffn(K, l, s, need_ctx)
        store_seq(K, s, final)
    P.emit()
    es.close()
    return nc, K


def prologue(K, layers):
    P = K.P
    with P.scope():
        sc = P.sb("silu_c", [128, 8, 5], F32)
        wt = [P.sb("adaw%d" % i, [128, 8, 128], F32) for i in range(3)]
        a, b = SM["cT"]
        P.act(sc[:].rearrange("p k s -> p (k s)"), K.small[:, a:b], AF.Silu)
        n = 0
        for l in layers:
            bank = K.pb[l % 2]
            for oc in range(48):
                w = wt[n % 3]
                n += 1
                P.dma(w[:].rearrange("p k c -> p (k c)"), K.dr["ada_w"][l * 48 + oc], q="sp")
                for kc in range(8):
                    P.mm(bank[:, oc * 5:oc * 5 + 5], w[:, kc, :], sc[:, kc, :], start=(kc == 0), stop=(kc == 7))
            a, b = SM["ada_b"]
            ab = K.small[:, a + l * 48:a + l * 48 + 48].unsqueeze(2).broadcast_to([128, 48, 5])
            P.tt(K.modT[:, l].rearrange("p m k s -> p (m k) s"), bank[:, 0:240].rearrange("p (o s) -> p o s", s=5), ab, ALU.add)
            a, b = SM["gains"]
            for w_ in range(2):
                g = K.small[:, a + (l * 2 + w_) * 8:a + (l * 2 + w_) * 8 + 8].unsqueeze(2).broadcast_to([128, 8, 5])
                P.ts(K.A[:, l, w_], K.modT[:, l, 1 + 3 * w_], 1.0, None, ALU.add)
                P.tt(K.A[:, l, w_], K.A[:, l, w_], g, ALU.mult)


def load_seq(K, s):
    P = K.P
    with P.scope():
        st = [P.sb("ldst%d" % i, [128, D], F32) for i in range(2)]
        for ti in range(18):
            src = K.dr["ctx"][s, ti * 128:(ti + 1) * 128, :] if ti < 2 else K.dr["x"][s, (ti - 2) * 128:(ti - 1) * 128, :]
            b = st[ti % 2]
            P.dma(b[:], src, q="sp")
            for hf in range(2):
                bank = K.pb[(ti * 2 + hf) % 4]
                for k in range(4):
                    P.tr(bank[:, k * 128:(k + 1) * 128], b[:, (hf * 4 + k) * 128:(hf * 4 + k + 1) * 128], K.identf[:])
                P.copy(K.xT[:, hf * 4:hf * 4 + 4, ti * 128:(ti + 1) * 128], bank[:].rearrange("p (k t) -> p k t", k=4),
                       eng=("act" if hf else "dve"))


def store_seq(K, s, final):
    P = K.P
    with P.scope():
        st = [P.sb("stst%d" % i, [128, D], F32) for i in range(2)]
        junk = P.sb("stjunk", [128, D], F32)
        ss = P.sb("stss", [128, 4], F32, chunk=1)
        fing = P.sb("fing", [128, D], F32)
        if final:
            P.dma(fing[:], K.dr["fin"], q="sp")
        tiles = range(2, 18) if final else range(18)
        for n, ti in enumerate(tiles):
            b = st[n % 2]
            banks = [K.pb[(n % 2) * 2], K.pb[(n % 2) * 2 + 1]]
            for kc in range(8):
                P.tr(banks[kc // 4][:, (kc % 4) * 128:(kc % 4 + 1) * 128], K.xT[:, kc, ti * 128:(ti + 1) * 128], K.identf[:])
            if final:
                q = ss[:, (n % 2) * 2:(n % 2) * 2 + 2]
                for hf in range(2):
                    P.act(junk[:, hf * 512:(hf + 1) * 512], banks[hf][:], AF.Square, accum_out=q[:, hf:hf + 1])
                P.tt(q[:, 0:1], q[:, 0:1], q[:, 1:2], ALU.add)
                P.act(q[:, 0:1], q[:, 0:1], AF.Sqrt, scale=1.0 / D, bias=EPS)
                P.add("dve", lambda e, q=q: e.reciprocal(q[:, 0:1], q[:, 0:1]), reads=(q[:, 0:1],), writes=(q[:, 0:1],))
                for hf in range(2):
                    P.stt(b[:, hf * 512:(hf + 1) * 512], banks[hf][:], q[:, 0:1], fing[:, hf * 512:(hf + 1) * 512],
                          ALU.mult, ALU.mult)
            else:
                for hf in range(2):
                    P.copy(b[:, hf * 512:(hf + 1) * 512], banks[hf][:], eng=("act" if hf else "dve"))
            if ti < 2:
                dst = K.yc[s, ti * 128:(ti + 1) * 128, :]
            else:
                dst = K.y[s, (ti - 2) * 128:(ti - 1) * 128, :]
            P.dma(dst, b[:], q="sp")


def compute_rstd(K, cols):
    P = K.P
    for bi, (a, b) in enumerate(cols):
        n = b - a
        bank = K.pb[6 + bi % 2]
        for kc in range(8):
            t = K.tmpf[kc % 2][:].bitcast(BF16)[:, 0:512]
            if kc % 2 == 0:
                P.act(t[:, :n], K.xT[:, kc, a:b], AF.Square)
            else:
                P.tt(t[:, :n], K.xT[:, kc, a:b], K.xT[:, kc, a:b], ALU.mult)
            P.mm(bank[:, :n], K.onesb[:], t[:, :n], start=(kc == 0), stop=(kc == 7))
        P.act(K.rstd[:, a:b], bank[:, :n], AF.Sqrt, scale=1.0 / D, bias=EPS)
        P.add("dve", lambda e, a=a, b=b: e.reciprocal(K.rstd[:, a:b], K.rstd[:, a:b]),
              reads=(K.rstd[:, a:b],), writes=(K.rstd[:, a:b],))


def make_h(K, dst, a, b, l, w, s, off=0):
    P = K.P
    n = b - a
    for kc in range(8):
        t = K.tmpf[kc % 2]
        P.stt(t[:, :n], K.xT[:, kc, a:b], K.A[:, l, w, kc, s:s + 1], K.rstd[:, a:b], ALU.mult, ALU.mult)
        P.act(dst[:, kc, off:off + n], t[:, :n], AF.Identity, bias=K.modT[:, l, 3 * w, kc, s:s + 1])


def ffn(K, l, s, need_ctx):
    P = K.P
    blocks = SBLOCKS if need_ctx else SBLOCKS[1:]
    compute_rstd(K, BLK512 if need_ctx else BLK512[1:])
    a0, _ = SM["ffn_cw"]
    with P.scope():
        wd = P.sb("ffn_wd", [128, NPAIR, D], BF16, chunk=D)
        wu = [P.sb("ffn_wu%d" % i, [128, 8, 256], BF16) for i in range(3)]
        hbs = [P.sb("ffn_h%d" % i, [128, 8, 512], BF16, chunk=512) for i in range(2)]
        aT = P.sb("ffn_a", [128, NPAIR, 512], BF16, chunk=512)
        hs = P.sb("ffn_hs", [128, 8, 1], BF16)
        tg = [P.sb("ffn_tg%d" % i, [128, 512], F32) for i in range(2)]
        tv = [P.sb("ffn_tv%d" % i, [128, 512], F32) for i in range(2)]
        for i in range(0, NPAIR, 2):
            P.dma(wd[:, i:i + 2, :], K.wb["ffn_down"][l * 22 + i:l * 22 + i + 2].rearrange("i p c -> p i c"), q="sp")
        nb = 0

        def geom(blk):
            c0, c1, s0, s1 = blk
            return c0, c1, max(c0 - 1, s0), min(c1 + 1, s1), (4 if c0 < CTXN else s)

        def build_h(k):
            c0, c1, lo, hi, st = geom(blocks[k])
            hb = hbs[k % 2]
            if lo < c0:
                make_h(K, hb, c0, hi, l, 1, st, off=1)
                P.copy(hb[:, :, 0:1], hs[:], eng="pool")
            else:
                make_h(K, hb, lo, hi, l, 1, st)
            P.copy(hs[:], hb[:, :, c1 - 1 - lo:c1 - lo], eng="pool")

        build_h(0)
        for k, blk in enumerate(blocks):
            c0, c1, lo, hi, st = geom(blk)
            hb = hbs[k % 2]
            n, W = hi - lo, c1 - c0
            for i in range(NPAIR):
                if i == 14 and k + 1 < len(blocks):
                    build_h(k + 1)
                w = wu[i % 3]
                P.dma(w[:].rearrange("p k c -> p (k c)"), K.wb["ffn_up"][l * 22 + i], q="sp")
                outs = []
                for hf in range(2):
                    bank = K.pb[nb % 6]
                    nb += 1
                    for kc in range(8):
                        P.mm(bank[:, :n], w[:, kc, hf * 128:(hf + 1) * 128], hb[:, kc, :n], start=(kc == 0), stop=(kc == 7))
                    t = (tg if hf == 0 else tv)[i % 2]
                    ch = i + 22 * hf
                    cw = K.small[:, a0 + (l * 44 + ch) * 4:a0 + (l * 44 + ch) * 4 + 4]
                    o = c0 - lo
                    P.act(t[:, :W], bank[:, o:o + W], AF.Identity, scale=cw[:, 1:2], bias=cw[:, 3:4])
                    if o == 1:
                        P.stt(t[:, :W], bank[:, 0:W], cw[:, 0:1], t[:, :W], ALU.mult, ALU.add)
                    else:
                        P.stt(t[:, 1:W], bank[:, 0:W - 1], cw[:, 0:1], t[:, 1:W], ALU.mult, ALU.add)
                    if hi == c1 + 1:
                        P.stt(t[:, :W], bank[:, o + 1:o + 1 + W], cw[:, 2:3], t[:, :W], ALU.mult, ALU.add)
                    else:
                        P.stt(t[:, :W - 1], bank[:, o + 1:o + W], cw[:, 2:3], t[:, :W - 1], ALU.mult, ALU.add)
                    outs.append(t)
                P.act(outs[0][:, :W], outs[0][:, :W], AF.Silu)
                P.tt(aT[:, i, :W], outs[0][:, :W], outs[1][:, :W], ALU.mult)
            for j in range(8):
                bank = K.pb[6 + j % 2]
                for i in range(NPAIR):
                    P.mm(bank[:, :W], wd[:, i, j * 128:(j + 1) * 128], aT[:, i, :W], start=(i == 0), stop=(i == NPAIR - 1))
                P.stt(K.xT[:, j, c0:c1], bank[:, :W], K.modT[:, l, 5, j, st:st + 1], K.xT[:, j, c0:c1], ALU.mult, ALU.add)


def ret_mixer(K, l, s, need_ctx):
    P = K.P
    j = l // 3
    compute_rstd(K, BLK512)
    a_dec, _ = SM["ret_decay"]
    a_gn, _ = SM["ret_gn"]
    if not hasattr(K, "sstate"):
        K.sstate = P.dram("sb_state", [18, 128, 1024], BF16, chunk=128 * 1024)
        K.kvc = P.dram("kv_cache", [18, 128, 768], BF16, chunk=128 * 768)
    with P.scope():
        hT = P.sb("rt_h", [128, 8, NT], BF16, chunk=128)
        for (a, b) in BLK512:
            make_h(K, hT[:, :, a:b], a, b, l, 0, 4 if a < CTXN else s)
        rc = P.sb("rt_rc", [128, 16, 2, 64], BF16)
        rs = P.sb("rt_rs", [128, 16, 2, 64], BF16)
        io = P.sb("rt_io", [128, 136], F32)
        P.dma(rc[:].rearrange("p m h f -> p (m h f)"), K.dr["rope"][:, 0:2048], q="pool")
        P.dma(rs[:].rearrange("p m h f -> p (m h f)"), K.dr["rope"][:, 2048:4096], q="pool")
        P.dma(io[:], K.dr["rope"][:, 4096:4232], q="sp")
        io_row, io_p = io[:, 0:128], io[:, 128:129]
        lg = P.sb("rt_lg", [128, 8], F32)
        ct = P.sb("rt_ct", [128, 8], F32)
        nlg = P.sb("rt_nlg", [128, 8], F32)
        lg127 = P.sb("rt_lg127", [128, 8], F32)
        lg128 = P.sb("rt_lg128", [128, 8], F32)
        gch = P.sb("rt_gch", [128, 8], F32)
        zeta = P.sb("rt_zeta", [128, 8], F32)
        logit = K.small[:, a_dec + j * 8:a_dec + j * 8 + 8]
        P.act(ct[:], logit, AF.Abs)
        P.act(ct[:], ct[:], AF.Exp, scale=-1.0)
        P.act(ct[:], ct[:], AF.Ln, bias=1.0)
        P.ts(nlg[:], logit, -1.0, 0.0, ALU.mult, ALU.max)
        P.tt(nlg[:], nlg[:], ct[:], ALU.add)
        P.ts(lg[:], nlg[:], -1.0, None, ALU.mult)
        P.ts(lg127[:], lg[:], 127.0, None, ALU.mult)
        P.ts(lg128[:], lg[:], 128.0, None, ALU.mult)
        P.act(gch[:], lg128[:], AF.Exp)
        MT = P.sb("rt_MT", [128, 4, 128], BF16)
        XI = P.sb("rt_XI", [128, 8, 128], BF16)
        with P.scope():
            Dm = P.sb("rt_D", [128, 128], F32)
            Dp = P.sb("rt_Dp", [128, 128], F32)
            Dn = P.sb("rt_Dn", [128, 128], F32)
            ge0 = P.sb("rt_ge0", [128, 128], F32)
            le0 = P.sb("rt_le0", [128, 128], F32)
            e1 = P.sb("rt_e1", [128, 128], F32)
            e2 = P.sb("rt_e2", [128, 128], F32)
            P.ts(Dm[:], io_row, io_p, None, ALU.subtract)
            P.ts(Dp[:], Dm[:], 0.0, None, ALU.max)
            P.ts(Dn[:], Dm[:], -1.0, 0.0, ALU.mult, ALU.max)
            P.ts(ge0[:], Dm[:], 0.0, None, ALU.is_ge)
            P.ts(le0[:], Dm[:], 0.0, None, ALU.is_le)
            for h in range(4):
                P.act(e1[:], Dp[:], AF.Exp, scale=lg[:, h:h + 1])
                P.tt(e1[:], e1[:], ge0[:], ALU.mult)
                P.act(e2[:], Dn[:], AF.Exp, scale=lg[:, 4 + h:5 + h])
                P.tt(e2[:], e2[:], le0[:], ALU.mult)
                P.tt(MT[:, h, :], e1[:], e2[:], ALU.add)
                P.act(XI[:, h, :], io_row, AF.Exp, scale=lg[:, h:h + 1], bias=lg[:, h:h + 1])
                P.act(XI[:, 4 + h, :], io_row, AF.Exp, scale=nlg[:, 4 + h:5 + h], bias=lg128[:, 4 + h:5 + h])
                P.act(zeta[:, h:h + 1], io_p, AF.Exp, scale=nlg[:, h:h + 1], bias=lg127[:, h:h + 1])
                P.act(zeta[:, 4 + h:5 + h], io_p, AF.Exp, scale=lg[:, 4 + h:5 + h])
        P.ts(zeta[:], zeta[:], 1.0 / 16.0, None, ALU.mult)

        mhalf = P.sb("rt_mhalf", [128, 1], F32)
        P.memset(mhalf[:], -0.5)
        W = P.sb("rt_w", [128, 8, 1536], BF16)
        Wo = P.sb("rt_wo", [128, 4, D], BF16, chunk=D)
        Sm = P.sb("rt_S", [128, 2, 512], F32)
        Sb = [P.sb("rt_Sb%d" % i, [128, 2, 512], BF16) for i in range(2)]
        Sl = [P.sb("rt_Sl%d" % i, [128, 2, 512], BF16) for i in range(2)]
        tu = K.tmpf[0][:, 0:256].rearrange("p (h r f) -> p h r f", h=2, r=2)
        tv_ = K.tmpf[0][:, 256:512].rearrange("p (h r f) -> p h r f", h=2, r=2)
        kvb = [P.sb("rt_kv%d" % i, [128, 768], BF16, chunk=64) for i in range(2)]
        krot = [kvb[i][:, 0:256] for i in range(2)]
        qrot = [P.sb("rt_qr%d" % i, [128, 256], BF16) for i in range(2)]
        kz = [P.sb("rt_kz%d" % i, [128, 256], BF16) for i in range(2)]
        vb = [kvb[i][:, 256:768] for i in range(2)]
        sg = [P.sb("rt_sg%d" % i, [128, 512], BF16) for i in range(2)]
        qT3 = [P.sb("rt_qT%d" % i, [128, 3, 2, 128], BF16) for i in range(2)]
        kT = [P.sb("rt_kT%d" % i, [128, 2, 128], BF16) for i in range(2)]
        inn = [P.sb("rt_in%d" % i, [128, 128], BF16) for i in range(2)]
        yn = [P.sb("rt_yn0", [128, 512], BF16)] * 2
        zz = [P.sb("rt_z0", [128, 512], BF16)] * 2
        zT = [P.sb("rt_zT0", [128, 4, 512], BF16)] * 2
        st6 = [P.sb("rt_st%d" % i, [128, 6], F32) for i in range(2)]
        mv = [P.sb("rt_mv%d" % i, [128, 4], F32) for i in range(2)]
        pb = K.pb
        tb3 = pb[3][:].bitcast(BF16)
        tb7 = pb[7][:].bitcast(BF16)

        def rope(src, dst, m):
            x4 = src.rearrange("p (h r f) -> p h r f", h=2, r=2)
            d4 = dst.rearrange("p (h r f) -> p h r f", h=2, r=2)
            C = rc[:, m].unsqueeze(2).broadcast_to([128, 2, 2, 64])
            S_ = rs[:, m].unsqueeze(2).broadcast_to([128, 2, 2, 64])
            P.tt(tu, x4, C, ALU.mult)
            P.tt(tv_, x4[:, :, ::-1, :], S_, ALU.mult)
            P.tt(d4[:, :, 0, :], tu[:, :, 0, :], tv_[:, :, 0, :], ALU.subtract)
            P.tt(d4[:, :, 1, :], tu[:, :, 1, :], tv_[:, :, 1, :], ALU.add)

        def kv_proj(t, q, with_q=False):
            cols = slice(t * 128, (t + 1) * 128)
            if with_q:
                for kc in range(8):
                    P.mm(pb[0][:], hT[:, kc, cols], W[:, kc, 0:512], start=(kc == 0), stop=(kc == 7))
            else:
                for kc in range(8):
                    P.mm(pb[0][:, 256:512], hT[:, kc, cols], W[:, kc, 256:512], start=(kc == 0), stop=(kc == 7))
            for kc in range(8):
                P.mm(pb[1][:], hT[:, kc, cols], W[:, kc, 512:1024], start=(kc == 0), stop=(kc == 7))
            if t >= 2:
                rope(pb[0][:, 256:512], krot[q][:], t - 2)
            else:
                P.copy(krot[q][:], pb[0][:, 256:512])
            P.copy(vb[q][:], pb[1][:], eng="act")

        def s_update(q, h, d):
            for c2 in range(2):
                P.mm(pb[5 + c2][:], kz[q][:, c2 * 128:(c2 + 1) * 128], vb[q][:])
            for c2 in range(2):
                P.stt(Sm[:, c2, :], Sm[:, c2, :], gch[:, d * 4 + h:d * 4 + h + 1], pb[5 + c2][:], ALU.mult, ALU.add)

        nq = 0
        for h in range(4):
            P.dma(W[:].rearrange("p k c -> p (k c)"), K.wb["ret_in"][j * 4 + h], q="sp")
            P.dma(Wo[:].rearrange("p k c -> p (k c)"), K.wb["ret_out"][j * 4 + h], q="sp")
            for kc in range(4):
                g_ = K.small[:, a_gn + (j * 4 + h) * 4 + kc:a_gn + (j * 4 + h) * 4 + kc + 1]
                P.act(Wo[:, kc, :], Wo[:, kc, :], AF.Copy, scale=g_)
            P.memset(Sm[:], 0.0)
            order1 = [1, 0] + list(range(17, 1, -1))
            q0 = nq % 2
            kv_proj(order1[0], q0)
            P.dma(K.kvc[order1[0]], kvb[q0][:], q="pool")
            P.ts(kz[q0][:], krot[q0][:], zeta[:, 4 + h:5 + h], None, ALU.mult)
            for idx, t in enumerate(order1):
                q = nq % 2
                nq += 1
                P.copy(Sb[q][:], Sm[:], eng="act")
                if need_ctx or t >= 2:
                    P.dma(K.sstate[t], Sb[q][:].rearrange("p c v -> p (c v)"), q="pool")
                if t == 2:
                    break
                tn = order1[idx + 1]
                qn = nq % 2
                kv_proj(tn, qn)
                P.dma(K.kvc[tn], kvb[qn][:], q="pool")
                if tn != 2:
                    P.ts(kz[qn][:], krot[qn][:], zeta[:, 4 + h:5 + h], None, ALU.mult)
                s_update(q, h, 1)
            P.memset(Sm[:], 0.0)
            base = nq
            nq += 18

            def A2(t):
                q = (base + t) % 2
                cols = slice(t * 128, (t + 1) * 128)
                active = need_ctx or t >= 2
                P.dma(kvb[q][:], K.kvc[t], q="sp")
                if active:
                    P.dma(Sl[q][:].rearrange("p c v -> p (c v)"), K.sstate[t], q="sp")
                    for kc in range(8):
                        P.mm(pb[0][:, 0:256], hT[:, kc, cols], W[:, kc, 0:256], start=(kc == 0), stop=(kc == 7))
                    for kc in range(8):
                        P.mm(pb[2][:], hT[:, kc, cols], W[:, kc, 1024:1536], start=(kc == 0), stop=(kc == 7))
                    if t >= 2:
                        rope(pb[0][:, 0:256], qrot[q][:], t - 2)
                    else:
                        P.copy(qrot[q][:], pb[0][:, 0:256])
                    P.act(sg[q][:], pb[2][:], AF.Silu)

            def B2a(t):
                q = (base + t) % 2
                active = need_ctx or t >= 2
                P.ts(kz[q][:], krot[q][:], zeta[:, h:h + 1], None, ALU.mult)
                if active:
                    for c2 in range(2):
                        P.tr(tb3[:, c2 * 128:(c2 + 1) * 128], qrot[q][:, c2 * 128:(c2 + 1) * 128], K.ident[:])
                        P.tr(tb3[:, 256 + c2 * 128:256 + (c2 + 1) * 128], krot[q][:, c2 * 128:(c2 + 1) * 128], K.ident[:])
                    tq = tb3[:, 0:256].rearrange("p (c t) -> p c t", c=2)
                    P.act(qT3[q][:, 0], tq, AF.Copy, scale=1.0 / 16.0)
                    P.tt(qT3[q][:, 1], tq, XI[:, h, :].unsqueeze(1).broadcast_to([128, 2, 128]), ALU.mult)
                    P.tt(qT3[q][:, 2], tq, XI[:, 4 + h, :].unsqueeze(1).broadcast_to([128, 2, 128]), ALU.mult)
                    P.copy(kT[q][:], tb3[:, 256:512].rearrange("p (c t) -> p c t", c=2), eng="act")

            def B2b(t):
                q = (base + t) % 2
                active = need_ctx or t >= 2
                if active:
                    for c2 in range(2):
                        P.mm(pb[3][:, 256:384], kT[q][:, c2, :], qT3[q][:, 0, c2, :], start=(c2 == 0), stop=(c2 == 1))
                    P.tt(inn[q][:], pb[3][:, 256:384], MT[:, h, :], ALU.mult)

            def C2a(t):
                q = (base + t) % 2
                active = need_ctx or t >= 2
                if active:
                    P.mm(pb[4][:], inn[q][:], vb[q][:], start=True, stop=False)
                    for c2 in range(2):
                        P.mm(pb[4][:], qT3[q][:, 1, c2, :], Sb[q][:, c2, :], start=False, stop=False)
                    for c2 in range(2):
                        P.mm(pb[4][:], qT3[q][:, 2, c2, :], Sl[q][:, c2, :], start=False, stop=(c2 == 1))
                    P.add("dve", lambda e, q=q: e.bn_stats(st6[q][:], pb[4][:]), reads=(pb[4][:],), writes=(st6[q][:],))
                    P.add("dve", lambda e, q=q: e.bn_aggr(mv[q][:, 0:2], st6[q][:]), reads=(st6[q][:],), writes=(mv[q][:, 0:2],))
                    P.ts(mv[q][:, 2:3], mv[q][:, 1:2], EPS, None, ALU.add)
                    P.tt(mv[q][:, 2:3], mv[q][:, 2:3], mhalf[:], ALU.pow, eng="pool")
                    P.stt(mv[q][:, 3:4], mv[q][:, 0:1], -1.0, mv[q][:, 2:3], ALU.mult, ALU.mult)
                    P.act(yn[q][:], pb[4][:], AF.Identity, scale=mv[q][:, 2:3], bias=mv[q][:, 3:4])
                    P.tt(zz[q][:], yn[q][:], sg[q][:], ALU.mult)

            def C2b(t):
                q = (base + t) % 2
                active = need_ctx or t >= 2
                if active:
                    blk = 0 if t < 2 else 1 + (t - 2) // 4
                    pos = t if t < 2 else (t - 2) % 4
                    zb = zT[blk % 2]
                    for c4 in range(4):
                        P.tr(tb7[:, c4 * 128:(c4 + 1) * 128], zz[q][:, c4 * 128:(c4 + 1) * 128], K.ident[:])
                    P.copy(zb[:, :, pos * 128:(pos + 1) * 128], tb7[:, 0:512].rearrange("p (c t) -> p c t", c=4), eng="act")

            def C2c(t):
                q = (base + t) % 2
                active = need_ctx or t >= 2
                s_update(q, h, 0)
                if t + 1 < 18:
                    P.copy(Sb[(base + t + 1) % 2][:], Sm[:], eng="act")
                if active and (t == 1 or (t >= 2 and (t - 2) % 4 == 3)):
                    blk = 0 if t < 2 else 1 + (t - 2) // 4
                    zb = zT[blk % 2]
                    a, b = BLK512[blk]
                    n = b - a
                    st = 4 if a < CTXN else s
                    for jj in range(8):
                        bank = pb[5 + jj % 2]
                        for c4 in range(4):
                            P.mm(bank[:, :n], Wo[:, c4, jj * 128:(jj + 1) * 128], zb[:, c4, :n], start=(c4 == 0), stop=(c4 == 3))
                        P.stt(K.xT[:, jj, a:b], bank[:, :n], K.modT[:, l, 2, jj, st:st + 1], K.xT[:, jj, a:b], ALU.mult, ALU.add)

            P.copy(Sb[base % 2][:], Sm[:], eng="act")
            A2(0)
            B2a(0)
            B2b(0)
            for t in range(18):
                nxt = t + 1 < 18
                if nxt:
                    A2(t + 1)
                C2a(t)
                if nxt:
                    B2a(t + 1)
                C2b(t)
                if nxt:
                    B2b(t + 1)
                C2c(t)


def lru_mixer(K, l, s, need_ctx):
    P = K.P
    compute_rstd(K, BLK512)
    a_cw, _ = SM["lru_cw"]
    a_gb, _ = SM["lru_gb"]
    a_lam, _ = SM["lru_lam"]
    with P.scope():
        xc = P.sb("lru_xc", [128, LKC, NT], BF16, chunk=128)
        gy = P.sb("lru_gy", [128, LKC, NT], BF16, chunk=128)
        coef = P.sb("lru_coef", [128, 20], F32)
        ct = P.sb("lru_ct", [128, 20], F32)
        lam = K.small[:, a_lam:a_lam + 20]
        P.act(ct[:], lam, AF.Abs)
        P.act(ct[:], ct[:], AF.Exp, scale=-1.0)
        P.act(ct[:], ct[:], AF.Ln, bias=1.0)
        P.ts(coef[:], lam, -1.0, 0.0, ALU.mult, ALU.max)
        P.tt(coef[:], coef[:], ct[:], ALU.add)
        P.ts(coef[:], coef[:], -4.0, None, ALU.mult)
        hgb = P.sb("lru_hgb", [128, 40], F32)
        P.ts(hgb[:], K.small[:, a_gb:a_gb + 40], 0.5, None, ALU.mult)
        nb = 0
        with P.scope():
            hb = P.sb("lru_h", [128, 8, 416], BF16, chunk=416)
            wi = [P.sb("lru_wi%d" % i, [128, 8, 256], BF16) for i in range(2)]
            tt_ = [P.sb("lru_t%d" % i, [128, 416], F32) for i in range(4)]
            for (c0, c1, s0, s1) in SBLOCKS:
                st = 4 if c0 < CTXN else s
                lo, hi = max(c0 - 2, s0), min(c1 + 1, s1)
                n, W = hi - lo, c1 - c0
                o = c0 - lo
                make_h(K, hb, lo, hi, l, 0, st)
                for c in range(LKC):
                    w = wi[c % 2]
                    P.dma(w[:].rearrange("p k c -> p (k c)"), K.wb["lru_in"][c], q="sp")
                    by = K.pb[nb % 6]
                    bx = K.pb[(nb + 1) % 6]
                    nb += 2
                    for kc in range(8):
                        P.mm(by[:, :W], w[:, kc, 0:128], hb[:, kc, o:o + W], start=(kc == 0), stop=(kc == 7))
                    for kc in range(8):
                        P.mm(bx[:, :n], w[:, kc, 128:256], hb[:, kc, :n], start=(kc == 0), stop=(kc == 7))
                    P.act(gy[:, c, c0:c1], by[:, :W], AF.Gelu_apprx_tanh)
                    t2 = tt_[(c % 2) * 2 + 1]
                    cw = K.small[:, a_cw + c * 5:a_cw + c * 5 + 5]
                    P.act(t2[:, :W], bx[:, o:o + W], AF.Identity, scale=cw[:, 2:3], bias=cw[:, 4:5])
                    taps = [(-2, 0), (-1, 1), (1, 3)]
                    for ti_, (dl, wk) in enumerate(taps):
                        cs = max(c0, lo - dl)
                        ce = min(c1, hi - dl)
                        dst = xc[:, c, cs:ce] if ti_ == 2 else t2[:, cs - c0:ce - c0]
                        P.stt(dst, bx[:, cs + dl - lo:ce + dl - lo], cw[:, wk:wk + 1], t2[:, cs - c0:ce - c0], ALU.mult, ALU.add)
                    if ce < c1:
                        P.copy(xc[:, c, ce:c1], t2[:, ce - c0:W])
        with P.scope():
            hf = P.sb("lru_hf", [128, NT], BF16, chunk=128)
            gwt = [P.sb("lru_gw%d" % i, [128, 2, 3, 128], BF16) for i in range(2)]
            tr_ = [P.sb("lru_r%d" % i, [128, 512], F32) for i in range(2)]
            ti2 = [P.sb("lru_i%d" % i, [128, 512], F32) for i in range(2)]
            ta = [P.sb("lru_a%d" % i, [128, 512], F32) for i in range(2)]
            th = K.tmpf
            nq = 0
            for c in range(LKC):
                slots = _lru_slots(c)
                for d in range(2):
                    gw = gwt[(c * 2 + d) % 2]
                    P.dma(gw[:].rearrange("p g s c -> p (g s c)"), K.wb["lru_gate"][d * 10 + c], q="sp")
                    order = BLK512 if d == 0 else [BLK512[0]] + BLK512[:0:-1]
                    gb = lambda g: hgb[:, (d * 2 + g) * 10 + c:(d * 2 + g) * 10 + c + 1]
                    cf = coef[:, d * 10 + c:d * 10 + c + 1]
                    state = {"prev": None}

                    def G(k):
                        a, b = order[k]
                        n = b - a
                        br = K.pb[(2 * k) % 6]
                        bi = K.pb[(2 * k + 1) % 6]
                        for g, bank in ((0, br), (1, bi)):
                            for si, kc in enumerate(slots):
                                P.mm(bank[:, :n], gw[:, g, si, :], xc[:, kc, a:b], start=(si == 0), stop=(si == len(slots) - 1))
                        r, i_, av = tr_[k % 2], ti2[k % 2], ta[k % 2]
                        P.act(r[:, :n], br[:, :n], AF.Tanh, scale=0.5, bias=gb(0))
                        P.act(i_[:, :n], bi[:, :n], AF.Tanh, scale=0.5, bias=gb(1))
                        P.act(av[:, :n], r[:, :n], AF.Exp, scale=cf, bias=cf)
                        P.tt(r[:, :n], av[:, :n], av[:, :n], ALU.mult)

                    def S(k):
                        a, b = order[k]
                        n = b - a
                        r, i_, av, hh = tr_[k % 2], ti2[k % 2], ta[k % 2], th[k % 2]
                        prev = state["prev"]
                        P.act(r[:, :n], r[:, :n], AF.Sqrt, scale=-1.0, bias=1.0 + 1e-6)
                        P.stt(i_[:, :n], i_[:, :n], 1.0, xc[:, c, a:b], ALU.add, ALU.mult)
                        P.stt(i_[:, :n], i_[:, :n], 0.5, r[:, :n], ALU.mult, ALU.mult)
                        init = 0.0 if prev is None else prev
                        rd = [av[:, :n], i_[:, :n]] + ([] if prev is None else [prev])
                        if d == 0:
                            P.add("dve", lambda e, hh=hh, av=av, i_=i_, n=n, init=init: e.tensor_tensor_scan(
                                hh[:, :n], av[:, :n], i_[:, :n], init, ALU.mult, ALU.add), reads=rd, writes=(hh[:, :n],))
                            state["prev"] = hh[:, n - 1:n]
                            P.copy(hf[:, a:b], hh[:, :n], eng="pool")
                        else:
                            P.add("dve", lambda e, hh=hh, av=av, i_=i_, n=n, init=init: e.tensor_tensor_scan(
                                hh[:, 0:n][:, ::-1], av[:, 0:n][:, ::-1], i_[:, 0:n][:, ::-1], init, ALU.mult, ALU.add),
                                reads=rd, writes=(hh[:, :n],))
                            state["prev"] = hh[:, 0:1]
                            P.tt(r[:, :n], hh[:, :n], hf[:, a:b], ALU.add)
                            P.tt(gy[:, c, a:b], r[:, :n], gy[:, c, a:b], ALU.mult)

                    k0 = 0
                    while k0 < len(order):
                        ks = list(range(k0, min(k0 + 2, len(order))))
                        for k in ks:
                            G(k)
                        for k in ks:
                            S(k)
                        k0 += 2
        with P.scope():
            wo = P.sb("lru_wo", [128, LKC, D], BF16, chunk=D)
            for c in range(0, LKC, 2):
                P.dma(wo[:, c:c + 2, :], K.wb["lru_out"][c:c + 2].rearrange("i p c -> p i c"), q="sp")
            for (a, b) in (BLK512 if need_ctx else BLK512[1:]):
                st = 4 if a < CTXN else s
                n = b - a
                for j in range(8):
                    bank = K.pb[6 + j % 2]
                    for c in range(LKC):
                        P.mm(bank[:, :n], wo[:, c, j * 128:(j + 1) * 128], gy[:, c, a:b], start=(c == 0), stop=(c == LKC - 1))
                    P.stt(K.xT[:, j, a:b], bank[:, :n], K.modT[:, l, 2, j, st:st + 1], K.xT[:, j, a:b], ALU.mult, ALU.add)


def _na_tiles(m):
    res = {}
    for ql in (0, 1):
        r = 2 * m + ql
        start = min(max(r - 4, 0), 24)
        for kt in range(16):
            v0 = start <= 2 * kt < start + 8
            v1 = start <= 2 * kt + 1 < start + 8
            if not (v0 or v1):
                continue
            if v0 and v1:
                sl = (2 * kt - r) + 7
                assert 0 <= sl <= 13
            elif v1:
                assert 2 * kt + 1 - r == -4
                sl = 14
            else:
                assert 2 * kt - r == 3
                sl = 15
            res.setdefault(kt, [None, None])[ql] = sl
    return sorted(res.items())


def na_mixer(K, l, s, need_ctx):
    P = K.P
    compute_rstd(K, BLK512)
    with P.scope():
        hT = P.sb("na_h", [128, 8, NT], BF16, chunk=128)
        W = P.sb("na_w", [128, 8, 768], BF16)
        Wo = P.sb("na_wo", [128, 2, D], BF16)
        tab = P.sb("na_tabs", [128, 4, 16, 64], BF16)
        qT = P.sb("na_q", [128, 2, NT], BF16, chunk=128)
        kT = P.sb("na_k", [128, 2, NT], BF16, chunk=128)
        V = P.sb("na_v", [128, 18, 4, 65], BF16, chunk=260)
        OT = P.sb("na_ot", [128, 2, NT], BF16, chunk=128)
        PT = [P.sb("na_pt%d" % i, [128, 7, 128], BF16, chunk=128) for i in range(2)]
        tmp = [P.sb("na_tmp%d" % i, [128, 5, 128], F32, chunk=64) for i in range(2)]
        Ot = [P.sb("na_o%d" % i, [128, 4, 64], BF16) for i in range(2)]
        rec = [P.sb("na_rec%d" % i, [128, 4], F32) for i in range(2)]
        for (a, b) in BLK512:
            make_h(K, hT[:, :, a:b], a, b, l, 0, 4 if a < CTXN else s)
        P.memset(V[:, :, :, 64:65], 1.0, eng="pool")
        nb = 0
        no = 0
        for G in range(4):
            P.dma(W[:].rearrange("p k c -> p (k c)"), K.wb["na_in"][G], q="sp")
            P.dma(Wo[:].rearrange("p k c -> p (k c)"), K.wb["na_out"][G], q="sp")
            P.dma(tab[:].rearrange("p h s c -> p h (s c)"), K.wb["na_tab"][4 * G:4 * G + 4].rearrange("h p c -> p h c"), q="sp")
            for (a, b) in BLK512:
                n = b - a
                for which, dst in ((0, qT), (1, kT)):
                    for k2 in range(2):
                        bank = K.pb[nb % 4]
                        nb += 1
                        for kc in range(8):
                            P.mm(bank[:, :n], W[:, kc, which * 256 + k2 * 128:which * 256 + (k2 + 1) * 128], hT[:, kc, a:b],
                                 start=(kc == 0), stop=(kc == 7))
                        if which == 0:
                            P.act(dst[:, k2, a:b], bank[:, :n], AF.Copy, scale=0.125)
                        else:
                            P.copy(dst[:, k2, a:b], bank[:, :n])
            for ti in range(18):
                bank = K.pb[nb % 4]
                nb += 1
                for kc in range(8):
                    P.mm(bank[:, :256], hT[:, kc, ti * 128:(ti + 1) * 128], W[:, kc, 512:768], start=(kc == 0), stop=(kc == 7))
                P.copy(V[:, ti, :, 0:64], bank[:, :256].rearrange("p (h d) -> p h d", h=4), eng=("act" if ti % 2 else "dve"))
            jobs = []
            if need_ctx:
                for qt in range(2):
                    jobs.append((qt, [(0, None), (1, None)]))
            for m in range(16):
                jobs.append((2 + m, [(2 + kt, sl) for kt, sl in _na_tiles(m)] + [(0, None), (1, None)]))
            steps = [(ji, hh) for ji in range(len(jobs)) for hh in range(4)]

            def scores(n):
                ji, hh = steps[n]
                qt, tiles = jobs[ji]
                q0 = qt * 128
                k2, p0 = hh // 2, (hh % 2) * 64
                sb = [K.pb[(n % 2) * 2], K.pb[(n % 2) * 2 + 1]]
                pt = PT[n % 2]
                tm = tmp[n % 2]
                nlat = sum(1 for _, sl in tiles if sl is not None)
                for i, (kt, sl) in enumerate(tiles):
                    P.mm(sb[i // 4][:, (i % 4) * 128:(i % 4 + 1) * 128], kT[p0:p0 + 64, k2, kt * 128:(kt + 1) * 128],
                         qT[p0:p0 + 64, k2, q0:q0 + 128])
                for i, (kt, sl) in enumerate(tiles):
                    if sl is None:
                        continue
                    if sl[0] is not None and sl[1] is not None and sl[0] < 14 and sl[1] == sl[0] - 1:
                        src = sb[i // 4][:, (i % 4) * 128:(i % 4 + 1) * 128].rearrange("p (a b) -> p a b", a=2)
                        P.tt(tm[:, i, :].rearrange("p (a b) -> p a b", a=2), src, tab[:, hh, sl[1]:sl[0] + 1, :][:, ::-1, :], ALU.add)
                        continue
                    for ql in range(2):
                        src = sb[i // 4][:, (i % 4) * 128 + ql * 64:(i % 4) * 128 + ql * 64 + 64]
                        if sl[ql] is None:
                            P.memset(tm[:, i, ql * 64:ql * 64 + 64], -30000.0, eng="pool")
                        else:
                            P.tt(tm[:, i, ql * 64:ql * 64 + 64], src, tab[:, hh, sl[ql], :], ALU.add)
                if nlat:
                    P.act(pt[:, 0:nlat, :], tm[:, 0:nlat, :], AF.Exp)
                ci = [i for i, (kt, sl) in enumerate(tiles) if sl is None]
                assert len(ci) == 2 and ci[1] == ci[0] + 1 and ci[0] // 4 == ci[1] // 4
                P.act(pt[:, ci[0]:ci[0] + 2, :], sb[ci[0] // 4][:, (ci[0] % 4) * 128:(ci[0] % 4 + 2) * 128].rearrange("p (a b) -> p a b", a=2), AF.Exp)

            def pv(n):
                ji, hh = steps[n]
                qt, tiles = jobs[ji]
                q0 = qt * 128
                no = ji
                ob = K.pb[4 + no % 2]
                pt = PT[n % 2]
                for i, (kt, sl) in enumerate(tiles):
                    P.mm(ob[:, hh * 65:hh * 65 + 65], pt[:, i, :], V[:, kt, hh, :], start=(i == 0), stop=(i == len(tiles) - 1))
                if hh < 3:
                    return
                o_t = Ot[no % 2]
                rc = rec[no % 2]
                obv = ob[:, 0:260].rearrange("p (h e) -> p h e", h=4)
                P.add("dve", lambda e, rc=rc, obv=obv: e.reciprocal(rc[:], obv[:, :, 64]), reads=(ob[:, 0:260],), writes=(rc[:],))
                P.tt(o_t[:], obv[:, :, 0:64], rc[:].unsqueeze(2).broadcast_to([128, 4, 64]), ALU.mult)
                tb = K.pb[6 + no % 2]
                tbv = tb[:].bitcast(BF16)
                for k2 in range(2):
                    P.tr(tbv[:, k2 * 128:(k2 + 1) * 128], o_t[:, 2 * k2:2 * k2 + 2, :].rearrange("p h d -> p (h d)"), K.ident[:])
                P.copy(OT[:, :, q0:q0 + 128], tbv[:, 0:256].rearrange("p (k t) -> p k t", k=2), eng="act")

            for n in range(len(steps) + 1):
                if n < len(steps):
                    scores(n)
                if n >= 1:
                    pv(n - 1)
            for (a, b) in (BLK512 if need_ctx else BLK512[1:]):
                st = 4 if a < CTXN else s
                n = b - a
                for j in range(8):
                    bank = K.pb[4 + j % 2]
                    for k2 in range(2):
                        P.mm(bank[:, :n], Wo[:, k2, j * 128:(j + 1) * 128], OT[:, k2, a:b], start=(k2 == 0), stop=(k2 == 1))
                    P.stt(K.xT[:, j, a:b], bank[:, :n], K.modT[:, l, 2, j, st:st + 1], K.xT[:, j, a:b], ALU.mult, ALU.add)


_SHARED = {}


def run(inputs, nseq, ncores, plan, final=True, trace=False):
    sh = host_shared(inputs)
    nc, K = build(nseq, plan, final)
    in_maps = []
    for c in range(ncores):
        m = host_core(inputs, sh["small"], c, nseq)
        for n in list(WSPEC) + ["rope", "fin"]:
            m[n] = sh[n]
        in_maps.append(m)
    res = run_bass_kernel_spmd(nc, in_maps, core_ids=list(range(ncores)), trace=trace)
    return res


FULL_PLAN = [(l, h) for l in range(NLAYER) for h in ("m", "f")]


def kernel(**inputs):
    inputs = {k: np.asarray(v) for k, v in inputs.items()}
    res = run(inputs, 4, 8, FULL_PLAN, final=True)
    return np.concatenate([r["y"] for r in res.results], axis=0).astype(np.float32)
```
